# Optimizing a Trainium2 kernel written in Bass

```python
import math
import jax
import jax.numpy as jnp
from jax import lax
import numpy as np

D_MODEL = 1024
BATCH = 2
SEQ = 8192
DEPTH = 2
DEC_BATCH = 8
DEC_SEQ = 2048
PAST_LEN = 128

HEAD_DIM = 64
N_A_HEADS = 4
N_A_KV_HEADS = 2
N_B_HEADS = 4
N_C_HEADS = 4
N_D_HEADS = 4
A_W = N_A_HEADS * HEAD_DIM
A_KV_W = N_A_KV_HEADS * HEAD_DIM
B_W = N_B_HEADS * HEAD_DIM
C_W = N_C_HEADS * HEAD_DIM
D_W = N_D_HEADS * HEAD_DIM
MIX_W = A_W + B_W + C_W + D_W
IN_WIDTHS = (A_W, A_KV_W, A_KV_W, A_W, B_W, B_W, B_W, B_W, C_W, C_W, C_W, C_W, D_W, D_W, D_W, D_W)
IN_W = 2 * A_W + 2 * A_KV_W + 4 * (B_W + C_W + D_W)
GRID_W = 64
Q_BLOCK = 128
B_Q_BLOCK = 32
ROPE_THETA = 10000.0
B_PATTERNS = ((128, 1), (512, 4), (2048, 16))
T5_BUCKETS = 32
T5_MAX_DIST = 128
N_BIAS_HEADS = N_B_HEADS + N_C_HEADS
C_HALF = HEAD_DIM // 2
NA_KH = 8
NA_KW = 16
RMS_EPS = 1e-6
NEG_INF = -1e30

kernel_name = 'hybrid_parallel_group_encoder'


def _rmsnorm(x, g):
    xf = x.astype(jnp.float32)
    y = xf * lax.rsqrt(jnp.mean(xf * xf, axis=-1, keepdims=True) + RMS_EPS)
    return (y * g.astype(jnp.float32)).astype(x.dtype)


def _t5_bucket(rel):
    half = T5_BUCKETS // 2
    max_exact = half // 2
    dist = jnp.abs(rel)
    large = max_exact + (jnp.log(jnp.maximum(dist, 1).astype(jnp.float32) / max_exact)
                         / math.log(T5_MAX_DIST / max_exact) * (half - max_exact)).astype(jnp.int32)
    large = jnp.minimum(large, half - 1)
    return jnp.where(rel > 0, half, 0) + jnp.where(dist < max_exact, dist, large)


def _rope_1d(x, pos):
    d = x.shape[-1]
    hf = d // 2
    freqs = ROPE_THETA ** (-jnp.arange(hf, dtype=jnp.float32) * 2.0 / d)
    ang = pos.astype(jnp.float32)[:, None] * freqs[None, :]
    cos = jnp.cos(ang)[None, :, None, :]
    sin = jnp.sin(ang)[None, :, None, :]
    xf = x.astype(jnp.float32)
    x1, x2 = xf[..., :hf], xf[..., hf:]
    return jnp.concatenate([x1 * cos - x2 * sin, x2 * cos + x1 * sin], axis=-1).astype(x.dtype)


def _axial_rope(x, row, col):
    half = x.shape[-1] // 2
    return jnp.concatenate([_rope_1d(x[..., :half], row), _rope_1d(x[..., half:], col)], axis=-1)


def _blocks(t, qb):
    bsz, n = t.shape[:2]
    return jnp.moveaxis(t.reshape(bsz, n // qb, qb, *t.shape[2:]), 1, 0)


def _unblocks(t):
    nb, bsz, qb = t.shape[:3]
    return jnp.moveaxis(t, 0, 1).reshape(bsz, nb * qb, *t.shape[3:])


def _mixer_a(q, k, v, q_gain, k_gain):
    bsz, n = q.shape[:2]
    q = _rmsnorm(q, q_gain)
    k = _rmsnorm(k, k_gain)
    t = jnp.arange(n, dtype=jnp.int32)
    row, col = t // GRID_W, t % GRID_W
    q = _axial_rope(q, row, col)
    k = _axial_rope(k, row, col)
    q = q.reshape(bsz, n, N_A_KV_HEADS, N_A_HEADS // N_A_KV_HEADS, HEAD_DIM)
    scale = HEAD_DIM ** -0.5

    def block(qb):
        s = jnp.einsum('bqkgd,bskd->bkgqs', qb, k).astype(jnp.float32) * scale
        p = jax.nn.softmax(s, axis=-1).astype(v.dtype)
        return jnp.einsum('bkgqs,bskd->bqkgd', p, v)

    o = _unblocks(lax.map(block, _blocks(q, Q_BLOCK)))
    return o.reshape(bsz, n, A_W)


def _dilated_offsets():
    return np.concatenate([d * np.arange(-(w // (2 * d)), w // (2 * d) + 1) for (w, d) in B_PATTERNS]).astype(np.int32)


def _mixer_b(q, k, v, t5_bias):
    bsz, n = q.shape[:2]
    offs_np = _dilated_offsets()
    n_pat = len(B_PATTERNS)
    n_key = offs_np.size // n_pat
    offs = jnp.asarray(offs_np)
    bias = jnp.take(t5_bias[:, :N_B_HEADS].T.astype(jnp.float32), _t5_bucket(offs), axis=1)
    scale = HEAD_DIM ** -0.5

    def block(args):
        qb, i0 = args
        idx = i0 + jnp.arange(B_Q_BLOCK, dtype=jnp.int32)[:, None] + offs[None, :]
        valid = (idx >= 0) & (idx < n)
        idx = jnp.clip(idx, 0, n - 1)
        kb = k[:, idx]
        vb = v[:, idx]
        s = jnp.einsum('bqhd,bqjhd->bhqj', qb, kb).astype(jnp.float32) * scale + bias[None, :, None, :]
        s = jnp.where(valid[None, None], s, NEG_INF)
        s = s.reshape(bsz, N_B_HEADS, B_Q_BLOCK, n_pat, n_key)
        m = jnp.max(s, axis=-1, keepdims=True)
        e = jnp.exp(s - m)
        den = jnp.sum(e, axis=-1, keepdims=True)
        vb = vb.reshape(bsz, B_Q_BLOCK, n_pat, n_key, N_B_HEADS, HEAD_DIM)
        o_pat = jnp.einsum('bhqgj,bqgjhd->bhqgd', (e / den).astype(v.dtype), vb)
        alpha = jax.nn.softmax((m + jnp.log(den))[..., 0], axis=-1)
        return jnp.einsum('bhqg,bhqgd->bqhd', alpha.astype(o_pat.dtype), o_pat)

    starts = jnp.arange(n // B_Q_BLOCK, dtype=jnp.int32) * B_Q_BLOCK
    o = _unblocks(lax.map(block, (_blocks(q, B_Q_BLOCK), starts)))
    return o.reshape(bsz, n, B_W)


def _mixer_c(q, k, v, t5_bias, lq1, lk1, lq2, lk2, subln_g, lambda_init):
    bsz, n = q.shape[:2]
    k1, k2 = k[..., :C_HALF], k[..., C_HALF:]
    lam = (jnp.exp(jnp.sum((lq1 * lk1).astype(jnp.float32)))
           - jnp.exp(jnp.sum((lq2 * lk2).astype(jnp.float32))) + lambda_init)
    table_c = t5_bias[:, N_B_HEADS:].T.astype(jnp.float32)
    scale = C_HALF ** -0.5
    keys = jnp.arange(n, dtype=jnp.int32)

    def block(args):
        qb, i0 = args
        rel = keys[None, :] - (i0 + jnp.arange(Q_BLOCK, dtype=jnp.int32))[:, None]
        bias = jnp.take(table_c, _t5_bucket(rel), axis=1)
        s1 = jnp.einsum('bqhd,bkhd->bhqk', qb[..., :C_HALF], k1).astype(jnp.float32) * scale + bias
        s2 = jnp.einsum('bqhd,bkhd->bhqk', qb[..., C_HALF:], k2).astype(jnp.float32) * scale + bias
        p = jax.nn.softmax(s1, axis=-1) - lam * jax.nn.softmax(s2, axis=-1)
        return jnp.einsum('bhqk,bkhd->bqhd', p.astype(v.dtype), v)

    starts = jnp.arange(n // Q_BLOCK, dtype=jnp.int32) * Q_BLOCK
    o = _unblocks(lax.map(block, (_blocks(q, Q_BLOCK), starts)))
    o = _rmsnorm(o, subln_g) * (1.0 - lambda_init)
    return o.reshape(bsz, n, C_W)


def _mixer_d(q, k, v, rpb):
    bsz, n = q.shape[:2]
    rows = n // GRID_W
    kh = min(NA_KH, rows)
    qg = q.reshape(bsz, rows, GRID_W, N_D_HEADS, HEAD_DIM)
    kg = k.reshape(bsz, rows, GRID_W, N_D_HEADS, HEAD_DIM)
    vg = v.reshape(bsz, rows, GRID_W, N_D_HEADS, HEAD_DIM)
    cols = np.arange(GRID_W)
    cs = np.clip(cols - NA_KW // 2, 0, GRID_W - NA_KW)
    col_idx = cs[:, None] + np.arange(NA_KW)[None, :]
    dc = col_idx - cols[:, None] + NA_KW - 1
    scale = HEAD_DIM ** -0.5

    def row_step(args):
        qr, r = args
        r0 = jnp.clip(r - kh // 2, 0, rows - kh)
        kr = lax.dynamic_slice_in_dim(kg, r0, kh, axis=1)
        vr = lax.dynamic_slice_in_dim(vg, r0, kh, axis=1)
        kn = kr[:, :, col_idx]
        vn = vr[:, :, col_idx]
        dr = r0 + jnp.arange(kh, dtype=jnp.int32) - r + NA_KH - 1
        bias = rpb[:, dr[None, :, None], dc[:, None, :]].astype(jnp.float32)
        s = jnp.einsum('bchd,bicjhd->bhcij', qr, kn).astype(jnp.float32) * scale + bias[None]
        p = jax.nn.softmax(s.reshape(bsz, N_D_HEADS, GRID_W, kh * NA_KW), axis=-1).reshape(s.shape)
        return jnp.einsum('bhcij,bicjhd->bchd', p.astype(v.dtype), vn)

    o = lax.map(row_step, (jnp.moveaxis(qg, 1, 0), jnp.arange(rows, dtype=jnp.int32)))
    return jnp.moveaxis(o, 0, 1).reshape(bsz, n, D_W)


def _layer(x, layer_idx, w_in, w_out, norm_g, a_q_gain, a_k_gain, t5_bias, lq1, lk1, lq2, lk2, subln_g, rpb):
    bsz, n, _ = x.shape
    proj = _rmsnorm(x, norm_g) @ w_in
    cuts = np.cumsum(IN_WIDTHS)[:-1].tolist()
    (aq, ak, av, ag, bq, bk, bv, bg, cq, ck, cv, cg, dq, dk, dv, dg) = jnp.split(proj, cuts, axis=-1)

    def heads(t):
        return t.reshape(bsz, n, -1, HEAD_DIM)

    lambda_init = 0.8 - 0.6 * math.exp(-0.3 * layer_idx)
    ya = _mixer_a(heads(aq), heads(ak), heads(av), a_q_gain, a_k_gain)
    yb = _mixer_b(heads(bq), heads(bk), heads(bv), t5_bias)
    yc = _mixer_c(heads(cq), heads(ck), heads(cv), t5_bias, lq1, lk1, lq2, lk2, subln_g, lambda_init)
    yd = _mixer_d(heads(dq), heads(dk), heads(dv), rpb)
    mixed = jnp.concatenate([ya * jax.nn.silu(ag), yb * jax.nn.silu(bg),
                             yc * jax.nn.silu(cg), yd * jax.nn.silu(dg)], axis=-1)
    return x + mixed @ w_out


def _trunk(x, w_in, w_out, norm_g, final_g, a_q_gain, a_k_gain, t5_bias,
           c_lambda_q1, c_lambda_k1, c_lambda_q2, c_lambda_k2, c_subln_g, d_rpb):
    for l in range(DEPTH):
        x = _layer(x, l, w_in[l], w_out[l], norm_g[l], a_q_gain[l], a_k_gain[l], t5_bias,
                   c_lambda_q1[l], c_lambda_k1[l], c_lambda_q2[l], c_lambda_k2[l], c_subln_g[l], d_rpb[l])
    return _rmsnorm(x, final_g)


def setup_inputs(seed: int = 0) -> dict:
    key = jax.random.key(seed)
    ks = jax.random.split(key, 16)
    nrm = jax.random.normal
    f32 = jnp.float32
    return {
        'x_prompt': nrm(ks[0], (BATCH, SEQ, D_MODEL), f32),
        'x_sample': nrm(ks[1], (DEC_BATCH, DEC_SEQ, D_MODEL), f32),
        'w_in': nrm(ks[2], (DEPTH, D_MODEL, IN_W), f32) * D_MODEL ** -0.5,
        'w_out': nrm(ks[3], (DEPTH, MIX_W, D_MODEL), f32) * MIX_W ** -0.5,
        'norm_g': 1.0 + 0.05 * nrm(ks[4], (DEPTH, D_MODEL), f32),
        'final_g': 1.0 + 0.05 * nrm(ks[5], (D_MODEL,), f32),
        'a_q_gain': 1.0 + 0.05 * nrm(ks[6], (DEPTH, HEAD_DIM), f32),
        'a_k_gain': 1.0 + 0.05 * nrm(ks[7], (DEPTH, HEAD_DIM), f32),
        't5_bias': 0.2 * nrm(ks[8], (T5_BUCKETS, N_BIAS_HEADS), f32),
        'c_lambda_q1': 0.1 * nrm(ks[9], (DEPTH, C_HALF), f32),
        'c_lambda_k1': 0.1 * nrm(ks[10], (DEPTH, C_HALF), f32),
        'c_lambda_q2': 0.1 * nrm(ks[11], (DEPTH, C_HALF), f32),
        'c_lambda_k2': 0.1 * nrm(ks[12], (DEPTH, C_HALF), f32),
        'c_subln_g': 1.0 + 0.05 * nrm(ks[13], (DEPTH, HEAD_DIM), f32),
        'd_rpb': 0.2 * nrm(ks[14], (DEPTH, N_D_HEADS, 2 * NA_KH - 1, 2 * NA_KW - 1), f32),
    }


def reference(x_prompt, x_sample, w_in, w_out, norm_g, final_g, a_q_gain, a_k_gain, t5_bias,
              c_lambda_q1, c_lambda_k1, c_lambda_q2, c_lambda_k2, c_subln_g, d_rpb):
    y_prompt = _trunk(x_prompt, w_in, w_out, norm_g, final_g, a_q_gain, a_k_gain, t5_bias,
                      c_lambda_q1, c_lambda_k1, c_lambda_q2, c_lambda_k2, c_subln_g, d_rpb)
    y_sample = _trunk(x_sample, w_in, w_out, norm_g, final_g, a_q_gain, a_k_gain, t5_bias,
                      c_lambda_q1, c_lambda_k1, c_lambda_q2, c_lambda_k2, c_subln_g, d_rpb)
    return (y_prompt, y_sample)
```

```python
import math
import numpy as np
import concourse.bass as bass
import concourse.mybir as mybir
from concourse.bass_utils import run_bass_kernel_spmd

F32 = mybir.dt.float32
BF16 = mybir.dt.bfloat16
ALU = mybir.AluOpType
AF = mybir.ActivationFunctionType

D = 1024
HD = 64
GRID_W = 64
EPS = 1e-6
NCOL = 18 * 64
C_AQ, C_BQ, C_AK, C_BK, C_AQS, C_AKS, C_AG, C_BG, C_AV, C_BV, C_CQ, C_DQ, C_CK, C_DK, C_CG, C_DG, C_CV, C_DV = range(18)

B_OMAX, B_NT = 11, 23
C_OMAX, C_NT = 5, 11
D_OMAX, D_NT = 5, 11
D_SPEC = 12
TB_B = B_NT * 128
TB_C = C_NT * 128
TB_D = D_NT * 128 + D_SPEC * 512


def _t5_bucket_np(rel):
    rel = np.asarray(rel, np.int64)
    half, max_exact = 16, 8
    dist = np.abs(rel)
    lg = np.log(np.maximum(dist, 1).astype(np.float32) / np.float32(max_exact)).astype(np.float32)
    large = max_exact + (lg / np.float32(math.log(128 / max_exact)) * np.float32(half - max_exact)).astype(np.int32)
    large = np.minimum(large, half - 1)
    return np.where(rel > 0, half, 0) + np.where(dist < max_exact, dist, large)


def _b_mult(off):
    off = np.asarray(off, np.int64)
    m = np.zeros(off.shape, np.float32)
    for (w, d) in ((128, 1), (512, 4), (2048, 16)):
        m += ((off % d == 0) & (np.abs(off) <= w // 2)).astype(np.float32)
    return m


def _toep_off(omax, nt):
    p = np.arange(128)[:, None]
    c = np.arange(128)[None, :]
    return np.concatenate([(omax - m) * 128 + p - c for m in range(nt)], axis=1)


def _d_idx(tk, tq, rows):
    rk, ck = tk // GRID_W, tk % GRID_W
    rq, cq = tq // GRID_W, tq % GRID_W
    r0 = np.clip(rq - 4, 0, rows - 8)
    cs = np.clip(cq - 8, 0, GRID_W - 16)
    valid = (rk >= r0) & (rk < r0 + 8) & (ck >= cs) & (ck < cs + 16) & (tk >= 0) & (tk < rows * GRID_W)
    dr = np.clip(rk - rq + 7, 0, 14)
    dc = np.clip(ck - cq + 15, 0, 30)
    return valid, dr, dc


def _d_tables():
    rows = 64
    n = rows * GRID_W
    p = np.arange(128)[:, None]
    c128 = np.arange(128)[None, :]
    c512 = np.arange(512)[None, :]
    vs, drs, dcs = [], [], []
    jq0 = 16
    for m in range(D_NT):
        o = D_OMAX - m
        v, dr, dc = _d_idx((jq0 + o) * 128 + p, jq0 * 128 + c128, rows)
        vs.append(v); drs.append(dr); dcs.append(dc)
    for kb in range(6):
        v, dr, dc = _d_idx(kb * 128 + p, c512, rows)
        vs.append(v); drs.append(dr); dcs.append(dc)
    nb = n // 128
    for i in range(6):
        v, dr, dc = _d_idx((nb - 6 + i) * 128 + p, n - 512 + c512, rows)
        vs.append(v); drs.append(dr); dcs.append(dc)
    return (np.concatenate(vs, 1), np.concatenate(drs, 1), np.concatenate(dcs, 1))


def _rope_tables(n):
    t = np.arange(n)
    row = (t // GRID_W).astype(np.float32)
    col = (t % GRID_W).astype(np.float32)
    freqs = (np.float32(10000.0) ** (-np.arange(16, dtype=np.float32) * np.float32(2.0) / np.float32(32))).astype(np.float32)
    cos = np.zeros((64, n), np.float32)
    sin = np.zeros((64, n), np.float32)
    for d in range(64):
        pos = row if d < 32 else col
        j = d % 16
        ang = (pos * freqs[j]).astype(np.float32)
        cos[d] = np.cos(ang)
        s = np.sin(ang)
        sin[d] = -s if (d % 32) < 16 else s
    return cos, sin


def _swap_perm():
    perm = np.arange(64)
    for d in range(64):
        perm[d] = d + 16 if (d % 32) < 16 else d - 16
    return perm


class Lane:
    def __init__(self, nc, name, step, is_pe=False):
        self.sem = nc.semaphore(name).__enter__()
        self.cnt = 0
        self.step = step
        self.is_pe = is_pe
        self.name = name

    def mark(self, ins):
        self.cnt += self.step
        ins.then_inc(self.sem, self.step)
        return (self, self.cnt)


class Eng:
    def __init__(self, nc, eng, name, is_pe=False):
        self.eng = eng
        self.lane = Lane(nc, "s_" + name, 1, is_pe)
        self.waited = {}
        self.is_pe = is_pe

    def wait(self, tok):
        if tok is None:
            return
        lane, cnt = tok
        if lane is self.lane and self.is_pe:
            return
        if self.waited.get(lane.name, 0) >= cnt:
            return
        self.waited[lane.name] = cnt
        self.eng.wait_ge(lane.sem, cnt)


class T:
    __slots__ = ("w", "r", "ps")

    def __init__(self, ps=False):
        self.w = {}
        self.r = {}
        self.ps = ps


def do(E, fn, reads=(), writes=(), lane=None, embed=True):
    need = {}

    def req(tok):
        ln, cnt = tok
        if ln is E.lane and E.is_pe:
            return
        if E.waited.get(ln.name, 0) >= cnt:
            return
        if ln.name not in need or need[ln.name][1] < cnt:
            need[ln.name] = tok

    for t in reads:
        for wt in t.w.values():
            req(wt)
        if t.ps:
            for rt in t.r.values():
                if rt[0] is not E.lane:
                    req(rt)
    for t in writes:
        for rt in t.r.values():
            req(rt)
        for wt in t.w.values():
            req(wt)
    toks = list(need.values())
    emb = None
    if toks and embed and lane is None and EMBED_WAITS:
        emb = toks.pop()
    for ln, cnt in toks:
        E.waited[ln.name] = cnt
        E.eng.wait_ge(ln.sem, cnt)
    ins = fn()
    if emb is not None:
        E.waited[emb[0].name] = emb[1]
        ins.wait_op(emb[0].sem, emb[1], "sem-ge")
    tok = (lane or E.lane).mark(ins)
    for t in reads:
        t.r[tok[0].name] = tok
    for t in writes:
        t.w[tok[0].name] = tok
        t.r = {}
    return tok


EMBED_WAITS = True


def build_program(NP, NS, L):
    nc = bass.Bass("TRN2", target_bir_lowering=False)
    NMAX = max(NP, NS)
    seqs = [("p", NP), ("s", NS)]

    def dram(name, shape, dt, kind):
        return nc.dram_tensor(name, list(shape), dt, kind=kind)

    x_in = {"p": dram("xp", [NP, D], F32, "ExternalInput"), "s": dram("xs", [NS, D], F32, "ExternalInput")}
    y_out = {"p": dram("yp", [NP, D], F32, "ExternalOutput"), "s": dram("ys", [NS, D], F32, "ExternalOutput")}
    win = dram("win", [L * 4, D, NCOL], F32, "ExternalInput")
    wout = dram("wout", [L, D, D], F32, "ExternalInput")
    gbc = dram("gbc", [L + 1, 128, D], F32, "ExternalInput")
    again = dram("again", [L, 64, 4], F32, "ExternalInput")
    clam = dram("clam", [L, 64, 4 * 32], F32, "ExternalInput")
    csub = dram("csub", [L, 64, 1], F32, "ExternalInput")
    cfar = dram("cfar", [4, 128, 2], F32, "ExternalInput")
    tabB = dram("tabB", [4, 128, TB_B], F32, "ExternalInput")
    mulB = dram("mulB", [128, TB_B], F32, "ExternalInput")
    tabC = dram("tabC", [4, 128, TB_C], F32, "ExternalInput")
    tabD = dram("tabD", [L * 4, 128, TB_D], F32, "ExternalInput")
    mskD = dram("mskD", [128, TB_D], F32, "ExternalInput")
    ropec = dram("ropec", [64, NMAX], F32, "ExternalInput")
    ropes = dram("ropes", [64, NMAX], F32, "ExternalInput")
    ident_d = dram("ident", [128, 128], F32, "ExternalInput")

    xnT = {k: dram("xnT_" + k, [8, 128, n], BF16, "Internal") for k, n in seqs}
    x1 = {k: dram("x1_" + k, [n, D], F32, "Internal") for k, n in seqs}
    mixT = {k: dram("mixT_" + k, [D, n], BF16, "Internal") for k, n in seqs}
    gT = {k: dram("gT_" + k, [4, 64, n], F32, "Internal") for k, n in seqs}

    def sb(name, shape, dt):
        return nc.sbuf_tensor(name, list(shape), dt).__enter__()

    NBMAX = NMAX // 128
    QT = sb("QT", [128, NMAX], BF16)
    KT = sb("KT", [128, NMAX], BF16)
    KT2 = sb("KT2", [64, NMAX], BF16)
    VA = sb("VA", [128, NBMAX, 2, 65], BF16)
    WB = sb("WB", [128, TB_B], BF16)
    WC = sb("WC", [128, TB_C], BF16)
    WD = sb("WD", [128, TB_D], BF16)
    wbuf = sb("wbuf", [128, 8, NCOL], BF16)
    wstage = [sb("wstage%d" % i, [128, 1024], F32) for i in range(2)]
    xt = [sb("xt%d" % i, [128, 8, 512], BF16) for i in range(2)]
    PT = [sb("PT%d" % i, [128, 1024], BF16) for i in range(4)]
    ftmp = [sb("ftmp%d" % i, [128, 512], F32) for i in range(10)]
    gtile = [sb("gtile%d" % i, [64, 512], F32) for i in range(2)]
    ropt = [[sb("rc%d" % i, [64, 512], F32), sb("rs%d" % i, [64, 512], F32)] for i in range(2)]
    mxt = [sb("mxt%d" % i, [64, 512], BF16) for i in range(2)]
    xrow = [sb("xrow%d" % i, [128, D], F32) for i in range(2)]
    xnrow = [sb("xnrow%d" % i, [128, D], F32) for i in range(2)]
    xnTs = [sb("xnTs%d" % i, [128, 8, 128], BF16) for i in range(2)]
    mts = [sb("mts%d" % i, [128, 8, 128], BF16) for i in range(2)]
    gsb = sb("gsb", [128, D], F32)
    ident = sb("identsb", [128, 128], F32)
    ones64 = sb("ones64", [128, 128], F32)
    sel65 = sb("sel65", [128, 128], F32)
    small = sb("small", [128, 32], F32)
    nsm = [sb("nsm%d" % i, [128, 4], F32) for i in range(2)]
    lamt = sb("lamt", [64, 4 * 32], F32)
    cfar_sb = sb("cfar_sb", [128, 2], F32)
    ps = nc.psum_tensor("ps", [128, 8, 512], F32).__enter__()

    PE = Eng(nc, nc.tensor, "pe", True)
    ACT = Eng(nc, nc.scalar, "act")
    DVE = Eng(nc, nc.vector, "dve")
    POOL = Eng(nc, nc.gpsimd, "pool")
    SP = Eng(nc, nc.sync, "sp")
    dl = {}

    def lane_for(key):
        if key not in dl:
            dl[key] = Lane(nc, "d_" + str(key).replace(" ", "").replace("'", "").replace("(", "").replace(")", "").replace(",", "_"), 16)
        return dl[key]

    NT5 = NMAX // 512
    tQT = [T() for _ in range(NT5)]
    tKT = [T() for _ in range(NT5)]
    tKT2 = [T() for _ in range(NT5)]
    tVA = [T() for _ in range(NT5)]
    tWB, tWC, tWD, tw = T(), T(), T(), T()
    twst = [T(), T()]
    txt = [T(), T()]
    tPT = [[T(), T()] for _ in PT]
    tf = [T() for _ in ftmp]
    tg = [T(), T()]
    trp = [T(), T()]
    tmx = [T(), T()]
    txrow = [T(), T()]
    txn = [T(), T()]
    txnT = [T(), T()]
    tmts = [T(), T()]
    tnsm = [T(), T()]
    tgsb, tident, tones, tsel, tsmall, tlam, tcfar = T(), T(), T(), T(), T(), T(), T()
    tps = [T(ps=True) for _ in range(8)]
    tD = {nm: {k: T() for k, _ in seqs} for nm in ("xnT", "x1", "mixT", "gT")}
    tDx1 = {(k, l_): T() for k, _ in seqs for l_ in range(L)}
    fidx = {id(t_): i_ for i_, t_ in enumerate(tf)}

    def dma(E, lane, out, in_, reads=(), writes=()):
        return do(E, lambda: E.eng.dma_start(out=out, in_=in_), reads, writes, lane=lane_for(lane))

    dma(SP, "ident", ident[:, :], ident_d[:, :], writes=[tident])
    do(DVE, lambda: nc.vector.memset(ones64[:, :], 0.0), writes=[tones])
    do(DVE, lambda: nc.vector.memset(ones64[0:64, :], 1.0 / 64.0), writes=[tones])
    do(DVE, lambda: nc.vector.memset(sel65[:, :], 0.0), writes=[tsel])
    do(DVE, lambda: nc.vector.memset(sel65[64:65, :], 1.0), writes=[tsel])
    for i_ in range(len(ftmp)):
        do(DVE, lambda i_=i_: nc.vector.memset(ftmp[i_][:, :], 0.0), writes=[tf[i_]])
    do(DVE, lambda: nc.vector.memset(VA[:, :, :, :], 1.0), writes=tVA)
    do(DVE, lambda: nc.vector.memset(KT2[:, :], 0.0), writes=tKT2)

    ftmp_rr = [0]

    def ft():
        i = ftmp_rr[0] % len(ftmp)
        ftmp_rr[0] += 1
        return ftmp[i], tf[i]

    rr = {}

    def nxt(key, n):
        i = rr.get(key, 0) % n
        rr[key] = rr.get(key, 0) + 1
        return i

    def norm_block(xr, txr, seqk, b, final):
        i = nxt("xn", 2)
        xn, tx = xnrow[i], txn[i]
        si = nxt("nsm", 2)
        sm, tsm = nsm[si], tnsm[si]
        do(ACT, lambda: nc.scalar.activation(out=xn[:, :], in_=xr[:, :], func=AF.Square, accum_out=sm[:, 0:1]),
           reads=[txr], writes=[tx, tsm], embed=False)
        do(ACT, lambda: nc.scalar.activation(out=sm[:, 1:2], in_=sm[:, 0:1], func=AF.Ln, bias=EPS, scale=1.0 / D),
           writes=[tsm])
        do(ACT, lambda: nc.scalar.activation(out=sm[:, 2:3], in_=sm[:, 1:2], func=AF.Exp, scale=-0.5),
           writes=[tsm])
        do(DVE, lambda: nc.vector.scalar_tensor_tensor(out=xn[:, :], in0=xr[:, :], scalar=sm[:, 2:3], in1=gsb[:, :],
                                                        op0=ALU.mult, op1=ALU.mult),
           reads=[txr, tsm, tgsb], writes=[tx])
        if final:
            dma(POOL, ("st_xn", i), y_out[seqk][b * 128:(b + 1) * 128, :], xn[:, :], reads=[tx])
            return
        j = nxt("xnT", 2)
        for half in range(2):
            bank = 4 + half
            for c4 in range(4):
                c = half * 4 + c4
                do(PE, lambda c=c, c4=c4, bank=bank: nc.tensor.transpose(out=ps[:, bank, c4 * 128:(c4 + 1) * 128],
                                                                         in_=xn[:, c * 128:(c + 1) * 128], identity=ident[:, :]),
                   reads=[tx, tident], writes=[tps[bank]])
            src = ps[:, bank, :].rearrange("p (c t) -> p c t", c=4)
            dst = xnTs[j][:, half * 4:(half + 1) * 4, :]
            if half == 0:
                do(ACT, lambda src=src, dst=dst: nc.scalar.copy(out=dst, in_=src), reads=[tps[bank]], writes=[txnT[j]])
            else:
                do(DVE, lambda src=src, dst=dst: nc.vector.tensor_copy(out=dst, in_=src), reads=[tps[bank]], writes=[txnT[j]])
        dma(POOL, ("st_xnT", j), xnT[seqk][:, :, b * 128:(b + 1) * 128].rearrange("c p t -> p c t"), xnTs[j][:, :, :],
            reads=[txnT[j]], writes=[tD["xnT"][seqk]])

    def load_g(l):
        dma(SP, "gsb", gsb[:, :], gbc[l, :, :], writes=[tgsb])

    def load_cast(dst_ap_fn, src_ap_fn, ncols_total, tdst):
        per = 128
        flip = 0
        for c0 in range(0, ncols_total, per):
            w = min(per, ncols_total - c0)
            i = nxt("wst", 2)
            st = wstage[i]
            stv = st[:, 0:8 * w].rearrange("p (c n) -> p c n", c=8)
            dma(SP, ("wst", i), stv, src_ap_fn(c0, w), writes=[twst[i]])
            if flip % 2 == 0:
                do(ACT, lambda stv=stv, c0=c0, w=w: nc.scalar.copy(out=dst_ap_fn(c0, w), in_=stv), reads=[twst[i]], writes=[tdst])
            else:
                do(DVE, lambda stv=stv, c0=c0, w=w: nc.vector.tensor_copy(out=dst_ap_fn(c0, w), in_=stv), reads=[twst[i]], writes=[tdst])
            flip += 1

    def build_tables(l, hg):
        def one(Wt, tW, src, ncols, mul_src):
            for c0 in range(0, ncols, 1024):
                w = min(1024, ncols - c0)
                i = nxt("wst", 2)
                st = wstage[i]
                dma(SP, ("wst", i), st[:, 0:w], src[:, c0:c0 + w], writes=[twst[i]])
                if mul_src is None:
                    do(ACT, lambda st=st, c0=c0, w=w: nc.scalar.activation(out=Wt[:, c0:c0 + w], in_=st[:, 0:w], func=AF.Exp),
                       reads=[twst[i]], writes=[tW])
                else:
                    do(ACT, lambda st=st, w=w: nc.scalar.activation(out=st[:, 0:w], in_=st[:, 0:w], func=AF.Exp),
                       reads=[twst[i]], writes=[twst[i]])
                    i2 = nxt("wst", 2)
                    st2 = wstage[i2]
                    dma(SP, ("wst", i2), st2[:, 0:w], mul_src[:, c0:c0 + w], writes=[twst[i2]])
                    do(DVE, lambda st=st, st2=st2, c0=c0, w=w: nc.vector.tensor_tensor(out=Wt[:, c0:c0 + w], in0=st[:, 0:w],
                                                                                        in1=st2[:, 0:w], op=ALU.mult),
                       reads=[twst[i], twst[i2]], writes=[tW])
        one(WB, tWB, tabB[hg], TB_B, mulB)
        one(WC, tWC, tabC[hg], TB_C, None)
        one(WD, tWD, tabD[l * 4 + hg], TB_D, mskD)
        dma(SP, "cfar", cfar_sb[:, :], cfar[hg, :, :], writes=[tcfar])

    def layer_params(l):
        dma(SP, "small", small[0:64, 8:12], again[l, :, :], writes=[tsmall])
        dma(SP, "lamt", lamt[:, :], clam[l, :, :], writes=[tlam])
        dma(SP, "small", small[0:64, 13:14], csub[l, :, :], writes=[tsmall])
        li = 0.8 - 0.6 * math.exp(-0.3 * l)
        pr, tpr = ft()
        do(DVE, lambda: nc.vector.tensor_tensor(out=pr[0:64, 0:32], in0=lamt[:, 0:32], in1=lamt[:, 32:64], op=ALU.mult),
           reads=[tlam], writes=[tpr])
        do(DVE, lambda: nc.vector.tensor_tensor(out=pr[0:64, 32:64], in0=lamt[:, 64:96], in1=lamt[:, 96:128], op=ALU.mult),
           reads=[tlam], writes=[tpr])
        do(DVE, lambda: nc.vector.reduce_sum(out=small[0:64, 14:15], in_=pr[0:64, 0:32], axis=mybir.AxisListType.X),
           reads=[tpr], writes=[tsmall])
        do(DVE, lambda: nc.vector.reduce_sum(out=small[0:64, 15:16], in_=pr[0:64, 32:64], axis=mybir.AxisListType.X),
           reads=[tpr], writes=[tsmall])
        do(ACT, lambda: nc.scalar.activation(out=small[0:64, 16:18], in_=small[0:64, 14:16], func=AF.Exp),
           writes=[tsmall])
        do(DVE, lambda: nc.vector.scalar_tensor_tensor(out=small[0:64, 12:13], in0=small[0:64, 17:18], scalar=-li,
                                                        in1=small[0:64, 16:17], op0=ALU.add, op1=ALU.subtract),
           writes=[tsmall])
        do(DVE, lambda: nc.vector.tensor_scalar(out=small[0:64, 13:14], in0=small[0:64, 13:14], scalar1=1.0 - li, scalar2=None,
                                                op0=ALU.mult),
           writes=[tsmall])

    def load_win(l, hg):
        load_cast(lambda c0, w: wbuf[:, :, c0:c0 + w],
                  lambda c0, w: win[l * 4 + hg, :, c0:c0 + w].rearrange("(c p) n -> p c n", p=128),
                  NCOL, tw)

    def inproj(seqk, n, l, hg, pr_):
        BST = globals().get("BSTEP", 99)
        for tt in range(n // 512):
            t0 = tt * 512
            xi = nxt("xt", 2)
            x_t, tx = xt[xi], txt[xi]
            dma(SP, ("xt", xi), x_t[:, :, :], xnT[seqk][:, :, t0:t0 + 512].rearrange("c p t -> p c t"),
                reads=[tD["xnT"][seqk]], writes=[tx])
            if pr_ == 0:
                ri = nxt("rp", 2)
                dma(SP, ("rp", ri), ropt[ri][0][:, :], ropec[:, t0:t0 + 512], writes=[trp[ri]])
                dma(SP, ("rp", ri), ropt[ri][1][:, :], ropes[:, t0:t0 + 512], writes=[trp[ri]])

            def proj(bank, col0, m):
                for c in range(8):
                    do(PE, lambda c=c: nc.tensor.matmul(out=ps[0:m, bank, :], lhsT=wbuf[:, c, col0:col0 + m], rhs=x_t[:, c, :],
                                                        start=(c == 0), stop=(c == 7)),
                       reads=[tx, tw], writes=[tps[bank]])

            def a_part(bank_main, bank_sw, gcol, dst, tdst):
                BSUB = globals().get("BSUB", 99)
                sq, tsq = ft()
                do(ACT, lambda: nc.scalar.activation(out=sq[0:64, :], in_=ps[0:64, bank_main, :], func=AF.Square),
                   reads=[tps[bank_main]], writes=[tsq])
                if globals().get("VARIANT", 0) in (7, 8):
                    t9, tt9 = ft()
                    do(DVE, lambda: nc.vector.tensor_copy(out=t9[:, :], in_=ps[:, bank_main, :]),
                       reads=[tps[bank_main]] + ([tsq] if globals().get("VARIANT", 0) == 8 else []), writes=[tt9])
                if BSUB <= 1:
                    return
                do(PE, lambda: nc.tensor.matmul(out=ps[:, 7, :], lhsT=ones64[:, :], rhs=sq[:, :], start=True, stop=True),
                   reads=[tsq, tones], writes=[tps[7]])
                if BSUB <= 2:
                    return
                ln_, tln = ft()
                do(ACT, lambda: nc.scalar.activation(out=ln_[0:64, :], in_=ps[0:64, 7, :], func=AF.Ln, bias=EPS, scale=1.0),
                   reads=[tps[7]], writes=[tln])
                if BSUB <= 3:
                    return
                do(ACT, lambda: nc.scalar.activation(out=ln_[0:64, :], in_=ln_[0:64, :], func=AF.Exp, scale=-0.5),
                   writes=[tln])
                if BSUB <= 4:
                    return
                t1, tt1 = ft()
                VAR = globals().get("VARIANT", 0)
                if VAR == 0:
                    do(DVE, lambda: nc.vector.scalar_tensor_tensor(out=t1[0:64, :], in0=ps[0:64, bank_main, :],
                                                                    scalar=small[0:64, gcol:gcol + 1], in1=ropt[ri][0][:, :],
                                                                    op0=ALU.mult, op1=ALU.mult),
                       reads=[tps[bank_main], tsmall, trp[ri]], writes=[tt1])
                elif VAR == 1:
                    do(DVE, lambda: nc.vector.tensor_scalar(out=t1[0:64, :], in0=ps[0:64, bank_main, :],
                                                            scalar1=small[0:64, gcol:gcol + 1], scalar2=None, op0=ALU.mult),
                       reads=[tps[bank_main], tsmall], writes=[tt1])
                elif VAR == 2:
                    do(DVE, lambda: nc.vector.tensor_tensor(out=t1[0:64, :], in0=ps[0:64, bank_main, :], in1=ropt[ri][0][:, :], op=ALU.mult),
                       reads=[tps[bank_main], trp[ri]], writes=[tt1])
                elif VAR == 4:
                    do(DVE, lambda: nc.vector.tensor_copy(out=t1[0:64, :], in_=ps[0:64, bank_main, :]),
                       reads=[tps[bank_main]], writes=[tt1])
                elif VAR == 5:
                    do(DVE, lambda: nc.vector.tensor_copy(out=t1[:, :], in_=ps[:, bank_main, :]),
                       reads=[tps[bank_main]], writes=[tt1])
                elif VAR == 6:
                    do(DVE, lambda: nc.vector.tensor_copy(out=t1[0:64, :], in_=sq[0:64, :]),
                       reads=[tsq], writes=[tt1])
                elif VAR == 3:
                    do(DVE, lambda: nc.vector.scalar_tensor_tensor(out=t1[0:64, :], in0=ps[0:64, bank_main, :],
                                                                    scalar=2.0, in1=ropt[ri][0][:, :],
                                                                    op0=ALU.mult, op1=ALU.mult),
                       reads=[tps[bank_main], trp[ri]], writes=[tt1])
                if BSUB <= 5:
                    return
                t2, tt2 = ft()
                do(DVE, lambda: nc.vector.scalar_tensor_tensor(out=t2[0:64, :], in0=ps[0:64, bank_sw, :],
                                                                scalar=small[0:64, gcol + 1:gcol + 2], in1=ropt[ri][1][:, :],
                                                                op0=ALU.mult, op1=ALU.mult),
                   reads=[tps[bank_sw], tsmall, trp[ri]], writes=[tt2])
                do(DVE, lambda: nc.vector.tensor_tensor(out=t1[0:64, :], in0=t1[0:64, :], in1=t2[0:64, :], op=ALU.add),
                   reads=[tt2], writes=[tt1])
                if BSUB <= 6:
                    return
                do(DVE, lambda: nc.vector.tensor_tensor(out=dst[0:64, t0:t0 + 512], in0=t1[0:64, :], in1=ln_[0:64, :], op=ALU.mult),
                   reads=[tt1, tln], writes=[tdst])

            if BST <= 1:
                continue
            if pr_ == 0:
                proj(0, C_AQ * 64, 128)
                if BST <= 2:
                    continue
                proj(1, C_AQS * 64, 64)
                if BST <= 3:
                    continue
                a_part(0, 1, 8, QT, tQT[tt])
                if BST <= 4:
                    continue
                do(ACT, lambda: nc.scalar.copy(out=QT[64:128, t0:t0 + 512], in_=ps[64:128, 0, :]),
                   reads=[tps[0]], writes=[tQT[tt]])
                if BST <= 5:
                    continue
                proj(2, C_AK * 64, 128)
                proj(3, C_AKS * 64, 64)
                a_part(2, 3, 10, KT, tKT[tt])
                do(ACT, lambda: nc.scalar.copy(out=KT[64:128, t0:t0 + 512], in_=ps[64:128, 2, :]),
                   reads=[tps[2]], writes=[tKT[tt]])
                gcol0, vcol0 = C_AG, C_AV
            else:
                proj(0, C_CQ * 64, 128)
                do(DVE, lambda: nc.vector.tensor_copy(out=QT[:, t0:t0 + 512], in_=ps[:, 0, :]),
                   reads=[tps[0]], writes=[tQT[tt]])
                proj(2, C_CK * 64, 128)
                do(DVE, lambda: nc.vector.tensor_copy(out=KT[0:32, t0:t0 + 512], in_=ps[0:32, 2, :]),
                   reads=[tps[2]], writes=[tKT[tt]])
                do(DVE, lambda: nc.vector.memset(KT[32:64, t0:t0 + 512], 0.0), writes=[tKT[tt]])
                do(DVE, lambda: nc.vector.tensor_copy(out=KT2[32:64, t0:t0 + 512], in_=ps[32:64, 2, :]),
                   reads=[tps[2]], writes=[tKT2[tt]])
                do(DVE, lambda: nc.vector.tensor_copy(out=KT[64:128, t0:t0 + 512], in_=ps[64:128, 2, :]),
                   reads=[tps[2]], writes=[tKT[tt]])
                gcol0, vcol0 = C_CG, C_CV
            if BST <= 6:
                continue
            proj(4, gcol0 * 64, 128)
            gt_, tgt = ft()
            do(ACT, lambda gt_=gt_: nc.scalar.activation(out=gt_[:, :], in_=ps[:, 4, :], func=AF.Silu),
               reads=[tps[4]], writes=[tgt])
            dma(POOL, ("st_f", fidx[id(tgt)]), gT[seqk][pr_ * 2:pr_ * 2 + 2, :, t0:t0 + 512].rearrange("m p t -> (m p) t"), gt_[:, :],
                reads=[tgt], writes=[tD["gT"][seqk]])
            if BST <= 7:
                continue
            for s4 in range(4):
                bank = 5 + (s4 % 2)
                for c in range(8):
                    do(PE, lambda c=c, s4=s4, bank=bank: nc.tensor.matmul(out=ps[:, bank, 0:128], lhsT=x_t[:, c, s4 * 128:(s4 + 1) * 128],
                                                                          rhs=wbuf[:, c, vcol0 * 64:vcol0 * 64 + 128],
                                                                          start=(c == 0), stop=(c == 7)),
                       reads=[tx, tw], writes=[tps[bank]])
                blk = tt * 4 + s4
                srcv = ps[:, bank, 0:128].rearrange("p (m d) -> p m d", m=2)
                if s4 % 2 == 0:
                    do(DVE, lambda blk=blk, srcv=srcv: nc.vector.tensor_copy(out=VA[:, blk, :, 0:64], in_=srcv),
                       reads=[tps[bank]], writes=[tVA[tt]])
                else:
                    do(ACT, lambda blk=blk, srcv=srcv: nc.scalar.copy(out=VA[:, blk, :, 0:64], in_=srcv),
                       reads=[tps[bank]], writes=[tVA[tt]])

    def attention(seqk, n, l, hg, pr_):
        nb = n // 128
        nq = n // 512
        sc64 = HD ** -0.5
        sc32 = 32 ** -0.5
        runs = []
        for m in (2 * pr_, 2 * pr_ + 1):
            for j in range(nq):
                units = []
                if m == 0:
                    for kb in range(0, nb, 2):
                        units.append((kb, "plain", None, None))
                    runs.append((m, 0, j, units))
                elif m == 1:
                    for kb in range(max(0, 4 * j - 8), min(nb, 4 * j + 12), 2):
                        v = [(B_OMAX - (k - 4 * j)) * 128 for k in (kb, kb + 1)]
                        units.append((kb, "w", WB[:, v[0]:v[0] + 512], WB[:, v[1]:v[1] + 512]))
                    runs.append((m, 0, j, units))
                elif m == 2:
                    for kb in range(0, nb, 2):
                        o = kb - 4 * j
                        if -2 <= o <= 4:
                            v = [(C_OMAX - (k - 4 * j)) * 128 for k in (kb, kb + 1)]
                            units.append((kb, "w", WC[:, v[0]:v[0] + 512], WC[:, v[1]:v[1] + 512]))
                        else:
                            units.append((kb, "farL" if o < 0 else "farR", None, None))
                    runs.append((m, 0, j, units))
                    runs.append((m, 1, j, units))
                else:
                    if j == 0:
                        for i in range(0, 6, 2):
                            b0 = D_NT * 128 + i * 512
                            units.append((i, "w", WD[:, b0:b0 + 512], WD[:, b0 + 512:b0 + 1024]))
                    elif j == nq - 1:
                        for i in range(0, 6, 2):
                            b0 = D_NT * 128 + (6 + i) * 512
                            units.append((nb - 6 + i, "w", WD[:, b0:b0 + 512], WD[:, b0 + 512:b0 + 1024]))
                    else:
                        for kb in range(4 * j - 2, 4 * j + 6, 2):
                            v = [(D_OMAX - (k - 4 * j)) * 128 for k in (kb, kb + 1)]
                            units.append((kb, "w", WD[:, v[0]:v[0] + 512], WD[:, v[1]:v[1] + 512]))
                    runs.append((m, 0, j, units))

        flat = []
        for ri_, (m, mp, j, units) in enumerate(runs):
            for ui, u in enumerate(units):
                flat.append((ri_, ui, len(units), m, mp, j, u))

        run_obank = {}
        unit_pi = {}
        pending = []
        c_hold = {}

        def qk_operands(m, mp, j, kb):
            p0 = 64 * (m % 2)
            if m == 2 and mp == 1:
                return (KT2[0:64, kb * 128:(kb + 1) * 128], QT[0:64, j * 512:(j + 1) * 512], tKT2[kb // 4], tQT[j])
            return (KT[p0:p0 + 64, kb * 128:(kb + 1) * 128], QT[p0:p0 + 64, j * 512:(j + 1) * 512], tKT[kb // 4], tQT[j])

        def emit_S(idx):
            ri_, ui, nu, m, mp, j, (kb, mode, w0, w1) = flat[idx]
            sp_ = idx % 2
            for h in range(2):
                lhsT, rhs, tk, tq = qk_operands(m, mp, j, kb + h)
                bank = sp_ * 2 + h
                do(PE, lambda lhsT=lhsT, rhs=rhs, bank=bank: nc.tensor.matmul(out=ps[:, bank, :], lhsT=lhsT, rhs=rhs, start=True, stop=True),
                   reads=[tk, tq], writes=[tps[bank]])

        def finalize_A(ob):
            osb, tosb = ft()
            ohi, tohi = ft()
            i_lo, i_hi = fidx[id(tosb)], fidx[id(tohi)]
            if i_hi == i_lo + 1:
                pass
            do(DVE, lambda: nc.vector.tensor_copy(out=osb[0:65, :], in_=ps[0:65, ob, :]), reads=[tps[ob]], writes=[tosb])
            do(DVE, lambda: nc.vector.tensor_tensor(out=osb[0:65, :], in0=ps[0:65, ob + 1, :], in1=osb[0:65, :], op=ALU.add),
               reads=[tps[ob + 1]], writes=[tosb])
            return osb, tosb, ob

        def finalize_B(m, j, holders):
            outs = []
            for (osb, tosb, ob_) in holders:
                do(PE, lambda osb=osb: nc.tensor.matmul(out=ps[:, 6, :], lhsT=sel65[:, :], rhs=osb[:, :], start=True, stop=True),
                   reads=[tosb, tsel], writes=[tps[6]])
                rd, trd = ft()
                do(ACT, lambda rd=rd: nc.scalar.activation(out=rd[0:64, :], in_=ps[0:64, 6, :], func=AF.Ln), reads=[tps[6]], writes=[trd])
                do(ACT, lambda rd=rd: nc.scalar.activation(out=rd[0:64, :], in_=rd[0:64, :], func=AF.Exp, scale=-1.0), writes=[trd])
                do(DVE, lambda rd=rd, osb=osb: nc.vector.tensor_tensor(out=rd[0:64, :], in0=osb[0:64, :], in1=rd[0:64, :], op=ALU.mult),
                   reads=[tosb], writes=[trd])
                outs.append((rd, trd))
            o, to = outs[0]
            gi = nxt("g", 2)
            dma(SP, ("g", gi), gtile[gi][:, :], gT[seqk][m, :, j * 512:(j + 1) * 512], reads=[tD["gT"][seqk]], writes=[tg[gi]])
            if m == 2:
                o2, to2 = outs[1]
                do(DVE, lambda: nc.vector.scalar_tensor_tensor(out=o[0:64, :], in0=o2[0:64, :], scalar=small[0:64, 12:13], in1=o[0:64, :],
                                                                op0=ALU.mult, op1=ALU.add),
                   reads=[to2, tsmall], writes=[to])
                sq, tsq = ft()
                do(ACT, lambda: nc.scalar.activation(out=sq[0:64, :], in_=o[0:64, :], func=AF.Square), reads=[to], writes=[tsq])
                obc = 6
                do(PE, lambda: nc.tensor.matmul(out=ps[:, obc, :], lhsT=ones64[:, :], rhs=sq[:, :], start=True, stop=True),
                   reads=[tsq, tones], writes=[tps[obc]])
                do(ACT, lambda: nc.scalar.activation(out=sq[0:64, :], in_=ps[0:64, obc, :], func=AF.Ln, bias=EPS, scale=1.0),
                   reads=[tps[obc]], writes=[tsq])
                do(ACT, lambda: nc.scalar.activation(out=sq[0:64, :], in_=sq[0:64, :], func=AF.Exp, scale=-0.5), writes=[tsq])
                do(DVE, lambda: nc.vector.scalar_tensor_tensor(out=o[0:64, :], in0=o[0:64, :], scalar=small[0:64, 13:14], in1=sq[0:64, :],
                                                                op0=ALU.mult, op1=ALU.mult),
                   reads=[tsq, tsmall], writes=[to])
            mi = nxt("mx", 2)
            do(DVE, lambda: nc.vector.tensor_tensor(out=mxt[mi][:, :], in0=o[0:64, :], in1=gtile[gi][:, :], op=ALU.mult),
               reads=[to, tg[gi]], writes=[tmx[mi]])
            r0 = m * 256 + hg * 64
            dma(SP, ("st_mx", mi), mixT[seqk][r0:r0 + 64, j * 512:(j + 1) * 512], mxt[mi][:, :], reads=[tmx[mi]], writes=[tD["mixT"][seqk]])

        def emit_rest(idx):
            ri_, ui, nu, m, mp, j, (kb, mode, w0, w1) = flat[idx]
            sp_ = idx % 2
            pi = nxt("pt", len(PT))
            scale = sc32 if m == 2 else sc64
            src = ps[:, sp_ * 2:sp_ * 2 + 2, :]
            dst = PT[pi][:, :].rearrange("p (b q) -> p b q", b=2)
            if mode in ("farL", "farR"):
                bcol = cfar_sb[:, 0:1] if mode == "farL" else cfar_sb[:, 1:2]
                do(ACT, lambda: nc.scalar.activation(out=dst, in_=src, func=AF.Exp, bias=bcol, scale=scale),
                   reads=[tps[sp_ * 2], tps[sp_ * 2 + 1], tcfar], writes=tPT[pi])
            else:
                do(ACT, lambda: nc.scalar.activation(out=dst, in_=src, func=AF.Exp, scale=scale),
                   reads=[tps[sp_ * 2], tps[sp_ * 2 + 1]], writes=tPT[pi])
            if mode == "w":
                tW = {1: tWB, 2: tWC, 3: tWD}[m]
                do(DVE, lambda: nc.vector.tensor_tensor(out=PT[pi][:, 0:512], in0=PT[pi][:, 0:512], in1=w0, op=ALU.mult),
                   reads=[tW], writes=[tPT[pi][0]])
                do(POOL, lambda: nc.gpsimd.tensor_tensor(out=PT[pi][:, 512:1024], in0=PT[pi][:, 512:1024], in1=w1, op=ALU.mult),
                   reads=[tW], writes=[tPT[pi][1]])
            unit_pi[idx] = pi

        def emit_pv(idx):
            ri_, ui, nu, m, mp, j, (kb, mode, w0, w1) = flat[idx]
            pi = unit_pi.pop(idx)
            if ui == 0:
                run_obank[ri_] = 4
            ob = run_obank[ri_]
            ml = m % 2
            for h in range(2):
                k = kb + h
                for half in range(2):
                    r0_ = 64 * half
                    do(PE, lambda h=h, k=k, half=half, r0_=r0_: nc.tensor.matmul(
                        out=ps[0:65, ob + half, :], lhsT=VA[r0_:r0_ + 64, k, ml, :], rhs=PT[pi][r0_:r0_ + 64, h * 512:(h + 1) * 512],
                        start=(ui == 0 and h == 0), stop=(ui == nu - 1 and h == 1)),
                       reads=[tPT[pi][h], tVA[k // 4]], writes=[tps[ob + half]])
            if ui == nu - 1:
                hold = finalize_A(ob)
                if m == 2:
                    c_hold.setdefault(j, []).append(hold)
                    if mp == 1:
                        hs = c_hold.pop(j)
                        pending.append((idx + 3, lambda hs=hs, m=m, j=j: finalize_B(m, j, hs)))
                else:
                    pending.append((idx + 3, lambda hold=hold, m=m, j=j: finalize_B(m, j, [hold])))

        NF = len(flat)
        emit_S(0)
        if NF > 1:
            emit_S(1)
        for idx in range(NF):
            emit_rest(idx)
            if idx + 2 < NF:
                emit_S(idx + 2)
            emit_pv(idx)
            while pending and pending[0][0] <= idx:
                pending.pop(0)[1]()
        while pending:
            pending.pop(0)[1]()

    def outphase(seqk, n, l):
        last = (l == L - 1)
        load_cast(lambda c0, w: wbuf[:, :, c0:c0 + w],
                  lambda c0, w: wout[l, :, c0:c0 + w].rearrange("(c p) n -> p c n", p=128),
                  D, tw)
        load_g(l + 1)
        xsrc = x_in[seqk] if l == 0 else x1[seqk]
        for b in range(n // 128):
            mi = nxt("mts", 2)
            dma(SP, ("mts", mi), mts[mi][:, :, :], mixT[seqk][:, b * 128:(b + 1) * 128].rearrange("(c p) t -> p c t", p=128),
                reads=[tD["mixT"][seqk]], writes=[tmts[mi]])
            xi = nxt("xrow", 2)
            dma(SP, ("xrow", xi), xrow[xi][:, :], xsrc[b * 128:(b + 1) * 128, :], reads=([tDx1[(seqk, l - 1)]] if l > 0 else []), writes=[txrow[xi]])
            for half in range(2):
                bank = half + 2 * (b % 2)
                for c in range(8):
                    do(PE, lambda c=c, half=half, bank=bank: nc.tensor.matmul(out=ps[:, bank, :], lhsT=mts[mi][:, c, :],
                                                                              rhs=wbuf[:, c, half * 512:(half + 1) * 512],
                                                                              start=(c == 0), stop=(c == 7)),
                       reads=[tmts[mi], tw], writes=[tps[bank]])
                do(DVE, lambda half=half, bank=bank: nc.vector.tensor_tensor(out=xrow[xi][:, half * 512:(half + 1) * 512],
                                                                             in0=ps[:, bank, :], in1=xrow[xi][:, half * 512:(half + 1) * 512],
                                                                             op=ALU.add),
                   reads=[tps[bank]], writes=[txrow[xi]])
            if not last:
                dma(POOL, ("st_xrow", xi), x1[seqk][b * 128:(b + 1) * 128, :], xrow[xi][:, :], reads=[txrow[xi]], writes=[tDx1[(seqk, l)]])
            norm_block(xrow[xi], txrow[xi], seqk, b, final=last)

    load_g(0)
    for seqk, n in seqs:
        for b in range(n // 128):
            xi = nxt("xrow", 2)
            dma(SP, ("xrow", xi), xrow[xi][:, :], x_in[seqk][b * 128:(b + 1) * 128, :], writes=[txrow[xi]])
            norm_block(xrow[xi], txrow[xi], seqk, b, final=False)
    BIS = globals().get("BISECT", "full")
    for l in range(L):
        if BIS == "norm":
            break
        layer_params(l)
        for hg in range(4):
            build_tables(l, hg)
            if BIS == "tables":
                continue
            load_win(l, hg)
            if BIS == "loadwin":
                continue
            for seqk, n in seqs:
                for pr_ in range(2):
                    inproj(seqk, n, l, hg, pr_)
                    if BIS == "inproj":
                        continue
                    attention(seqk, n, l, hg, pr_)
        if BIS in ("tables", "inproj", "attn", "loadwin"):
            continue
        for seqk, n in seqs:
            outphase(seqk, n, l)
    engs = [PE, ACT, DVE, POOL, SP]
    for E in engs:
        for O in engs:
            if O is not E and O.lane.cnt > 0:
                E.eng.wait_ge(O.lane.sem, O.lane.cnt)
        for k, ln in dl.items():
            if ln.cnt > 0:
                E.eng.wait_ge(ln.sem, ln.cnt)
    return nc


def prepare_shared(w_in, w_out, norm_g, final_g, a_q_gain, a_k_gain, t5_bias, c_lambda_q1, c_lambda_k1,
                   c_lambda_q2, c_lambda_k2, c_subln_g, d_rpb, NMAX):
    L = w_in.shape[0]
    f = np.float32
    w_in = np.asarray(w_in, f); w_out = np.asarray(w_out, f)
    perm = _swap_perm()
    offs = np.cumsum([0, 256, 128, 128, 256] + [256] * 12)
    names = ["aq", "ak", "av", "ag", "bq", "bk", "bv", "bg", "cq", "ck", "cv", "cg", "dq", "dk", "dv", "dg"]
    o = {nm: int(offs[i]) for i, nm in enumerate(names)}
    win = np.zeros((L * 4, D, NCOL), f)
    for l in range(L):
        for hg in range(4):
            W = w_in[l]
            kvh = hg // 2
            aq = W[:, o["aq"] + hg * 64: o["aq"] + (hg + 1) * 64]
            ak = W[:, o["ak"] + kvh * 64: o["ak"] + (kvh + 1) * 64]
            av = W[:, o["av"] + kvh * 64: o["av"] + (kvh + 1) * 64]
            sl = lambda nm: W[:, o[nm] + hg * 64: o[nm] + (hg + 1) * 64]
            blocks = [None] * 18
            blocks[C_AQ], blocks[C_BQ], blocks[C_CQ], blocks[C_DQ] = aq, sl("bq"), sl("cq"), sl("dq")
            blocks[C_AK], blocks[C_BK], blocks[C_CK], blocks[C_DK] = ak, sl("bk"), sl("ck"), sl("dk")
            blocks[C_AQS], blocks[C_AKS] = aq[:, perm], ak[:, perm]
            blocks[C_AG], blocks[C_BG], blocks[C_CG], blocks[C_DG] = sl("ag"), sl("bg"), sl("cg"), sl("dg")
            blocks[C_AV], blocks[C_BV], blocks[C_CV], blocks[C_DV] = av, sl("bv"), sl("cv"), sl("dv")
            win[l * 4 + hg] = np.concatenate(blocks, axis=1)
    gb = np.concatenate([np.asarray(norm_g, f), np.asarray(final_g, f)[None]], 0)
    gbc = np.ascontiguousarray(np.broadcast_to(gb[:, None, :], (L + 1, 128, D)))
    aqg = np.asarray(a_q_gain, f); akg = np.asarray(a_k_gain, f)
    again = np.stack([aqg, aqg[:, perm], akg, akg[:, perm]], axis=2)
    lam = np.concatenate([np.asarray(c_lambda_q1, f), np.asarray(c_lambda_k1, f),
                          np.asarray(c_lambda_q2, f), np.asarray(c_lambda_k2, f)], axis=1)
    clam = np.ascontiguousarray(np.broadcast_to(lam[:, None, :], (L, 64, 128)))
    csub = np.asarray(c_subln_g, f)[:, :, None].copy()
    t5 = np.asarray(t5_bias, f)
    cfar = np.zeros((4, 128, 2), f)
    for h in range(4):
        cfar[h, :, 0] = t5[15, 4 + h]
        cfar[h, :, 1] = t5[31, 4 + h]
    offB = _toep_off(B_OMAX, B_NT)
    bkB = _t5_bucket_np(offB)
    tabB = np.stack([t5[bkB, h] for h in range(4)], 0).astype(f)
    mulB = _b_mult(offB).astype(f)
    offC = _toep_off(C_OMAX, C_NT)
    bkC = _t5_bucket_np(offC)
    tabC = np.stack([t5[bkC, 4 + h] for h in range(4)], 0).astype(f)
    vD, drD, dcD = _d_tables()
    rpb = np.asarray(d_rpb, f)
    tabD = np.stack([rpb[l, h][drD, dcD] for l in range(L) for h in range(4)], 0).astype(f)
    mskD = vD.astype(f)
    cos, sin = _rope_tables(NMAX)
    return dict(win=win, wout=w_out, gbc=gbc, again=np.ascontiguousarray(again), clam=clam, csub=csub, cfar=cfar,
                tabB=np.ascontiguousarray(tabB), mulB=np.ascontiguousarray(mulB), tabC=np.ascontiguousarray(tabC),
                tabD=np.ascontiguousarray(tabD), mskD=np.ascontiguousarray(mskD), ropec=cos, ropes=sin,
                ident=np.eye(128, dtype=f))


_NC_CACHE = {}


def run(x_prompt, x_sample, **params):
    x_prompt = np.asarray(x_prompt, np.float32)
    x_sample = np.asarray(x_sample, np.float32)
    BP, NP, _ = x_prompt.shape
    BS, NS, _ = x_sample.shape
    L = np.asarray(params["w_in"]).shape[0]
    shared = prepare_shared(NMAX=max(NP, NS), **params)
    key = (NP, NS, L)
    if key not in _NC_CACHE:
        _NC_CACHE[key] = build_program(NP, NS, L)
    nc = _NC_CACHE[key]
    ncores = 8
    in_maps = []
    for c in range(ncores):
        m = dict(shared)
        m["xp"] = np.ascontiguousarray(x_prompt[(c * BP) // ncores])
        m["xs"] = np.ascontiguousarray(x_sample[c % BS])
        in_maps.append(m)
    res = run_bass_kernel_spmd(nc, in_maps, core_ids=list(range(ncores)))
    yp = np.stack([np.asarray(res.results[(b * ncores) // BP]["yp"], np.float32) for b in range(BP)], 0)
    ys = np.stack([np.asarray(res.results[c]["ys"], np.float32) for c in range(BS)], 0)
    return yp, ys


def kernel(x_prompt, x_sample, w_in, w_out, norm_g, final_g, a_q_gain, a_k_gain, t5_bias,
           c_lambda_q1, c_lambda_k1, c_lambda_q2, c_lambda_k2, c_subln_g, d_rpb):
    return run(x_prompt, x_sample, w_in=w_in, w_out=w_out, norm_g=norm_g, final_g=final_g, a_q_gain=a_q_gain,
               a_k_gain=a_k_gain, t5_bias=t5_bias, c_lambda_q1=c_lambda_q1, c_lambda_k1=c_lambda_k1,
               c_lambda_q2=c_lambda_q2, c_lambda_k2=c_lambda_k2, c_subln_g=c_subln_g, d_rpb=d_rpb)
```

```python
import math
import numpy as np
import concourse.bass as bass
import concourse.mybir as mybir
from concourse.bass_utils import run_bass_kernel_spmd

F32 = mybir.dt.float32
BF16 = mybir.dt.bfloat16
ALU = mybir.AluOpType
AF = mybir.ActivationFunctionType

D = 1024
HD = 64
GRID_W = 64
EPS = 1e-6
NCOL = 18 * 64
C_AQ, C_BQ, C_AK, C_BK, C_AQS, C_AKS, C_AG, C_BG, C_AV, C_BV, C_CQ, C_DQ, C_CK, C_DK, C_CG, C_DG, C_CV, C_DV = range(18)

B_OMAX, B_NT = 11, 23
C_OMAX, C_NT = 5, 11
D_OMAX, D_NT = 5, 11
D_SPEC = 12
TB_B = B_NT * 128
TB_C = C_NT * 128
TB_D = D_NT * 128 + D_SPEC * 512


def _t5_bucket_np(rel):
    rel = np.asarray(rel, np.int64)
    half, max_exact = 16, 8
    dist = np.abs(rel)
    lg = np.log(np.maximum(dist, 1).astype(np.float32) / np.float32(max_exact)).astype(np.float32)
    large = max_exact + (lg / np.float32(math.log(128 / max_exact)) * np.float32(half - max_exact)).astype(np.int32)
    large = np.minimum(large, half - 1)
    return np.where(rel > 0, half, 0) + np.where(dist < max_exact, dist, large)


def _b_mult(off):
    off = np.asarray(off, np.int64)
    m = np.zeros(off.shape, np.float32)
    for (w, d) in ((128, 1), (512, 4), (2048, 16)):
        m += ((off % d == 0) & (np.abs(off) <= w // 2)).astype(np.float32)
    return m


def _toep_off(omax, nt):
    p = np.arange(128)[:, None]
    c = np.arange(128)[None, :]
    return np.concatenate([(omax - m) * 128 + p - c for m in range(nt)], axis=1)


def _d_idx(tk, tq, rows):
    rk, ck = tk // GRID_W, tk % GRID_W
    rq, cq = tq // GRID_W, tq % GRID_W
    r0 = np.clip(rq - 4, 0, rows - 8)
    cs = np.clip(cq - 8, 0, GRID_W - 16)
    valid = (rk >= r0) & (rk < r0 + 8) & (ck >= cs) & (ck < cs + 16) & (tk >= 0) & (tk < rows * GRID_W)
    dr = np.clip(rk - rq + 7, 0, 14)
    dc = np.clip(ck - cq + 15, 0, 30)
    return valid, dr, dc


def _d_tables():
    rows = 64
    n = rows * GRID_W
    p = np.arange(128)[:, None]
    c128 = np.arange(128)[None, :]
    c512 = np.arange(512)[None, :]
    vs, drs, dcs = [], [], []
    jq0 = 16
    for m in range(D_NT):
        o = D_OMAX - m
        v, dr, dc = _d_idx((jq0 + o) * 128 + p, jq0 * 128 + c128, rows)
        vs.append(v); drs.append(dr); dcs.append(dc)
    for kb in range(6):
        v, dr, dc = _d_idx(kb * 128 + p, c512, rows)
        vs.append(v); drs.append(dr); dcs.append(dc)
    nb = n // 128
    for i in range(6):
        v, dr, dc = _d_idx((nb - 6 + i) * 128 + p, n - 512 + c512, rows)
        vs.append(v); drs.append(dr); dcs.append(dc)
    return (np.concatenate(vs, 1), np.concatenate(drs, 1), np.concatenate(dcs, 1))


def _rope_tables(n):
    t = np.arange(n)
    row = (t // GRID_W).astype(np.float32)
    col = (t % GRID_W).astype(np.float32)
    freqs = (np.float32(10000.0) ** (-np.arange(16, dtype=np.float32) * np.float32(2.0) / np.float32(32))).astype(np.float32)
    cos = np.zeros((64, n), np.float32)
    sin = np.zeros((64, n), np.float32)
    for d in range(64):
        pos = row if d < 32 else col
        j = d % 16
        ang = (pos * freqs[j]).astype(np.float32)
        cos[d] = np.cos(ang)
        s = np.sin(ang)
        sin[d] = -s if (d % 32) < 16 else s
    return cos, sin


def _swap_perm():
    perm = np.arange(64)
    for d in range(64):
        perm[d] = d + 16 if (d % 32) < 16 else d - 16
    return perm


class Lane:
    def __init__(self, nc, name, step, is_pe=False):
        self.sem = nc.semaphore(name).__enter__()
        self.cnt = 0
        self.step = step
        self.is_pe = is_pe
        self.name = name

    def mark(self, ins):
        self.cnt += self.step
        ins.then_inc(self.sem, self.step)
        return (self, self.cnt)


class Eng:
    def __init__(self, nc, eng, name, is_pe=False):
        self.eng = eng
        self.lane = Lane(nc, "s_" + name, 1, is_pe)
        self.waited = {}
        self.is_pe = is_pe

    def wait(self, tok):
        if tok is None:
            return
        lane, cnt = tok
        if lane is self.lane and self.is_pe:
            return
        if self.waited.get(lane.name, 0) >= cnt:
            return
        self.waited[lane.name] = cnt
        self.eng.wait_ge(lane.sem, cnt)


class T:
    __slots__ = ("w", "r", "ps")

    def __init__(self, ps=False):
        self.w = {}
        self.r = {}
        self.ps = ps


def do(E, fn, reads=(), writes=(), lane=None, embed=True):
    need = {}

    def req(tok):
        ln, cnt = tok
        if ln is E.lane and E.is_pe:
            return
        if E.waited.get(ln.name, 0) >= cnt:
            return
        if ln.name not in need or need[ln.name][1] < cnt:
            need[ln.name] = tok

    for t in reads:
        for wt in t.w.values():
            req(wt)
        if t.ps:
            for rt in t.r.values():
                if rt[0] is not E.lane:
                    req(rt)
    for t in writes:
        for rt in t.r.values():
            req(rt)
        for wt in t.w.values():
            req(wt)
    toks = list(need.values())
    emb = None
    if toks and embed and lane is None and EMBED_WAITS:
        emb = toks.pop()
    for ln, cnt in toks:
        E.waited[ln.name] = cnt
        E.eng.wait_ge(ln.sem, cnt)
    ins = fn()
    if emb is not None:
        E.waited[emb[0].name] = emb[1]
        ins.wait_op(emb[0].sem, emb[1], "sem-ge")
    tok = (lane or E.lane).mark(ins)
    for t in reads:
        t.r[tok[0].name] = tok
    for t in writes:
        t.w[tok[0].name] = tok
        t.r = {}
    return tok


EMBED_WAITS = True


def build_program(NP, NS, L):
    nc = bass.Bass("TRN2", target_bir_lowering=False)
    NMAX = max(NP, NS)
    seqs = [("p", NP), ("s", NS)]

    def dram(name, shape, dt, kind):
        return nc.dram_tensor(name, list(shape), dt, kind=kind)

    x_in = {"p": dram("xp", [NP, D], F32, "ExternalInput"), "s": dram("xs", [NS, D], F32, "ExternalInput")}
    y_out = {"p": dram("yp", [NP, D], F32, "ExternalOutput"), "s": dram("ys", [NS, D], F32, "ExternalOutput")}
    win = dram("win", [L * 4, D, NCOL], F32, "ExternalInput")
    wout = dram("wout", [L, D, D], F32, "ExternalInput")
    gbc = dram("gbc", [L + 1, 128, D], F32, "ExternalInput")
    again = dram("again", [L, 64, 4], F32, "ExternalInput")
    clam = dram("clam", [L, 64, 4 * 32], F32, "ExternalInput")
    csub = dram("csub", [L, 64, 1], F32, "ExternalInput")
    cfar = dram("cfar", [4, 128, 2], F32, "ExternalInput")
    tabB = dram("tabB", [4, 128, TB_B], F32, "ExternalInput")
    mulB = dram("mulB", [128, TB_B], F32, "ExternalInput")
    tabC = dram("tabC", [4, 128, TB_C], F32, "ExternalInput")
    tabD = dram("tabD", [L * 4, 128, TB_D], F32, "ExternalInput")
    mskD = dram("mskD", [128, TB_D], F32, "ExternalInput")
    ropec = dram("ropec", [64, NMAX], F32, "ExternalInput")
    ropes = dram("ropes", [64, NMAX], F32, "ExternalInput")
    ident_d = dram("ident", [128, 128], F32, "ExternalInput")

    xnT = {k: dram("xnT_" + k, [8, 128, n], BF16, "Internal") for k, n in seqs}
    x1 = {k: dram("x1_" + k, [n, D], F32, "Internal") for k, n in seqs}
    mixT = {k: dram("mixT_" + k, [D, n], BF16, "Internal") for k, n in seqs}
    gT = {k: dram("gT_" + k, [4, 64, n], F32, "Internal") for k, n in seqs}

    def sb(name, shape, dt):
        return nc.sbuf_tensor(name, list(shape), dt).__enter__()

    NBMAX = NMAX // 128
    QT = sb("QT", [128, NMAX], BF16)
    KT = sb("KT", [128, NMAX], BF16)
    KT2 = sb("KT2", [64, NMAX], BF16)
    VA = sb("VA", [128, NBMAX, 2, 65], BF16)
    WB = sb("WB", [128, TB_B], BF16)
    WC = sb("WC", [128, TB_C], BF16)
    WD = sb("WD", [128, TB_D], BF16)
    wbuf = sb("wbuf", [128, 8, NCOL], BF16)
    wstage = [sb("wstage%d" % i, [128, 1024], F32) for i in range(2)]
    xt = [sb("xt%d" % i, [128, 8, 512], BF16) for i in range(2)]
    PT = [sb("PT%d" % i, [128, 1024], BF16) for i in range(4)]
    ftmp = [sb("ftmp%d" % i, [128, 512], F32) for i in range(10)]
    gtile = [sb("gtile%d" % i, [64, 512], F32) for i in range(2)]
    ropt = [[sb("rc%d" % i, [64, 512], F32), sb("rs%d" % i, [64, 512], F32)] for i in range(2)]
    mxt = [sb("mxt%d" % i, [64, 512], BF16) for i in range(2)]
    xrow = [sb("xrow%d" % i, [128, D], F32) for i in range(2)]
    xnrow = [sb("xnrow%d" % i, [128, D], F32) for i in range(2)]
    xnTs = [sb("xnTs%d" % i, [128, 8, 128], BF16) for i in range(2)]
    mts = [sb("mts%d" % i, [128, 8, 128], BF16) for i in range(2)]
    gsb = sb("gsb", [128, D], F32)
    ident = sb("identsb", [128, 128], F32)
    ones64 = sb("ones64", [128, 128], F32)
    sel65 = sb("sel65", [128, 128], F32)
    small = sb("small", [128, 32], F32)
    nsm = [sb("nsm%d" % i, [128, 4], F32) for i in range(2)]
    lamt = sb("lamt", [64, 4 * 32], F32)
    cfar_sb = sb("cfar_sb", [128, 2], F32)
    ps = nc.psum_tensor("ps", [128, 8, 512], F32).__enter__()

    PE = Eng(nc, nc.tensor, "pe", True)
    ACT = Eng(nc, nc.scalar, "act")
    DVE = Eng(nc, nc.vector, "dve")
    POOL = Eng(nc, nc.gpsimd, "pool")
    SP = Eng(nc, nc.sync, "sp")
    dl = {}

    def lane_for(key):
        if key not in dl:
            dl[key] = Lane(nc, "d_" + str(key).replace(" ", "").replace("'", "").replace("(", "").replace(")", "").replace(",", "_"), 16)
        return dl[key]

    NT5 = NMAX // 512
    tQT = [T() for _ in range(NT5)]
    tKT = [T() for _ in range(NT5)]
    tKT2 = [T() for _ in range(NT5)]
    tVA = [T() for _ in range(NT5)]
    tWB, tWC, tWD, tw = T(), T(), T(), T()
    twst = [T(), T()]
    txt = [T(), T()]
    tPT = [[T(), T()] for _ in PT]
    tf = [T() for _ in ftmp]
    tg = [T(), T()]
    trp = [T(), T()]
    tmx = [T(), T()]
    txrow = [T(), T()]
    txn = [T(), T()]
    txnT = [T(), T()]
    tmts = [T(), T()]
    tnsm = [T(), T()]
    tgsb, tident, tones, tsel, tsmall, tlam, tcfar = T(), T(), T(), T(), T(), T(), T()
    tps = [T(ps=True) for _ in range(8)]
    tD = {nm: {k: T() for k, _ in seqs} for nm in ("xnT", "x1", "mixT", "gT")}
    tDx1 = {(k, l_): T() for k, _ in seqs for l_ in range(L)}
    fidx = {id(t_): i_ for i_, t_ in enumerate(tf)}

    def dma(E, lane, out, in_, reads=(), writes=()):
        return do(E, lambda: E.eng.dma_start(out=out, in_=in_), reads, writes, lane=lane_for(lane))

    dma(SP, "ident", ident[:, :], ident_d[:, :], writes=[tident])
    do(DVE, lambda: nc.vector.memset(ones64[:, :], 0.0), writes=[tones])
    do(DVE, lambda: nc.vector.memset(ones64[0:64, :], 1.0 / 64.0), writes=[tones])
    do(DVE, lambda: nc.vector.memset(sel65[:, :], 0.0), writes=[tsel])
    do(DVE, lambda: nc.vector.memset(sel65[64:65, :], 1.0), writes=[tsel])
    for i_ in range(len(ftmp)):
        do(DVE, lambda i_=i_: nc.vector.memset(ftmp[i_][:, :], 0.0), writes=[tf[i_]])
    do(DVE, lambda: nc.vector.memset(VA[:, :, :, :], 1.0), writes=tVA)
    do(DVE, lambda: nc.vector.memset(KT2[:, :], 0.0), writes=tKT2)

    ftmp_rr = [0]

    def ft():
        i = ftmp_rr[0] % len(ftmp)
        ftmp_rr[0] += 1
        return ftmp[i], tf[i]

    rr = {}

    def nxt(key, n):
        i = rr.get(key, 0) % n
        rr[key] = rr.get(key, 0) + 1
        return i

    def norm_block(xr, txr, seqk, b, final):
        i = nxt("xn", 2)
        xn, tx = xnrow[i], txn[i]
        si = nxt("nsm", 2)
        sm, tsm = nsm[si], tnsm[si]
        do(ACT, lambda: nc.scalar.activation(out=xn[:, :], in_=xr[:, :], func=AF.Square, accum_out=sm[:, 0:1]),
           reads=[txr], writes=[tx, tsm], embed=False)
        do(ACT, lambda: nc.scalar.activation(out=sm[:, 1:2], in_=sm[:, 0:1], func=AF.Ln, bias=EPS, scale=1.0 / D),
           writes=[tsm])
        do(ACT, lambda: nc.scalar.activation(out=sm[:, 2:3], in_=sm[:, 1:2], func=AF.Exp, scale=-0.5),
           writes=[tsm])
        do(DVE, lambda: nc.vector.scalar_tensor_tensor(out=xn[:, :], in0=xr[:, :], scalar=sm[:, 2:3], in1=gsb[:, :],
                                                        op0=ALU.mult, op1=ALU.mult),
           reads=[txr, tsm, tgsb], writes=[tx])
        if final:
            dma(POOL, ("st_xn", i), y_out[seqk][b * 128:(b + 1) * 128, :], xn[:, :], reads=[tx])
            return
        j = nxt("xnT", 2)
        for half in range(2):
            bank = 4 + half
            for c4 in range(4):
                c = half * 4 + c4
                do(PE, lambda c=c, c4=c4, bank=bank: nc.tensor.transpose(out=ps[:, bank, c4 * 128:(c4 + 1) * 128],
                                                                         in_=xn[:, c * 128:(c + 1) * 128], identity=ident[:, :]),
                   reads=[tx, tident], writes=[tps[bank]])
            src = ps[:, bank, :].rearrange("p (c t) -> p c t", c=4)
            dst = xnTs[j][:, half * 4:(half + 1) * 4, :]
            if half == 0:
                do(ACT, lambda src=src, dst=dst: nc.scalar.copy(out=dst, in_=src), reads=[tps[bank]], writes=[txnT[j]])
            else:
                do(DVE, lambda src=src, dst=dst: nc.vector.tensor_copy(out=dst, in_=src), reads=[tps[bank]], writes=[txnT[j]])
        dma(POOL, ("st_xnT", j), xnT[seqk][:, :, b * 128:(b + 1) * 128].rearrange("c p t -> p c t"), xnTs[j][:, :, :],
            reads=[txnT[j]], writes=[tD["xnT"][seqk]])

    def load_g(l):
        dma(SP, "gsb", gsb[:, :], gbc[l, :, :], writes=[tgsb])

    def load_cast(dst_ap_fn, src_ap_fn, ncols_total, tdst):
        per = 128
        flip = 0
        for c0 in range(0, ncols_total, per):
            w = min(per, ncols_total - c0)
            i = nxt("wst", 2)
            st = wstage[i]
            stv = st[:, 0:8 * w].rearrange("p (c n) -> p c n", c=8)
            dma(SP, ("wst", i), stv, src_ap_fn(c0, w), writes=[twst[i]])
            if flip % 2 == 0:
                do(ACT, lambda stv=stv, c0=c0, w=w: nc.scalar.copy(out=dst_ap_fn(c0, w), in_=stv), reads=[twst[i]], writes=[tdst])
            else:
                do(DVE, lambda stv=stv, c0=c0, w=w: nc.vector.tensor_copy(out=dst_ap_fn(c0, w), in_=stv), reads=[twst[i]], writes=[tdst])
            flip += 1

    def build_tables(l, hg):
        def one(Wt, tW, src, ncols, mul_src):
            for c0 in range(0, ncols, 1024):
                w = min(1024, ncols - c0)
                i = nxt("wst", 2)
                st = wstage[i]
                dma(SP, ("wst", i), st[:, 0:w], src[:, c0:c0 + w], writes=[twst[i]])
                if mul_src is None:
                    do(ACT, lambda st=st, c0=c0, w=w: nc.scalar.activation(out=Wt[:, c0:c0 + w], in_=st[:, 0:w], func=AF.Exp),
                       reads=[twst[i]], writes=[tW])
                else:
                    do(ACT, lambda st=st, w=w: nc.scalar.activation(out=st[:, 0:w], in_=st[:, 0:w], func=AF.Exp),
                       reads=[twst[i]], writes=[twst[i]])
                    i2 = nxt("wst", 2)
                    st2 = wstage[i2]
                    dma(SP, ("wst", i2), st2[:, 0:w], mul_src[:, c0:c0 + w], writes=[twst[i2]])
                    do(DVE, lambda st=st, st2=st2, c0=c0, w=w: nc.vector.tensor_tensor(out=Wt[:, c0:c0 + w], in0=st[:, 0:w],
                                                                                        in1=st2[:, 0:w], op=ALU.mult),
                       reads=[twst[i], twst[i2]], writes=[tW])
        one(WB, tWB, tabB[hg], TB_B, mulB)
        one(WC, tWC, tabC[hg], TB_C, None)
        one(WD, tWD, tabD[l * 4 + hg], TB_D, mskD)
        dma(SP, "cfar", cfar_sb[:, :], cfar[hg, :, :], writes=[tcfar])

    def layer_params(l):
        dma(SP, "small", small[0:64, 8:12], again[l, :, :], writes=[tsmall])
        dma(SP, "lamt", lamt[:, :], clam[l, :, :], writes=[tlam])
        dma(SP, "small", small[0:64, 13:14], csub[l, :, :], writes=[tsmall])
        li = 0.8 - 0.6 * math.exp(-0.3 * l)
        pr, tpr = ft()
        do(DVE, lambda: nc.vector.tensor_tensor(out=pr[0:64, 0:32], in0=lamt[:, 0:32], in1=lamt[:, 32:64], op=ALU.mult),
           reads=[tlam], writes=[tpr])
        do(DVE, lambda: nc.vector.tensor_tensor(out=pr[0:64, 32:64], in0=lamt[:, 64:96], in1=lamt[:, 96:128], op=ALU.mult),
           reads=[tlam], writes=[tpr])
        do(DVE, lambda: nc.vector.reduce_sum(out=small[0:64, 14:15], in_=pr[0:64, 0:32], axis=mybir.AxisListType.X),
           reads=[tpr], writes=[tsmall])
        do(DVE, lambda: nc.vector.reduce_sum(out=small[0:64, 15:16], in_=pr[0:64, 32:64], axis=mybir.AxisListType.X),
           reads=[tpr], writes=[tsmall])
        do(ACT, lambda: nc.scalar.activation(out=small[0:64, 16:18], in_=small[0:64, 14:16], func=AF.Exp),
           writes=[tsmall])
        do(DVE, lambda: nc.vector.scalar_tensor_tensor(out=small[0:64, 12:13], in0=small[0:64, 17:18], scalar=-li,
                                                        in1=small[0:64, 16:17], op0=ALU.add, op1=ALU.subtract),
           writes=[tsmall])
        do(DVE, lambda: nc.vector.tensor_scalar(out=small[0:64, 13:14], in0=small[0:64, 13:14], scalar1=1.0 - li, scalar2=None,
                                                op0=ALU.mult),
           writes=[tsmall])

    def load_win(l, hg):
        load_cast(lambda c0, w: wbuf[:, :, c0:c0 + w],
                  lambda c0, w: win[l * 4 + hg, :, c0:c0 + w].rearrange("(c p) n -> p c n", p=128),
                  NCOL, tw)

    def inproj(seqk, n, l, hg, pr_):
        BST = globals().get("BSTEP", 99)
        for tt in range(n // 512):
            t0 = tt * 512
            xi = nxt("xt", 2)
            x_t, tx = xt[xi], txt[xi]
            dma(SP, ("xt", xi), x_t[:, :, :], xnT[seqk][:, :, t0:t0 + 512].rearrange("c p t -> p c t"),
                reads=[tD["xnT"][seqk]], writes=[tx])
            if pr_ == 0:
                ri = nxt("rp", 2)
                dma(SP, ("rp", ri), ropt[ri][0][:, :], ropec[:, t0:t0 + 512], writes=[trp[ri]])
                dma(SP, ("rp", ri), ropt[ri][1][:, :], ropes[:, t0:t0 + 512], writes=[trp[ri]])

            def proj(bank, col0, m):
                for c in range(8):
                    do(PE, lambda c=c: nc.tensor.matmul(out=ps[0:m, bank, :], lhsT=wbuf[:, c, col0:col0 + m], rhs=x_t[:, c, :],
                                                        start=(c == 0), stop=(c == 7)),
                       reads=[tx, tw], writes=[tps[bank]])

            def a_part(bank_main, bank_sw, gcol, dst, tdst):
                BSUB = globals().get("BSUB", 99)
                sq, tsq = ft()
                do(ACT, lambda: nc.scalar.activation(out=sq[0:64, :], in_=ps[0:64, bank_main, :], func=AF.Square),
                   reads=[tps[bank_main]], writes=[tsq])
                if globals().get("VARIANT", 0) in (7, 8):
                    t9, tt9 = ft()
                    do(DVE, lambda: nc.vector.tensor_copy(out=t9[:, :], in_=ps[:, bank_main, :]),
                       reads=[tps[bank_main]] + ([tsq] if globals().get("VARIANT", 0) == 8 else []), writes=[tt9])
                if BSUB <= 1:
                    return
                do(PE, lambda: nc.tensor.matmul(out=ps[:, 7, :], lhsT=ones64[:, :], rhs=sq[:, :], start=True, stop=True),
                   reads=[tsq, tones], writes=[tps[7]])
                if BSUB <= 2:
                    return
                ln_, tln = ft()
                do(ACT, lambda: nc.scalar.activation(out=ln_[0:64, :], in_=ps[0:64, 7, :], func=AF.Ln, bias=EPS, scale=1.0),
                   reads=[tps[7]], writes=[tln])
                if BSUB <= 3:
                    return
                do(ACT, lambda: nc.scalar.activation(out=ln_[0:64, :], in_=ln_[0:64, :], func=AF.Exp, scale=-0.5),
                   writes=[tln])
                if BSUB <= 4:
                    return
                t1, tt1 = ft()
                VAR = globals().get("VARIANT", 0)
                if VAR == 0:
                    do(DVE, lambda: nc.vector.scalar_tensor_tensor(out=t1[0:64, :], in0=ps[0:64, bank_main, :],
                                                                    scalar=small[0:64, gcol:gcol + 1], in1=ropt[ri][0][:, :],
                                                                    op0=ALU.mult, op1=ALU.mult),
                       reads=[tps[bank_main], tsmall, trp[ri]], writes=[tt1])
                elif VAR == 1:
                    do(DVE, lambda: nc.vector.tensor_scalar(out=t1[0:64, :], in0=ps[0:64, bank_main, :],
                                                            scalar1=small[0:64, gcol:gcol + 1], scalar2=None, op0=ALU.mult),
                       reads=[tps[bank_main], tsmall], writes=[tt1])
                elif VAR == 2:
                    do(DVE, lambda: nc.vector.tensor_tensor(out=t1[0:64, :], in0=ps[0:64, bank_main, :], in1=ropt[ri][0][:, :], op=ALU.mult),
                       reads=[tps[bank_main], trp[ri]], writes=[tt1])
                elif VAR == 4:
                    do(DVE, lambda: nc.vector.tensor_copy(out=t1[0:64, :], in_=ps[0:64, bank_main, :]),
                       reads=[tps[bank_main]], writes=[tt1])
                elif VAR == 5:
                    do(DVE, lambda: nc.vector.tensor_copy(out=t1[:, :], in_=ps[:, bank_main, :]),
                       reads=[tps[bank_main]], writes=[tt1])
                elif VAR == 6:
                    do(DVE, lambda: nc.vector.tensor_copy(out=t1[0:64, :], in_=sq[0:64, :]),
                       reads=[tsq], writes=[tt1])
                elif VAR == 3:
                    do(DVE, lambda: nc.vector.scalar_tensor_tensor(out=t1[0:64, :], in0=ps[0:64, bank_main, :],
                                                                    scalar=2.0, in1=ropt[ri][0][:, :],
                                                                    op0=ALU.mult, op1=ALU.mult),
                       reads=[tps[bank_main], trp[ri]], writes=[tt1])
                if BSUB <= 5:
                    return
                t2, tt2 = ft()
                do(DVE, lambda: nc.vector.scalar_tensor_tensor(out=t2[0:64, :], in0=ps[0:64, bank_sw, :],
                                                                scalar=small[0:64, gcol + 1:gcol + 2], in1=ropt[ri][1][:, :],
                                                                op0=ALU.mult, op1=ALU.mult),
                   reads=[tps[bank_sw], tsmall, trp[ri]], writes=[tt2])
                do(DVE, lambda: nc.vector.tensor_tensor(out=t1[0:64, :], in0=t1[0:64, :], in1=t2[0:64, :], op=ALU.add),
                   reads=[tt2], writes=[tt1])
                if BSUB <= 6:
                    return
                do(DVE, lambda: nc.vector.tensor_tensor(out=dst[0:64, t0:t0 + 512], in0=t1[0:64, :], in1=ln_[0:64, :], op=ALU.mult),
                   reads=[tt1, tln], writes=[tdst])

            if BST <= 1:
                continue
            if pr_ == 0:
                proj(0, C_AQ * 64, 128)
                if BST <= 2:
                    continue
                proj(1, C_AQS * 64, 64)
                if BST <= 3:
                    continue
                a_part(0, 1, 8, QT, tQT[tt])
                if BST <= 4:
                    continue
                do(ACT, lambda: nc.scalar.copy(out=QT[64:128, t0:t0 + 512], in_=ps[64:128, 0, :]),
                   reads=[tps[0]], writes=[tQT[tt]])
                if BST <= 5:
                    continue
                proj(2, C_AK * 64, 128)
                proj(3, C_AKS * 64, 64)
                a_part(2, 3, 10, KT, tKT[tt])
                do(ACT, lambda: nc.scalar.copy(out=KT[64:128, t0:t0 + 512], in_=ps[64:128, 2, :]),
                   reads=[tps[2]], writes=[tKT[tt]])
                gcol0, vcol0 = C_AG, C_AV
            else:
                proj(0, C_CQ * 64, 128)
                do(DVE, lambda: nc.vector.tensor_copy(out=QT[:, t0:t0 + 512], in_=ps[:, 0, :]),
                   reads=[tps[0]], writes=[tQT[tt]])
                proj(2, C_CK * 64, 128)
                do(DVE, lambda: nc.vector.tensor_copy(out=KT[0:32, t0:t0 + 512], in_=ps[0:32, 2, :]),
                   reads=[tps[2]], writes=[tKT[tt]])
                do(DVE, lambda: nc.vector.memset(KT[32:64, t0:t0 + 512], 0.0), writes=[tKT[tt]])
                do(DVE, lambda: nc.vector.tensor_copy(out=KT2[32:64, t0:t0 + 512], in_=ps[32:64, 2, :]),
                   reads=[tps[2]], writes=[tKT2[tt]])
                do(DVE, lambda: nc.vector.tensor_copy(out=KT[64:128, t0:t0 + 512], in_=ps[64:128, 2, :]),
                   reads=[tps[2]], writes=[tKT[tt]])
                gcol0, vcol0 = C_CG, C_CV
            if BST <= 6:
                continue
            proj(4, gcol0 * 64, 128)
            gt_, tgt = ft()
            do(ACT, lambda gt_=gt_: nc.scalar.activation(out=gt_[:, :], in_=ps[:, 4, :], func=AF.Silu),
               reads=[tps[4]], writes=[tgt])
            dma(POOL, ("st_f", fidx[id(tgt)]), gT[seqk][pr_ * 2:pr_ * 2 + 2, :, t0:t0 + 512].rearrange("m p t -> (m p) t"), gt_[:, :],
                reads=[tgt], writes=[tD["gT"][seqk]])
            if BST <= 7:
                continue
            for s4 in range(4):
                bank = 5 + (s4 % 2)
                for c in range(8):
                    do(PE, lambda c=c, s4=s4, bank=bank: nc.tensor.matmul(out=ps[:, bank, 0:128], lhsT=x_t[:, c, s4 * 128:(s4 + 1) * 128],
                                                                          rhs=wbuf[:, c, vcol0 * 64:vcol0 * 64 + 128],
                                                                          start=(c == 0), stop=(c == 7)),
                       reads=[tx, tw], writes=[tps[bank]])
                blk = tt * 4 + s4
                srcv = ps[:, bank, 0:128].rearrange("p (m d) -> p m d", m=2)
                if s4 % 2 == 0:
                    do(DVE, lambda blk=blk, srcv=srcv: nc.vector.tensor_copy(out=VA[:, blk, :, 0:64], in_=srcv),
                       reads=[tps[bank]], writes=[tVA[tt]])
                else:
                    do(ACT, lambda blk=blk, srcv=srcv: nc.scalar.copy(out=VA[:, blk, :, 0:64], in_=srcv),
                       reads=[tps[bank]], writes=[tVA[tt]])

    def attention(seqk, n, l, hg, pr_):
        nb = n // 128
        nq = n // 512
        sc64 = HD ** -0.5
        sc32 = 32 ** -0.5
        runs = []
        for m in (2 * pr_, 2 * pr_ + 1):
            for j in range(nq):
                units = []
                if m == 0:
                    for kb in range(0, nb, 2):
                        units.append((kb, "plain", None, None))
                    runs.append((m, 0, j, units))
                elif m == 1:
                    for kb in range(max(0, 4 * j - 8), min(nb, 4 * j + 12), 2):
                        v = [(B_OMAX - (k - 4 * j)) * 128 for k in (kb, kb + 1)]
                        units.append((kb, "w", WB[:, v[0]:v[0] + 512], WB[:, v[1]:v[1] + 512]))
                    runs.append((m, 0, j, units))
                elif m == 2:
                    for kb in range(0, nb, 2):
                        o = kb - 4 * j
                        if -2 <= o <= 4:
                            v = [(C_OMAX - (k - 4 * j)) * 128 for k in (kb, kb + 1)]
                            units.append((kb, "w", WC[:, v[0]:v[0] + 512], WC[:, v[1]:v[1] + 512]))
                        else:
                            units.append((kb, "farL" if o < 0 else "farR", None, None))
                    runs.append((m, 0, j, units))
                    runs.append((m, 1, j, units))
                else:
                    if j == 0:
                        for i in range(0, 6, 2):
                            b0 = D_NT * 128 + i * 512
                            units.append((i, "w", WD[:, b0:b0 + 512], WD[:, b0 + 512:b0 + 1024]))
                    elif j == nq - 1:
                        for i in range(0, 6, 2):
                            b0 = D_NT * 128 + (6 + i) * 512
                            units.append((nb - 6 + i, "w", WD[:, b0:b0 + 512], WD[:, b0 + 512:b0 + 1024]))
                    else:
                        for kb in range(4 * j - 2, 4 * j + 6, 2):
                            v = [(D_OMAX - (k - 4 * j)) * 128 for k in (kb, kb + 1)]
                            units.append((kb, "w", WD[:, v[0]:v[0] + 512], WD[:, v[1]:v[1] + 512]))
                    runs.append((m, 0, j, units))

        flat = []
        for ri_, (m, mp, j, units) in enumerate(runs):
            for ui, u in enumerate(units):
                flat.append((ri_, ui, len(units), m, mp, j, u))

        run_obank = {}
        unit_pi = {}
        pending = []
        c_hold = {}

        def qk_operands(m, mp, j, kb):
            p0 = 64 * (m % 2)
            if m == 2 and mp == 1:
                return (KT2[0:64, kb * 128:(kb + 1) * 128], QT[0:64, j * 512:(j + 1) * 512], tKT2[kb // 4], tQT[j])
            return (KT[p0:p0 + 64, kb * 128:(kb + 1) * 128], QT[p0:p0 + 64, j * 512:(j + 1) * 512], tKT[kb // 4], tQT[j])

        def emit_S(idx):
            ri_, ui, nu, m, mp, j, (kb, mode, w0, w1) = flat[idx]
            sp_ = idx % 2
            for h in range(2):
                lhsT, rhs, tk, tq = qk_operands(m, mp, j, kb + h)
                bank = sp_ * 2 + h
                do(PE, lambda lhsT=lhsT, rhs=rhs, bank=bank: nc.tensor.matmul(out=ps[:, bank, :], lhsT=lhsT, rhs=rhs, start=True, stop=True),
                   reads=[tk, tq], writes=[tps[bank]])

        def finalize_A(ob):
            osb, tosb = ft()
            ohi, tohi = ft()
            i_lo, i_hi = fidx[id(tosb)], fidx[id(tohi)]
            if i_hi == i_lo + 1:
                pass
            do(DVE, lambda: nc.vector.tensor_copy(out=osb[0:65, :], in_=ps[0:65, ob, :]), reads=[tps[ob]], writes=[tosb])
            do(DVE, lambda: nc.vector.tensor_tensor(out=osb[0:65, :], in0=ps[0:65, ob + 1, :], in1=osb[0:65, :], op=ALU.add),
               reads=[tps[ob + 1]], writes=[tosb])
            return osb, tosb, ob

        def finalize_B(m, j, holders):
            outs = []
            mb = holders[-1][2]
            for (osb, tosb, ob_) in holders:
                do(PE, lambda osb=osb: nc.tensor.matmul(out=ps[:, mb, :], lhsT=sel65[:, :], rhs=osb[:, :], start=True, stop=True),
                   reads=[tosb, tsel], writes=[tps[mb]])
                rd, trd = ft()
                do(ACT, lambda rd=rd: nc.scalar.activation(out=rd[0:64, :], in_=ps[0:64, mb, :], func=AF.Ln), reads=[tps[mb]], writes=[trd])
                do(ACT, lambda rd=rd: nc.scalar.activation(out=rd[0:64, :], in_=rd[0:64, :], func=AF.Exp, scale=-1.0), writes=[trd])
                do(DVE, lambda rd=rd, osb=osb: nc.vector.tensor_tensor(out=rd[0:64, :], in0=osb[0:64, :], in1=rd[0:64, :], op=ALU.mult),
                   reads=[tosb], writes=[trd])
                outs.append((rd, trd))
            o, to = outs[0]
            gi = nxt("g", 2)
            dma(SP, ("g", gi), gtile[gi][:, :], gT[seqk][m, :, j * 512:(j + 1) * 512], reads=[tD["gT"][seqk]], writes=[tg[gi]])
            if m == 2:
                o2, to2 = outs[1]
                do(DVE, lambda: nc.vector.scalar_tensor_tensor(out=o[0:64, :], in0=o2[0:64, :], scalar=small[0:64, 12:13], in1=o[0:64, :],
                                                                op0=ALU.mult, op1=ALU.add),
                   reads=[to2, tsmall], writes=[to])
                sq, tsq = ft()
                do(ACT, lambda: nc.scalar.activation(out=sq[0:64, :], in_=o[0:64, :], func=AF.Square), reads=[to], writes=[tsq])
                obc = mb
                do(PE, lambda: nc.tensor.matmul(out=ps[:, obc, :], lhsT=ones64[:, :], rhs=sq[:, :], start=True, stop=True),
                   reads=[tsq, tones], writes=[tps[obc]])
                do(ACT, lambda: nc.scalar.activation(out=sq[0:64, :], in_=ps[0:64, obc, :], func=AF.Ln, bias=EPS, scale=1.0),
                   reads=[tps[obc]], writes=[tsq])
                do(ACT, lambda: nc.scalar.activation(out=sq[0:64, :], in_=sq[0:64, :], func=AF.Exp, scale=-0.5), writes=[tsq])
                do(DVE, lambda: nc.vector.scalar_tensor_tensor(out=o[0:64, :], in0=o[0:64, :], scalar=small[0:64, 13:14], in1=sq[0:64, :],
                                                                op0=ALU.mult, op1=ALU.mult),
                   reads=[tsq, tsmall], writes=[to])
            mi = nxt("mx", 2)
            do(DVE, lambda: nc.vector.tensor_tensor(out=mxt[mi][:, :], in0=o[0:64, :], in1=gtile[gi][:, :], op=ALU.mult),
               reads=[to, tg[gi]], writes=[tmx[mi]])
            r0 = m * 256 + hg * 64
            dma(POOL, ("st_mx", mi), mixT[seqk][r0:r0 + 64, j * 512:(j + 1) * 512], mxt[mi][:, :], reads=[tmx[mi]], writes=[tD["mixT"][seqk]])

        def emit_rest(idx):
            ri_, ui, nu, m, mp, j, (kb, mode, w0, w1) = flat[idx]
            sp_ = idx % 2
            pi = nxt("pt", len(PT))
            scale = sc32 if m == 2 else sc64
            src = ps[:, sp_ * 2:sp_ * 2 + 2, :]
            dst = PT[pi][:, :].rearrange("p (b q) -> p b q", b=2)
            if mode in ("farL", "farR"):
                bcol = cfar_sb[:, 0:1] if mode == "farL" else cfar_sb[:, 1:2]
                do(ACT, lambda: nc.scalar.activation(out=dst, in_=src, func=AF.Exp, bias=bcol, scale=scale),
                   reads=[tps[sp_ * 2], tps[sp_ * 2 + 1], tcfar], writes=tPT[pi])
            else:
                do(ACT, lambda: nc.scalar.activation(out=dst, in_=src, func=AF.Exp, scale=scale),
                   reads=[tps[sp_ * 2], tps[sp_ * 2 + 1]], writes=tPT[pi])
            if mode == "w":
                tW = {1: tWB, 2: tWC, 3: tWD}[m]
                for h, wv in ((0, w0), (1, w1)):
                    do(DVE, lambda h=h, wv=wv: nc.vector.tensor_tensor(out=PT[pi][:, h * 512:(h + 1) * 512], in0=PT[pi][:, h * 512:(h + 1) * 512],
                                                                        in1=wv, op=ALU.mult),
                       reads=[tW], writes=[tPT[pi][h]])
            unit_pi[idx] = pi

        def emit_pv(idx):
            ri_, ui, nu, m, mp, j, (kb, mode, w0, w1) = flat[idx]
            pi = unit_pi.pop(idx)
            if ui == 0:
                run_obank[ri_] = 4 + 2 * nxt("ob", 2)
            ob = run_obank[ri_]
            ml = m % 2
            for h in range(2):
                k = kb + h
                for half in range(2):
                    r0_ = 64 * half
                    do(PE, lambda h=h, k=k, half=half, r0_=r0_: nc.tensor.matmul(
                        out=ps[0:65, ob + half, :], lhsT=VA[r0_:r0_ + 64, k, ml, :], rhs=PT[pi][r0_:r0_ + 64, h * 512:(h + 1) * 512],
                        start=(ui == 0 and h == 0), stop=(ui == nu - 1 and h == 1)),
                       reads=[tPT[pi][h], tVA[k // 4]], writes=[tps[ob + half]])
            if ui == nu - 1:
                hold = finalize_A(ob)
                if m == 2:
                    c_hold.setdefault(j, []).append(hold)
                    if mp == 1:
                        hs = c_hold.pop(j)
                        pending.append((idx + 3, lambda hs=hs, m=m, j=j: finalize_B(m, j, hs)))
                else:
                    pending.append((idx + 3, lambda hold=hold, m=m, j=j: finalize_B(m, j, [hold])))

        NF = len(flat)
        emit_S(0)
        if NF > 1:
            emit_S(1)
        for idx in range(NF):
            emit_rest(idx)
            if idx + 2 < NF:
                emit_S(idx + 2)
            emit_pv(idx)
            while pending and pending[0][0] <= idx:
                pending.pop(0)[1]()
        while pending:
            pending.pop(0)[1]()

    def outphase(seqk, n, l):
        last = (l == L - 1)
        load_cast(lambda c0, w: wbuf[:, :, c0:c0 + w],
                  lambda c0, w: wout[l, :, c0:c0 + w].rearrange("(c p) n -> p c n", p=128),
                  D, tw)
        load_g(l + 1)
        xsrc = x_in[seqk] if l == 0 else x1[seqk]
        for b in range(n // 128):
            mi = nxt("mts", 2)
            dma(SP, ("mts", mi), mts[mi][:, :, :], mixT[seqk][:, b * 128:(b + 1) * 128].rearrange("(c p) t -> p c t", p=128),
                reads=[tD["mixT"][seqk]], writes=[tmts[mi]])
            xi = nxt("xrow", 2)
            dma(SP, ("xrow", xi), xrow[xi][:, :], xsrc[b * 128:(b + 1) * 128, :], reads=([tDx1[(seqk, l - 1)]] if l > 0 else []), writes=[txrow[xi]])
            for half in range(2):
                bank = half + 2 * (b % 2)
                for c in range(8):
                    do(PE, lambda c=c, half=half, bank=bank: nc.tensor.matmul(out=ps[:, bank, :], lhsT=mts[mi][:, c, :],
                                                                              rhs=wbuf[:, c, half * 512:(half + 1) * 512],
                                                                              start=(c == 0), stop=(c == 7)),
                       reads=[tmts[mi], tw], writes=[tps[bank]])
                do(DVE, lambda half=half, bank=bank: nc.vector.tensor_tensor(out=xrow[xi][:, half * 512:(half + 1) * 512],
                                                                             in0=ps[:, bank, :], in1=xrow[xi][:, half * 512:(half + 1) * 512],
                                                                             op=ALU.add),
                   reads=[tps[bank]], writes=[txrow[xi]])
            if not last:
                dma(POOL, ("st_xrow", xi), x1[seqk][b * 128:(b + 1) * 128, :], xrow[xi][:, :], reads=[txrow[xi]], writes=[tDx1[(seqk, l)]])
            norm_block(xrow[xi], txrow[xi], seqk, b, final=last)

    load_g(0)
    for seqk, n in seqs:
        for b in range(n // 128):
            xi = nxt("xrow", 2)
            dma(SP, ("xrow", xi), xrow[xi][:, :], x_in[seqk][b * 128:(b + 1) * 128, :], writes=[txrow[xi]])
            norm_block(xrow[xi], txrow[xi], seqk, b, final=False)
    BIS = globals().get("BISECT", "full")
    for l in range(L):
        if BIS == "norm":
            break
        layer_params(l)
        for hg in range(4):
            build_tables(l, hg)
            if BIS == "tables":
                continue
            load_win(l, hg)
            if BIS == "loadwin":
                continue
            for seqk, n in seqs:
                for pr_ in range(2):
                    inproj(seqk, n, l, hg, pr_)
                    if BIS == "inproj":
                        continue
                    attention(seqk, n, l, hg, pr_)
        if BIS in ("tables", "inproj", "attn", "loadwin"):
            continue
        for seqk, n in seqs:
            outphase(seqk, n, l)
    engs = [PE, ACT, DVE, POOL, SP]
    for E in engs:
        for O in engs:
            if O is not E and O.lane.cnt > 0:
                E.eng.wait_ge(O.lane.sem, O.lane.cnt)
        for k, ln in dl.items():
            if ln.cnt > 0:
                E.eng.wait_ge(ln.sem, ln.cnt)
    return nc


def prepare_shared(w_in, w_out, norm_g, final_g, a_q_gain, a_k_gain, t5_bias, c_lambda_q1, c_lambda_k1,
                   c_lambda_q2, c_lambda_k2, c_subln_g, d_rpb, NMAX):
    L = w_in.shape[0]
    f = np.float32
    w_in = np.asarray(w_in, f); w_out = np.asarray(w_out, f)
    perm = _swap_perm()
    offs = np.cumsum([0, 256, 128, 128, 256] + [256] * 12)
    names = ["aq", "ak", "av", "ag", "bq", "bk", "bv", "bg", "cq", "ck", "cv", "cg", "dq", "dk", "dv", "dg"]
    o = {nm: int(offs[i]) for i, nm in enumerate(names)}
    win = np.zeros((L * 4, D, NCOL), f)
    for l in range(L):
        for hg in range(4):
            W = w_in[l]
            kvh = hg // 2
            aq = W[:, o["aq"] + hg * 64: o["aq"] + (hg + 1) * 64]
            ak = W[:, o["ak"] + kvh * 64: o["ak"] + (kvh + 1) * 64]
            av = W[:, o["av"] + kvh * 64: o["av"] + (kvh + 1) * 64]
            sl = lambda nm: W[:, o[nm] + hg * 64: o[nm] + (hg + 1) * 64]
            blocks = [None] * 18
            blocks[C_AQ], blocks[C_BQ], blocks[C_CQ], blocks[C_DQ] = aq, sl("bq"), sl("cq"), sl("dq")
            blocks[C_AK], blocks[C_BK], blocks[C_CK], blocks[C_DK] = ak, sl("bk"), sl("ck"), sl("dk")
            blocks[C_AQS], blocks[C_AKS] = aq[:, perm], ak[:, perm]
            blocks[C_AG], blocks[C_BG], blocks[C_CG], blocks[C_DG] = sl("ag"), sl("bg"), sl("cg"), sl("dg")
            blocks[C_AV], blocks[C_BV], blocks[C_CV], blocks[C_DV] = av, sl("bv"), sl("cv"), sl("dv")
            win[l * 4 + hg] = np.concatenate(blocks, axis=1)
    gb = np.concatenate([np.asarray(norm_g, f), np.asarray(final_g, f)[None]], 0)
    gbc = np.ascontiguousarray(np.broadcast_to(gb[:, None, :], (L + 1, 128, D)))
    aqg = np.asarray(a_q_gain, f); akg = np.asarray(a_k_gain, f)
    again = np.stack([aqg, aqg[:, perm], akg, akg[:, perm]], axis=2)
    lam = np.concatenate([np.asarray(c_lambda_q1, f), np.asarray(c_lambda_k1, f),
                          np.asarray(c_lambda_q2, f), np.asarray(c_lambda_k2, f)], axis=1)
    clam = np.ascontiguousarray(np.broadcast_to(lam[:, None, :], (L, 64, 128)))
    csub = np.asarray(c_subln_g, f)[:, :, None].copy()
    t5 = np.asarray(t5_bias, f)
    cfar = np.zeros((4, 128, 2), f)
    for h in range(4):
        cfar[h, :, 0] = t5[15, 4 + h]
        cfar[h, :, 1] = t5[31, 4 + h]
    offB = _toep_off(B_OMAX, B_NT)
    bkB = _t5_bucket_np(offB)
    tabB = np.stack([t5[bkB, h] for h in range(4)], 0).astype(f)
    mulB = _b_mult(offB).astype(f)
    offC = _toep_off(C_OMAX, C_NT)
    bkC = _t5_bucket_np(offC)
    tabC = np.stack([t5[bkC, 4 + h] for h in range(4)], 0).astype(f)
    vD, drD, dcD = _d_tables()
    rpb = np.asarray(d_rpb, f)
    tabD = np.stack([rpb[l, h][drD, dcD] for l in range(L) for h in range(4)], 0).astype(f)
    mskD = vD.astype(f)
    cos, sin = _rope_tables(NMAX)
    return dict(win=win, wout=w_out, gbc=gbc, again=np.ascontiguousarray(again), clam=clam, csub=csub, cfar=cfar,
                tabB=np.ascontiguousarray(tabB), mulB=np.ascontiguousarray(mulB), tabC=np.ascontiguousarray(tabC),
                tabD=np.ascontiguousarray(tabD), mskD=np.ascontiguousarray(mskD), ropec=cos, ropes=sin,
                ident=np.eye(128, dtype=f))


_NC_CACHE = {}


def run(x_prompt, x_sample, **params):
    x_prompt = np.asarray(x_prompt, np.float32)
    x_sample = np.asarray(x_sample, np.float32)
    BP, NP, _ = x_prompt.shape
    BS, NS, _ = x_sample.shape
    L = np.asarray(params["w_in"]).shape[0]
    shared = prepare_shared(NMAX=max(NP, NS), **params)
    key = (NP, NS, L)
    if key not in _NC_CACHE:
        _NC_CACHE[key] = build_program(NP, NS, L)
    nc = _NC_CACHE[key]
    ncores = 8
    in_maps = []
    for c in range(ncores):
        m = dict(shared)
        m["xp"] = np.ascontiguousarray(x_prompt[(c * BP) // ncores])
        m["xs"] = np.ascontiguousarray(x_sample[c % BS])
        in_maps.append(m)
    res = run_bass_kernel_spmd(nc, in_maps, core_ids=list(range(ncores)))
    yp = np.stack([np.asarray(res.results[(b * ncores) // BP]["yp"], np.float32) for b in range(BP)], 0)
    ys = np.stack([np.asarray(res.results[c]["ys"], np.float32) for c in range(BS)], 0)
    return yp, ys


def kernel(x_prompt, x_sample, w_in, w_out, norm_g, final_g, a_q_gain, a_k_gain, t5_bias,
           c_lambda_q1, c_lambda_k1, c_lambda_q2, c_lambda_k2, c_subln_g, d_rpb):
    return run(x_prompt, x_sample, w_in=w_in, w_out=w_out, norm_g=norm_g, final_g=final_g, a_q_gain=a_q_gain,
               a_k_gain=a_k_gain, t5_bias=t5_bias, c_lambda_q1=c_lambda_q1, c_lambda_k1=c_lambda_k1,
               c_lambda_q2=c_lambda_q2, c_lambda_k2=c_lambda_k2, c_subln_g=c_subln_g, d_rpb=d_rpb)
```

```python
import math
import numpy as np
import concourse.bass as bass
import concourse.mybir as mybir
from concourse.bass_utils import run_bass_kernel_spmd

F32 = mybir.dt.float32
BF16 = mybir.dt.bfloat16
ALU = mybir.AluOpType
AF = mybir.ActivationFunctionType

D = 1024
HD = 64
GRID_W = 64
EPS = 1e-6
NCOL = 18 * 64
C_AQ, C_BQ, C_AK, C_BK, C_AQS, C_AKS, C_AG, C_BG, C_AV, C_BV, C_CQ, C_DQ, C_CK, C_DK, C_CG, C_DG, C_CV, C_DV = range(18)

B_OMAX, B_NT = 11, 23
C_OMAX, C_NT = 5, 11
D_OMAX, D_NT = 5, 11
D_SPEC = 12
TB_B = B_NT * 128
TB_C = C_NT * 128
TB_D = D_NT * 128 + D_SPEC * 512


def _t5_bucket_np(rel):
    rel = np.asarray(rel, np.int64)
    half, max_exact = 16, 8
    dist = np.abs(rel)
    lg = np.log(np.maximum(dist, 1).astype(np.float32) / np.float32(max_exact)).astype(np.float32)
    large = max_exact + (lg / np.float32(math.log(128 / max_exact)) * np.float32(half - max_exact)).astype(np.int32)
    large = np.minimum(large, half - 1)
    return np.where(rel > 0, half, 0) + np.where(dist < max_exact, dist, large)


def _b_mult(off):
    off = np.asarray(off, np.int64)
    m = np.zeros(off.shape, np.float32)
    for (w, d) in ((128, 1), (512, 4), (2048, 16)):
        m += ((off % d == 0) & (np.abs(off) <= w // 2)).astype(np.float32)
    return m


def _toep_off(omax, nt):
    p = np.arange(128)[:, None]
    c = np.arange(128)[None, :]
    return np.concatenate([(omax - m) * 128 + p - c for m in range(nt)], axis=1)


def _d_idx(tk, tq, rows):
    rk, ck = tk // GRID_W, tk % GRID_W
    rq, cq = tq // GRID_W, tq % GRID_W
    r0 = np.clip(rq - 4, 0, rows - 8)
    cs = np.clip(cq - 8, 0, GRID_W - 16)
    valid = (rk >= r0) & (rk < r0 + 8) & (ck >= cs) & (ck < cs + 16) & (tk >= 0) & (tk < rows * GRID_W)
    dr = np.clip(rk - rq + 7, 0, 14)
    dc = np.clip(ck - cq + 15, 0, 30)
    return valid, dr, dc


def _d_tables():
    rows = 64
    n = rows * GRID_W
    p = np.arange(128)[:, None]
    c128 = np.arange(128)[None, :]
    c512 = np.arange(512)[None, :]
    vs, drs, dcs = [], [], []
    jq0 = 16
    for m in range(D_NT):
        o = D_OMAX - m
        v, dr, dc = _d_idx((jq0 + o) * 128 + p, jq0 * 128 + c128, rows)
        vs.append(v); drs.append(dr); dcs.append(dc)
    for kb in range(6):
        v, dr, dc = _d_idx(kb * 128 + p, c512, rows)
        vs.append(v); drs.append(dr); dcs.append(dc)
    nb = n // 128
    for i in range(6):
        v, dr, dc = _d_idx((nb - 6 + i) * 128 + p, n - 512 + c512, rows)
        vs.append(v); drs.append(dr); dcs.append(dc)
    return (np.concatenate(vs, 1), np.concatenate(drs, 1), np.concatenate(dcs, 1))


def _rope_tables(n):
    t = np.arange(n)
    row = (t // GRID_W).astype(np.float32)
    col = (t % GRID_W).astype(np.float32)
    freqs = (np.float32(10000.0) ** (-np.arange(16, dtype=np.float32) * np.float32(2.0) / np.float32(32))).astype(np.float32)
    cos = np.zeros((64, n), np.float32)
    sin = np.zeros((64, n), np.float32)
    for d in range(64):
        pos = row if d < 32 else col
        j = d % 16
        ang = (pos * freqs[j]).astype(np.float32)
        cos[d] = np.cos(ang)
        s = np.sin(ang)
        sin[d] = -s if (d % 32) < 16 else s
    return cos, sin


def _swap_perm():
    perm = np.arange(64)
    for d in range(64):
        perm[d] = d + 16 if (d % 32) < 16 else d - 16
    return perm


class Lane:
    def __init__(self, nc, name, step, is_pe=False):
        self.sem = nc.semaphore(name).__enter__()
        self.cnt = 0
        self.step = step
        self.is_pe = is_pe
        self.name = name

    def mark(self, ins):
        self.cnt += self.step
        ins.then_inc(self.sem, self.step)
        return (self, self.cnt)


class Eng:
    def __init__(self, nc, eng, name, is_pe=False):
        self.eng = eng
        self.lane = Lane(nc, "s_" + name, 1, is_pe)
        self.waited = {}
        self.is_pe = is_pe

    def wait(self, tok):
        if tok is None:
            return
        lane, cnt = tok
        if lane is self.lane and self.is_pe:
            return
        if self.waited.get(lane.name, 0) >= cnt:
            return
        self.waited[lane.name] = cnt
        self.eng.wait_ge(lane.sem, cnt)


class T:
    __slots__ = ("w", "r", "ps")

    def __init__(self, ps=False):
        self.w = {}
        self.r = {}
        self.ps = ps


def do(E, fn, reads=(), writes=(), lane=None, embed=True):
    need = {}

    def req(tok):
        ln, cnt = tok
        if ln is E.lane and E.is_pe:
            return
        if E.waited.get(ln.name, 0) >= cnt:
            return
        if ln.name not in need or need[ln.name][1] < cnt:
            need[ln.name] = tok

    for t in reads:
        for wt in t.w.values():
            req(wt)
        if t.ps:
            for rt in t.r.values():
                if rt[0] is not E.lane:
                    req(rt)
    for t in writes:
        for rt in t.r.values():
            req(rt)
        for wt in t.w.values():
            req(wt)
    toks = list(need.values())
    emb = None
    if toks and embed and lane is None and EMBED_WAITS:
        emb = toks.pop()
    for ln, cnt in toks:
        E.waited[ln.name] = cnt
        E.eng.wait_ge(ln.sem, cnt)
    ins = fn()
    if emb is not None:
        E.waited[emb[0].name] = emb[1]
        ins.wait_op(emb[0].sem, emb[1], "sem-ge")
    tok = (lane or E.lane).mark(ins)
    for t in reads:
        t.r[tok[0].name] = tok
    for t in writes:
        t.w[tok[0].name] = tok
        t.r = {}
    return tok


EMBED_WAITS = True


def build_program(NP, NS, L):
    nc = bass.Bass("TRN2", target_bir_lowering=False)
    NMAX = max(NP, NS)
    seqs = [("p", NP), ("s", NS)]

    def dram(name, shape, dt, kind):
        return nc.dram_tensor(name, list(shape), dt, kind=kind)

    x_in = {"p": dram("xp", [NP, D], F32, "ExternalInput"), "s": dram("xs", [NS, D], F32, "ExternalInput")}
    y_out = {"p": dram("yp", [NP, D], F32, "ExternalOutput"), "s": dram("ys", [NS, D], F32, "ExternalOutput")}
    win = dram("win", [L * 4, D, NCOL], F32, "ExternalInput")
    wout = dram("wout", [L, D, D], F32, "ExternalInput")
    gbc = dram("gbc", [L + 1, 128, D], F32, "ExternalInput")
    again = dram("again", [L, 64, 4], F32, "ExternalInput")
    clam = dram("clam", [L, 64, 4 * 32], F32, "ExternalInput")
    csub = dram("csub", [L, 64, 1], F32, "ExternalInput")
    cfar = dram("cfar", [4, 128, 2], F32, "ExternalInput")
    tabB = dram("tabB", [4, 128, TB_B], F32, "ExternalInput")
    mulB = dram("mulB", [128, TB_B], F32, "ExternalInput")
    tabC = dram("tabC", [4, 128, TB_C], F32, "ExternalInput")
    tabD = dram("tabD", [L * 4, 128, TB_D], F32, "ExternalInput")
    mskD = dram("mskD", [128, TB_D], F32, "ExternalInput")
    ropec = dram("ropec", [64, NMAX], F32, "ExternalInput")
    ropes = dram("ropes", [64, NMAX], F32, "ExternalInput")
    ident_d = dram("ident", [128, 128], F32, "ExternalInput")

    xnT = {k: dram("xnT_" + k, [8, 128, n], BF16, "Internal") for k, n in seqs}
    x1 = {k: dram("x1_" + k, [n, D], F32, "Internal") for k, n in seqs}
    mixT = {k: dram("mixT_" + k, [D, n], BF16, "Internal") for k, n in seqs}
    gT = {k: dram("gT_" + k, [4, 64, n], F32, "Internal") for k, n in seqs}

    def sb(name, shape, dt):
        return nc.sbuf_tensor(name, list(shape), dt).__enter__()

    NBMAX = NMAX // 128
    QT = sb("QT", [128, NMAX], BF16)
    KT = sb("KT", [128, NMAX], BF16)
    KT2 = sb("KT2", [64, NMAX], BF16)
    VA = sb("VA", [128, NBMAX, 2, 65], BF16)
    WB = sb("WB", [128, TB_B], BF16)
    WC = sb("WC", [128, 2, TB_C], BF16)
    WD = sb("WD", [128, TB_D], BF16)
    wbuf = sb("wbuf", [128, 8, NCOL], BF16)
    wstage = [sb("wstage%d" % i, [128, 1024], F32) for i in range(2)]
    xt = [sb("xt%d" % i, [128, 8, 512], BF16) for i in range(2)]
    PT = [sb("PT%d" % i, [128, 1024], BF16) for i in range(4)]
    ftmp = [sb("ftmp%d" % i, [128, 512], F32) for i in range(10)]
    gtile = [sb("gtile%d" % i, [64, 512], F32) for i in range(2)]
    ropt = [[sb("rc%d" % i, [64, 512], F32), sb("rs%d" % i, [64, 512], F32)] for i in range(2)]
    mxt = [sb("mxt%d" % i, [64, 512], BF16) for i in range(2)]
    xrow = [sb("xrow%d" % i, [128, D], F32) for i in range(2)]
    xnrow = [sb("xnrow%d" % i, [128, D], F32) for i in range(2)]
    xnTs = [sb("xnTs%d" % i, [128, 8, 128], BF16) for i in range(2)]
    mts = [sb("mts%d" % i, [128, 8, 128], BF16) for i in range(2)]
    gsb = sb("gsb", [128, D], F32)
    ident = sb("identsb", [128, 128], F32)
    ones64 = sb("ones64", [128, 128], F32)
    sel65 = sb("sel65", [128, 128], F32)
    small = sb("small", [128, 32], F32)
    nsm = [sb("nsm%d" % i, [128, 4], F32) for i in range(2)]
    lamt = sb("lamt", [64, 4 * 32], F32)
    cfar_sb = sb("cfar_sb", [128, 8], F32)
    ps = nc.psum_tensor("ps", [128, 8, 512], F32).__enter__()

    PE = Eng(nc, nc.tensor, "pe", True)
    ACT = Eng(nc, nc.scalar, "act")
    DVE = Eng(nc, nc.vector, "dve")
    POOL = Eng(nc, nc.gpsimd, "pool")
    SP = Eng(nc, nc.sync, "sp")
    dl = {}

    def lane_for(key):
        if key not in dl:
            dl[key] = Lane(nc, "d_" + str(key).replace(" ", "").replace("'", "").replace("(", "").replace(")", "").replace(",", "_"), 16)
        return dl[key]

    NT5 = NMAX // 512
    tQT = [T() for _ in range(NT5)]
    tKT = [T() for _ in range(NT5)]
    tKT2 = [T() for _ in range(NT5)]
    tVA = [T() for _ in range(NT5)]
    tWB, tWC, tWD, tw = T(), T(), T(), T()
    twst = [T(), T()]
    txt = [T(), T()]
    tPT = [[T(), T()] for _ in PT]
    tf = [T() for _ in ftmp]
    tg = [T(), T()]
    trp = [T(), T()]
    tmx = [T(), T()]
    txrow = [T(), T()]
    txn = [T(), T()]
    txnT = [T(), T()]
    tmts = [T(), T()]
    tnsm = [T(), T()]
    tgsb, tident, tones, tsel, tsmall, tlam, tcfar = T(), T(), T(), T(), T(), T(), T()
    tps = [T(ps=True) for _ in range(8)]
    tD = {nm: {k: T() for k, _ in seqs} for nm in ("xnT", "x1", "mixT", "gT")}
    tDx1 = {(k, l_): T() for k, _ in seqs for l_ in range(L)}
    fidx = {id(t_): i_ for i_, t_ in enumerate(tf)}

    def dma(E, lane, out, in_, reads=(), writes=()):
        return do(E, lambda: E.eng.dma_start(out=out, in_=in_), reads, writes, lane=lane_for(lane))

    dma(SP, "ident", ident[:, :], ident_d[:, :], writes=[tident])
    do(DVE, lambda: nc.vector.memset(ones64[:, :], 0.0), writes=[tones])
    do(DVE, lambda: nc.vector.memset(ones64[0:64, :], 1.0 / 64.0), writes=[tones])
    do(DVE, lambda: nc.vector.memset(sel65[:, :], 0.0), writes=[tsel])
    do(DVE, lambda: nc.vector.memset(sel65[64:65, :], 1.0), writes=[tsel])
    for i_ in range(len(ftmp)):
        do(DVE, lambda i_=i_: nc.vector.memset(ftmp[i_][:, :], 0.0), writes=[tf[i_]])
    do(DVE, lambda: nc.vector.memset(VA[:, :, :, :], 1.0), writes=tVA)
    do(DVE, lambda: nc.vector.memset(KT2[:, :], 0.0), writes=tKT2)

    ftmp_rr = [0]

    def ft():
        i = ftmp_rr[0] % len(ftmp)
        ftmp_rr[0] += 1
        return ftmp[i], tf[i]

    rr = {}

    def nxt(key, n):
        i = rr.get(key, 0) % n
        rr[key] = rr.get(key, 0) + 1
        return i

    def norm_block(xr, txr, seqk, b, final):
        i = nxt("xn", 2)
        xn, tx = xnrow[i], txn[i]
        si = nxt("nsm", 2)
        sm, tsm = nsm[si], tnsm[si]
        do(ACT, lambda: nc.scalar.activation(out=xn[:, :], in_=xr[:, :], func=AF.Square, accum_out=sm[:, 0:1]),
           reads=[txr], writes=[tx, tsm], embed=False)
        do(ACT, lambda: nc.scalar.activation(out=sm[:, 1:2], in_=sm[:, 0:1], func=AF.Ln, bias=EPS, scale=1.0 / D),
           writes=[tsm])
        do(ACT, lambda: nc.scalar.activation(out=sm[:, 2:3], in_=sm[:, 1:2], func=AF.Exp, scale=-0.5),
           writes=[tsm])
        do(DVE, lambda: nc.vector.scalar_tensor_tensor(out=xn[:, :], in0=xr[:, :], scalar=sm[:, 2:3], in1=gsb[:, :],
                                                        op0=ALU.mult, op1=ALU.mult),
           reads=[txr, tsm, tgsb], writes=[tx])
        if final:
            dma(POOL, ("st_xn", i), y_out[seqk][b * 128:(b + 1) * 128, :], xn[:, :], reads=[tx])
            return
        j = nxt("xnT", 2)
        for half in range(2):
            bank = 4 + half
            for c4 in range(4):
                c = half * 4 + c4
                do(PE, lambda c=c, c4=c4, bank=bank: nc.tensor.transpose(out=ps[:, bank, c4 * 128:(c4 + 1) * 128],
                                                                         in_=xn[:, c * 128:(c + 1) * 128], identity=ident[:, :]),
                   reads=[tx, tident], writes=[tps[bank]])
            src = ps[:, bank, :].rearrange("p (c t) -> p c t", c=4)
            dst = xnTs[j][:, half * 4:(half + 1) * 4, :]
            if half == 0:
                do(ACT, lambda src=src, dst=dst: nc.scalar.copy(out=dst, in_=src), reads=[tps[bank]], writes=[txnT[j]])
            else:
                do(DVE, lambda src=src, dst=dst: nc.vector.tensor_copy(out=dst, in_=src), reads=[tps[bank]], writes=[txnT[j]])
        dma(POOL, ("st_xnT", j), xnT[seqk][:, :, b * 128:(b + 1) * 128].rearrange("c p t -> p c t"), xnTs[j][:, :, :],
            reads=[txnT[j]], writes=[tD["xnT"][seqk]])

    def load_g(l):
        dma(SP, "gsb", gsb[:, :], gbc[l, :, :], writes=[tgsb])

    def load_cast(dst_ap_fn, src_ap_fn, ncols_total, tdst):
        per = 128
        flip = 0
        for c0 in range(0, ncols_total, per):
            w = min(per, ncols_total - c0)
            i = nxt("wst", 2)
            st = wstage[i]
            stv = st[:, 0:8 * w].rearrange("p (c n) -> p c n", c=8)
            dma(SP, ("wst", i), stv, src_ap_fn(c0, w), writes=[twst[i]])
            if flip % 2 == 0:
                do(ACT, lambda stv=stv, c0=c0, w=w: nc.scalar.copy(out=dst_ap_fn(c0, w), in_=stv), reads=[twst[i]], writes=[tdst])
            else:
                do(DVE, lambda stv=stv, c0=c0, w=w: nc.vector.tensor_copy(out=dst_ap_fn(c0, w), in_=stv), reads=[twst[i]], writes=[tdst])
            flip += 1

    def build_tables(l, hg):
        def one(Wt, tW, src, ncols, mul_src):
            for c0 in range(0, ncols, 1024):
                w = min(1024, ncols - c0)
                i = nxt("wst", 2)
                st = wstage[i]
                dma(SP, ("wst", i), st[:, 0:w], src[:, c0:c0 + w], writes=[twst[i]])
                if mul_src is None:
                    do(ACT, lambda st=st, c0=c0, w=w: nc.scalar.activation(out=Wt[:, c0:c0 + w], in_=st[:, 0:w], func=AF.Exp),
                       reads=[twst[i]], writes=[tW])
                else:
                    do(ACT, lambda st=st, w=w: nc.scalar.activation(out=st[:, 0:w], in_=st[:, 0:w], func=AF.Exp),
                       reads=[twst[i]], writes=[twst[i]])
                    i2 = nxt("wst", 2)
                    st2 = wstage[i2]
                    dma(SP, ("wst", i2), st2[:, 0:w], mul_src[:, c0:c0 + w], writes=[twst[i2]])
                    do(DVE, lambda st=st, st2=st2, c0=c0, w=w: nc.vector.tensor_tensor(out=Wt[:, c0:c0 + w], in0=st[:, 0:w],
                                                                                        in1=st2[:, 0:w], op=ALU.mult),
                       reads=[twst[i], twst[i2]], writes=[tW])
        one(WB, tWB, tabB[hg], TB_B, mulB)
        one(WD, tWD, tabD[l * 4 + hg], TB_D, mskD)
        dma(SP, "cfar", cfar_sb[:, 0:2], cfar[hg, :, :], writes=[tcfar])
        do(DVE, lambda: nc.vector.tensor_scalar(out=cfar_sb[:, 2:4], in0=cfar_sb[:, 0:2], scalar1=-1.0, scalar2=None, op0=ALU.mult),
           reads=[tcfar], writes=[tcfar])
        do(DVE, lambda: nc.vector.tensor_tensor(out=cfar_sb[:, 4:5], in0=cfar_sb[:, 1:2], in1=cfar_sb[:, 0:1], op=ALU.subtract),
           reads=[tcfar], writes=[tcfar])
        do(DVE, lambda: nc.vector.tensor_tensor(out=cfar_sb[:, 5:6], in0=cfar_sb[:, 0:1], in1=cfar_sb[:, 1:2], op=ALU.subtract),
           reads=[tcfar], writes=[tcfar])
        for c0 in range(0, TB_C, 1024):
            w = min(1024, TB_C - c0)
            i = nxt("wst", 2)
            st = wstage[i]
            dma(SP, ("wst", i), st[:, 0:w], tabC[hg][:, c0:c0 + w], writes=[twst[i]])
            for var in range(2):
                do(ACT, lambda st=st, c0=c0, w=w, var=var: nc.scalar.activation(out=WC[:, var, c0:c0 + w], in_=st[:, 0:w], func=AF.Exp,
                                                                                bias=cfar_sb[:, 2 + var:3 + var]),
                   reads=[twst[i], tcfar], writes=[tWC])

    def layer_params(l):
        dma(SP, "small", small[0:64, 8:12], again[l, :, :], writes=[tsmall])
        dma(SP, "lamt", lamt[:, :], clam[l, :, :], writes=[tlam])
        dma(SP, "small", small[0:64, 13:14], csub[l, :, :], writes=[tsmall])
        li = 0.8 - 0.6 * math.exp(-0.3 * l)
        pr, tpr = ft()
        do(DVE, lambda: nc.vector.tensor_tensor(out=pr[0:64, 0:32], in0=lamt[:, 0:32], in1=lamt[:, 32:64], op=ALU.mult),
           reads=[tlam], writes=[tpr])
        do(DVE, lambda: nc.vector.tensor_tensor(out=pr[0:64, 32:64], in0=lamt[:, 64:96], in1=lamt[:, 96:128], op=ALU.mult),
           reads=[tlam], writes=[tpr])
        do(DVE, lambda: nc.vector.reduce_sum(out=small[0:64, 14:15], in_=pr[0:64, 0:32], axis=mybir.AxisListType.X),
           reads=[tpr], writes=[tsmall])
        do(DVE, lambda: nc.vector.reduce_sum(out=small[0:64, 15:16], in_=pr[0:64, 32:64], axis=mybir.AxisListType.X),
           reads=[tpr], writes=[tsmall])
        do(ACT, lambda: nc.scalar.activation(out=small[0:64, 16:18], in_=small[0:64, 14:16], func=AF.Exp),
           writes=[tsmall])
        do(DVE, lambda: nc.vector.scalar_tensor_tensor(out=small[0:64, 12:13], in0=small[0:64, 17:18], scalar=-li,
                                                        in1=small[0:64, 16:17], op0=ALU.add, op1=ALU.subtract),
           writes=[tsmall])
        do(DVE, lambda: nc.vector.tensor_scalar(out=small[0:64, 13:14], in0=small[0:64, 13:14], scalar1=1.0 - li, scalar2=None,
                                                op0=ALU.mult),
           writes=[tsmall])

    def load_win(l, hg):
        load_cast(lambda c0, w: wbuf[:, :, c0:c0 + w],
                  lambda c0, w: win[l * 4 + hg, :, c0:c0 + w].rearrange("(c p) n -> p c n", p=128),
                  NCOL, tw)

    def inproj(seqk, n, l, hg, pr_):
        BST = globals().get("BSTEP", 99)
        for tt in range(n // 512):
            t0 = tt * 512
            xi = nxt("xt", 2)
            x_t, tx = xt[xi], txt[xi]
            dma(SP, ("xt", xi), x_t[:, :, :], xnT[seqk][:, :, t0:t0 + 512].rearrange("c p t -> p c t"),
                reads=[tD["xnT"][seqk]], writes=[tx])
            if pr_ == 0:
                ri = nxt("rp", 2)
                dma(SP, ("rp", ri), ropt[ri][0][:, :], ropec[:, t0:t0 + 512], writes=[trp[ri]])
                dma(SP, ("rp", ri), ropt[ri][1][:, :], ropes[:, t0:t0 + 512], writes=[trp[ri]])

            def proj(bank, col0, m):
                for c in range(8):
                    do(PE, lambda c=c: nc.tensor.matmul(out=ps[0:m, bank, :], lhsT=wbuf[:, c, col0:col0 + m], rhs=x_t[:, c, :],
                                                        start=(c == 0), stop=(c == 7)),
                       reads=[tx, tw], writes=[tps[bank]])

            def a_part(bank_main, bank_sw, gcol, dst, tdst):
                BSUB = globals().get("BSUB", 99)
                sq, tsq = ft()
                do(ACT, lambda: nc.scalar.activation(out=sq[0:64, :], in_=ps[0:64, bank_main, :], func=AF.Square),
                   reads=[tps[bank_main]], writes=[tsq])
                if globals().get("VARIANT", 0) in (7, 8):
                    t9, tt9 = ft()
                    do(DVE, lambda: nc.vector.tensor_copy(out=t9[:, :], in_=ps[:, bank_main, :]),
                       reads=[tps[bank_main]] + ([tsq] if globals().get("VARIANT", 0) == 8 else []), writes=[tt9])
                if BSUB <= 1:
                    return
                do(PE, lambda: nc.tensor.matmul(out=ps[:, 7, :], lhsT=ones64[:, :], rhs=sq[:, :], start=True, stop=True),
                   reads=[tsq, tones], writes=[tps[7]])
                if BSUB <= 2:
                    return
                ln_, tln = ft()
                do(ACT, lambda: nc.scalar.activation(out=ln_[0:64, :], in_=ps[0:64, 7, :], func=AF.Ln, bias=EPS, scale=1.0),
                   reads=[tps[7]], writes=[tln])
                if BSUB <= 3:
                    return
                do(ACT, lambda: nc.scalar.activation(out=ln_[0:64, :], in_=ln_[0:64, :], func=AF.Exp, scale=-0.5),
                   writes=[tln])
                if BSUB <= 4:
                    return
                t1, tt1 = ft()
                VAR = globals().get("VARIANT", 0)
                if VAR == 0:
                    do(DVE, lambda: nc.vector.scalar_tensor_tensor(out=t1[0:64, :], in0=ps[0:64, bank_main, :],
                                                                    scalar=small[0:64, gcol:gcol + 1], in1=ropt[ri][0][:, :],
                                                                    op0=ALU.mult, op1=ALU.mult),
                       reads=[tps[bank_main], tsmall, trp[ri]], writes=[tt1])
                elif VAR == 1:
                    do(DVE, lambda: nc.vector.tensor_scalar(out=t1[0:64, :], in0=ps[0:64, bank_main, :],
                                                            scalar1=small[0:64, gcol:gcol + 1], scalar2=None, op0=ALU.mult),
                       reads=[tps[bank_main], tsmall], writes=[tt1])
                elif VAR == 2:
                    do(DVE, lambda: nc.vector.tensor_tensor(out=t1[0:64, :], in0=ps[0:64, bank_main, :], in1=ropt[ri][0][:, :], op=ALU.mult),
                       reads=[tps[bank_main], trp[ri]], writes=[tt1])
                elif VAR == 4:
                    do(DVE, lambda: nc.vector.tensor_copy(out=t1[0:64, :], in_=ps[0:64, bank_main, :]),
                       reads=[tps[bank_main]], writes=[tt1])
                elif VAR == 5:
                    do(DVE, lambda: nc.vector.tensor_copy(out=t1[:, :], in_=ps[:, bank_main, :]),
                       reads=[tps[bank_main]], writes=[tt1])
                elif VAR == 6:
                    do(DVE, lambda: nc.vector.tensor_copy(out=t1[0:64, :], in_=sq[0:64, :]),
                       reads=[tsq], writes=[tt1])
                elif VAR == 3:
                    do(DVE, lambda: nc.vector.scalar_tensor_tensor(out=t1[0:64, :], in0=ps[0:64, bank_main, :],
                                                                    scalar=2.0, in1=ropt[ri][0][:, :],
                                                                    op0=ALU.mult, op1=ALU.mult),
                       reads=[tps[bank_main], trp[ri]], writes=[tt1])
                if BSUB <= 5:
                    return
                t2, tt2 = ft()
                do(DVE, lambda: nc.vector.scalar_tensor_tensor(out=t2[0:64, :], in0=ps[0:64, bank_sw, :],
                                                                scalar=small[0:64, gcol + 1:gcol + 2], in1=ropt[ri][1][:, :],
                                                                op0=ALU.mult, op1=ALU.mult),
                   reads=[tps[bank_sw], tsmall, trp[ri]], writes=[tt2])
                do(DVE, lambda: nc.vector.tensor_tensor(out=t1[0:64, :], in0=t1[0:64, :], in1=t2[0:64, :], op=ALU.add),
                   reads=[tt2], writes=[tt1])
                if BSUB <= 6:
                    return
                do(DVE, lambda: nc.vector.tensor_tensor(out=dst[0:64, t0:t0 + 512], in0=t1[0:64, :], in1=ln_[0:64, :], op=ALU.mult),
                   reads=[tt1, tln], writes=[tdst])

            if BST <= 1:
                continue
            if pr_ == 0:
                proj(0, C_AQ * 64, 128)
                if BST <= 2:
                    continue
                proj(1, C_AQS * 64, 64)
                if BST <= 3:
                    continue
                a_part(0, 1, 8, QT, tQT[tt])
                if BST <= 4:
                    continue
                do(ACT, lambda: nc.scalar.copy(out=QT[64:128, t0:t0 + 512], in_=ps[64:128, 0, :]),
                   reads=[tps[0]], writes=[tQT[tt]])
                if BST <= 5:
                    continue
                proj(2, C_AK * 64, 128)
                proj(3, C_AKS * 64, 64)
                a_part(2, 3, 10, KT, tKT[tt])
                do(ACT, lambda: nc.scalar.copy(out=KT[64:128, t0:t0 + 512], in_=ps[64:128, 2, :]),
                   reads=[tps[2]], writes=[tKT[tt]])
                gcol0, vcol0 = C_AG, C_AV
            else:
                proj(0, C_CQ * 64, 128)
                do(DVE, lambda: nc.vector.tensor_copy(out=QT[:, t0:t0 + 512], in_=ps[:, 0, :]),
                   reads=[tps[0]], writes=[tQT[tt]])
                proj(2, C_CK * 64, 128)
                do(DVE, lambda: nc.vector.tensor_copy(out=KT[0:32, t0:t0 + 512], in_=ps[0:32, 2, :]),
                   reads=[tps[2]], writes=[tKT[tt]])
                do(DVE, lambda: nc.vector.memset(KT[32:64, t0:t0 + 512], 0.0), writes=[tKT[tt]])
                do(DVE, lambda: nc.vector.tensor_copy(out=KT2[32:64, t0:t0 + 512], in_=ps[32:64, 2, :]),
                   reads=[tps[2]], writes=[tKT2[tt]])
                do(DVE, lambda: nc.vector.tensor_copy(out=KT[64:128, t0:t0 + 512], in_=ps[64:128, 2, :]),
                   reads=[tps[2]], writes=[tKT[tt]])
                gcol0, vcol0 = C_CG, C_CV
            if BST <= 6:
                continue
            proj(4, gcol0 * 64, 128)
            gt_, tgt = ft()
            do(ACT, lambda gt_=gt_: nc.scalar.activation(out=gt_[:, :], in_=ps[:, 4, :], func=AF.Silu),
               reads=[tps[4]], writes=[tgt])
            dma(POOL, ("st_f", fidx[id(tgt)]), gT[seqk][pr_ * 2:pr_ * 2 + 2, :, t0:t0 + 512].rearrange("m p t -> (m p) t"), gt_[:, :],
                reads=[tgt], writes=[tD["gT"][seqk]])
            if BST <= 7:
                continue
            for s4 in range(4):
                bank = 5 + (s4 % 2)
                for c in range(8):
                    do(PE, lambda c=c, s4=s4, bank=bank: nc.tensor.matmul(out=ps[:, bank, 0:128], lhsT=x_t[:, c, s4 * 128:(s4 + 1) * 128],
                                                                          rhs=wbuf[:, c, vcol0 * 64:vcol0 * 64 + 128],
                                                                          start=(c == 0), stop=(c == 7)),
                       reads=[tx, tw], writes=[tps[bank]])
                blk = tt * 4 + s4
                srcv = ps[:, bank, 0:128].rearrange("p (m d) -> p m d", m=2)
                if s4 % 2 == 0:
                    do(DVE, lambda blk=blk, srcv=srcv: nc.vector.tensor_copy(out=VA[:, blk, :, 0:64], in_=srcv),
                       reads=[tps[bank]], writes=[tVA[tt]])
                else:
                    do(ACT, lambda blk=blk, srcv=srcv: nc.scalar.copy(out=VA[:, blk, :, 0:64], in_=srcv),
                       reads=[tps[bank]], writes=[tVA[tt]])

    def attention(seqk, n, l, hg, pr_):
        nb = n // 128
        nq = n // 512
        sc64 = HD ** -0.5
        sc32 = 32 ** -0.5
        runs = []
        for m in (2 * pr_, 2 * pr_ + 1):
            for j in range(nq):
                units = []
                if m == 0:
                    for kb in range(0, nb, 2):
                        units.append((kb, "plain", None, None))
                    runs.append((m, 0, j, units))
                elif m == 1:
                    for kb in range(max(0, 4 * j - 8), min(nb, 4 * j + 12), 2):
                        v = [(B_OMAX - (k - 4 * j)) * 128 for k in (kb, kb + 1)]
                        units.append((kb, "w", WB[:, v[0]:v[0] + 512], WB[:, v[1]:v[1] + 512]))
                    runs.append((m, 0, j, units))
                elif m == 2:
                    nL = len([kb for kb in range(0, nb, 2) if kb - 4 * j < -2])
                    nR = len([kb for kb in range(0, nb, 2) if kb - 4 * j > 4])
                    ref = 0 if nL >= nR else 1
                    for kb in range(0, nb, 2):
                        o = kb - 4 * j
                        if -2 <= o <= 4:
                            v = [(C_OMAX - (k - 4 * j)) * 128 for k in (kb, kb + 1)]
                            units.append((kb, "w", WC[:, ref, v[0]:v[0] + 512], WC[:, ref, v[1]:v[1] + 512]))
                        elif (o < 0) == (ref == 0):
                            units.append((kb, "plain", None, None))
                        else:
                            units.append((kb, "farL" if o < 0 else "farR", None, None))
                    runs.append((m, 0, j, units))
                    runs.append((m, 1, j, units))
                else:
                    if j == 0:
                        for i in range(0, 6, 2):
                            b0 = D_NT * 128 + i * 512
                            units.append((i, "w", WD[:, b0:b0 + 512], WD[:, b0 + 512:b0 + 1024]))
                    elif j == nq - 1:
                        for i in range(0, 6, 2):
                            b0 = D_NT * 128 + (6 + i) * 512
                            units.append((nb - 6 + i, "w", WD[:, b0:b0 + 512], WD[:, b0 + 512:b0 + 1024]))
                    else:
                        for kb in range(4 * j - 2, 4 * j + 6, 2):
                            v = [(D_OMAX - (k - 4 * j)) * 128 for k in (kb, kb + 1)]
                            units.append((kb, "w", WD[:, v[0]:v[0] + 512], WD[:, v[1]:v[1] + 512]))
                    runs.append((m, 0, j, units))

        flat = []
        for ri_, (m, mp, j, units) in enumerate(runs):
            for ui, u in enumerate(units):
                flat.append((ri_, ui, len(units), m, mp, j, u))

        run_obank = {}
        unit_pi = {}
        pending = []
        c_hold = {}

        def qk_operands(m, mp, j, kb):
            p0 = 64 * (m % 2)
            if m == 2 and mp == 1:
                return (KT2[0:64, kb * 128:(kb + 1) * 128], QT[0:64, j * 512:(j + 1) * 512], tKT2[kb // 4], tQT[j])
            return (KT[p0:p0 + 64, kb * 128:(kb + 1) * 128], QT[p0:p0 + 64, j * 512:(j + 1) * 512], tKT[kb // 4], tQT[j])

        def emit_S(idx):
            ri_, ui, nu, m, mp, j, (kb, mode, w0, w1) = flat[idx]
            sp_ = idx % 2
            for h in range(2):
                lhsT, rhs, tk, tq = qk_operands(m, mp, j, kb + h)
                bank = sp_ * 2 + h
                do(PE, lambda lhsT=lhsT, rhs=rhs, bank=bank: nc.tensor.matmul(out=ps[:, bank, :], lhsT=lhsT, rhs=rhs, start=True, stop=True),
                   reads=[tk, tq], writes=[tps[bank]])

        def finalize_A(ob):
            osb, tosb = ft()
            ohi, tohi = ft()
            i_lo, i_hi = fidx[id(tosb)], fidx[id(tohi)]
            if i_hi == i_lo + 1:
                pass
            do(DVE, lambda: nc.vector.tensor_copy(out=osb[0:65, :], in_=ps[0:65, ob, :]), reads=[tps[ob]], writes=[tosb])
            do(DVE, lambda: nc.vector.tensor_tensor(out=osb[0:65, :], in0=ps[0:65, ob + 1, :], in1=osb[0:65, :], op=ALU.add),
               reads=[tps[ob + 1]], writes=[tosb])
            return osb, tosb, ob

        def finalize_B(m, j, holders):
            outs = []
            mb = holders[-1][2]
            for (osb, tosb, ob_) in holders:
                do(PE, lambda osb=osb: nc.tensor.matmul(out=ps[:, mb, :], lhsT=sel65[:, :], rhs=osb[:, :], start=True, stop=True),
                   reads=[tosb, tsel], writes=[tps[mb]])
                rd, trd = ft()
                do(ACT, lambda rd=rd: nc.scalar.activation(out=rd[0:64, :], in_=ps[0:64, mb, :], func=AF.Ln), reads=[tps[mb]], writes=[trd])
                do(ACT, lambda rd=rd: nc.scalar.activation(out=rd[0:64, :], in_=rd[0:64, :], func=AF.Exp, scale=-1.0), writes=[trd])
                do(DVE, lambda rd=rd, osb=osb: nc.vector.tensor_tensor(out=rd[0:64, :], in0=osb[0:64, :], in1=rd[0:64, :], op=ALU.mult),
                   reads=[tosb], writes=[trd])
                outs.append((rd, trd))
            o, to = outs[0]
            gi = nxt("g", 2)
            dma(SP, ("g", gi), gtile[gi][:, :], gT[seqk][m, :, j * 512:(j + 1) * 512], reads=[tD["gT"][seqk]], writes=[tg[gi]])
            if m == 2:
                o2, to2 = outs[1]
                do(DVE, lambda: nc.vector.scalar_tensor_tensor(out=o[0:64, :], in0=o2[0:64, :], scalar=small[0:64, 12:13], in1=o[0:64, :],
                                                                op0=ALU.mult, op1=ALU.add),
                   reads=[to2, tsmall], writes=[to])
                sq, tsq = ft()
                do(ACT, lambda: nc.scalar.activation(out=sq[0:64, :], in_=o[0:64, :], func=AF.Square), reads=[to], writes=[tsq])
                obc = mb
                do(PE, lambda: nc.tensor.matmul(out=ps[:, obc, :], lhsT=ones64[:, :], rhs=sq[:, :], start=True, stop=True),
                   reads=[tsq, tones], writes=[tps[obc]])
                do(ACT, lambda: nc.scalar.activation(out=sq[0:64, :], in_=ps[0:64, obc, :], func=AF.Ln, bias=EPS, scale=1.0),
                   reads=[tps[obc]], writes=[tsq])
                do(ACT, lambda: nc.scalar.activation(out=sq[0:64, :], in_=sq[0:64, :], func=AF.Exp, scale=-0.5), writes=[tsq])
                do(DVE, lambda: nc.vector.scalar_tensor_tensor(out=o[0:64, :], in0=o[0:64, :], scalar=small[0:64, 13:14], in1=sq[0:64, :],
                                                                op0=ALU.mult, op1=ALU.mult),
                   reads=[tsq, tsmall], writes=[to])
            mi = nxt("mx", 2)
            do(DVE, lambda: nc.vector.tensor_tensor(out=mxt[mi][:, :], in0=o[0:64, :], in1=gtile[gi][:, :], op=ALU.mult),
               reads=[to, tg[gi]], writes=[tmx[mi]])
            r0 = m * 256 + hg * 64
            dma(POOL, ("st_mx", mi), mixT[seqk][r0:r0 + 64, j * 512:(j + 1) * 512], mxt[mi][:, :], reads=[tmx[mi]], writes=[tD["mixT"][seqk]])

        def emit_rest(idx):
            ri_, ui, nu, m, mp, j, (kb, mode, w0, w1) = flat[idx]
            sp_ = idx % 2
            pi = nxt("pt", len(PT))
            scale = sc32 if m == 2 else sc64
            src = ps[:, sp_ * 2:sp_ * 2 + 2, :]
            dst = PT[pi][:, :].rearrange("p (b q) -> p b q", b=2)
            if mode in ("farL", "farR"):
                bcol = cfar_sb[:, 5:6] if mode == "farL" else cfar_sb[:, 4:5]
                do(ACT, lambda: nc.scalar.activation(out=dst, in_=src, func=AF.Exp, bias=bcol, scale=scale),
                   reads=[tps[sp_ * 2], tps[sp_ * 2 + 1], tcfar], writes=tPT[pi])
            else:
                do(ACT, lambda: nc.scalar.activation(out=dst, in_=src, func=AF.Exp, scale=scale),
                   reads=[tps[sp_ * 2], tps[sp_ * 2 + 1]], writes=tPT[pi])
            if mode == "w":
                tW = {1: tWB, 2: tWC, 3: tWD}[m]
                for h, wv in ((0, w0), (1, w1)):
                    do(DVE, lambda h=h, wv=wv: nc.vector.tensor_tensor(out=PT[pi][:, h * 512:(h + 1) * 512], in0=PT[pi][:, h * 512:(h + 1) * 512],
                                                                        in1=wv, op=ALU.mult),
                       reads=[tW], writes=[tPT[pi][h]])
            unit_pi[idx] = pi

        def emit_pv(idx):
            ri_, ui, nu, m, mp, j, (kb, mode, w0, w1) = flat[idx]
            pi = unit_pi.pop(idx)
            if ui == 0:
                run_obank[ri_] = 4 + 2 * nxt("ob", 2)
            ob = run_obank[ri_]
            ml = m % 2
            for h in range(2):
                k = kb + h
                for half in range(2):
                    r0_ = 64 * half
                    do(PE, lambda h=h, k=k, half=half, r0_=r0_: nc.tensor.matmul(
                        out=ps[0:65, ob + half, :], lhsT=VA[r0_:r0_ + 64, k, ml, :], rhs=PT[pi][r0_:r0_ + 64, h * 512:(h + 1) * 512],
                        start=(ui == 0 and h == 0), stop=(ui == nu - 1 and h == 1)),
                       reads=[tPT[pi][h], tVA[k // 4]], writes=[tps[ob + half]])
            if ui == nu - 1:
                hold = finalize_A(ob)
                if m == 2:
                    c_hold.setdefault(j, []).append(hold)
                    if mp == 1:
                        hs = c_hold.pop(j)
                        pending.append((idx + 3, lambda hs=hs, m=m, j=j: finalize_B(m, j, hs)))
                else:
                    pending.append((idx + 3, lambda hold=hold, m=m, j=j: finalize_B(m, j, [hold])))

        NF = len(flat)
        emit_S(0)
        if NF > 1:
            emit_S(1)
        for idx in range(NF):
            emit_rest(idx)
            if idx + 2 < NF:
                emit_S(idx + 2)
            if idx >= 1:
                emit_pv(idx - 1)
            while pending and pending[0][0] <= idx:
                pending.pop(0)[1]()
        emit_pv(NF - 1)
        while pending:
            pending.pop(0)[1]()

    def outphase(seqk, n, l):
        last = (l == L - 1)
        load_cast(lambda c0, w: wbuf[:, :, c0:c0 + w],
                  lambda c0, w: wout[l, :, c0:c0 + w].rearrange("(c p) n -> p c n", p=128),
                  D, tw)
        load_g(l + 1)
        xsrc = x_in[seqk] if l == 0 else x1[seqk]
        for b in range(n // 128):
            mi = nxt("mts", 2)
            dma(SP, ("mts", mi), mts[mi][:, :, :], mixT[seqk][:, b * 128:(b + 1) * 128].rearrange("(c p) t -> p c t", p=128),
                reads=[tD["mixT"][seqk]], writes=[tmts[mi]])
            xi = nxt("xrow", 2)
            dma(SP, ("xrow", xi), xrow[xi][:, :], xsrc[b * 128:(b + 1) * 128, :], reads=([tDx1[(seqk, l - 1)]] if l > 0 else []), writes=[txrow[xi]])
            for half in range(2):
                bank = half + 2 * (b % 2)
                for c in range(8):
                    do(PE, lambda c=c, half=half, bank=bank: nc.tensor.matmul(out=ps[:, bank, :], lhsT=mts[mi][:, c, :],
                                                                              rhs=wbuf[:, c, half * 512:(half + 1) * 512],
                                                                              start=(c == 0), stop=(c == 7)),
                       reads=[tmts[mi], tw], writes=[tps[bank]])
                do(DVE, lambda half=half, bank=bank: nc.vector.tensor_tensor(out=xrow[xi][:, half * 512:(half + 1) * 512],
                                                                             in0=ps[:, bank, :], in1=xrow[xi][:, half * 512:(half + 1) * 512],
                                                                             op=ALU.add),
                   reads=[tps[bank]], writes=[txrow[xi]])
            if not last:
                dma(POOL, ("st_xrow", xi), x1[seqk][b * 128:(b + 1) * 128, :], xrow[xi][:, :], reads=[txrow[xi]], writes=[tDx1[(seqk, l)]])
            norm_block(xrow[xi], txrow[xi], seqk, b, final=last)

    load_g(0)
    for seqk, n in seqs:
        for b in range(n // 128):
            xi = nxt("xrow", 2)
            dma(SP, ("xrow", xi), xrow[xi][:, :], x_in[seqk][b * 128:(b + 1) * 128, :], writes=[txrow[xi]])
            norm_block(xrow[xi], txrow[xi], seqk, b, final=False)
    BIS = globals().get("BISECT", "full")
    for l in range(L):
        if BIS == "norm":
            break
        layer_params(l)
        for hg in range(4):
            build_tables(l, hg)
            if BIS == "tables":
                continue
            load_win(l, hg)
            if BIS == "loadwin":
                continue
            for seqk, n in seqs:
                for pr_ in range(2):
                    inproj(seqk, n, l, hg, pr_)
                    if BIS == "inproj":
                        continue
                    attention(seqk, n, l, hg, pr_)
        if BIS in ("tables", "inproj", "attn", "loadwin"):
            continue
        for seqk, n in seqs:
            outphase(seqk, n, l)
    engs = [PE, ACT, DVE, POOL, SP]
    for E in engs:
        for O in engs:
            if O is not E and O.lane.cnt > 0:
                E.eng.wait_ge(O.lane.sem, O.lane.cnt)
        for k, ln in dl.items():
            if ln.cnt > 0:
                E.eng.wait_ge(ln.sem, ln.cnt)
    return nc


def prepare_shared(w_in, w_out, norm_g, final_g, a_q_gain, a_k_gain, t5_bias, c_lambda_q1, c_lambda_k1,
                   c_lambda_q2, c_lambda_k2, c_subln_g, d_rpb, NMAX):
    L = w_in.shape[0]
    f = np.float32
    w_in = np.asarray(w_in, f); w_out = np.asarray(w_out, f)
    perm = _swap_perm()
    offs = np.cumsum([0, 256, 128, 128, 256] + [256] * 12)
    names = ["aq", "ak", "av", "ag", "bq", "bk", "bv", "bg", "cq", "ck", "cv", "cg", "dq", "dk", "dv", "dg"]
    o = {nm: int(offs[i]) for i, nm in enumerate(names)}
    win = np.zeros((L * 4, D, NCOL), f)
    for l in range(L):
        for hg in range(4):
            W = w_in[l]
            kvh = hg // 2
            aq = W[:, o["aq"] + hg * 64: o["aq"] + (hg + 1) * 64]
            ak = W[:, o["ak"] + kvh * 64: o["ak"] + (kvh + 1) * 64]
            av = W[:, o["av"] + kvh * 64: o["av"] + (kvh + 1) * 64]
            sl = lambda nm: W[:, o[nm] + hg * 64: o[nm] + (hg + 1) * 64]
            blocks = [None] * 18
            blocks[C_AQ], blocks[C_BQ], blocks[C_CQ], blocks[C_DQ] = aq, sl("bq"), sl("cq"), sl("dq")
            blocks[C_AK], blocks[C_BK], blocks[C_CK], blocks[C_DK] = ak, sl("bk"), sl("ck"), sl("dk")
            blocks[C_AQS], blocks[C_AKS] = aq[:, perm], ak[:, perm]
            blocks[C_AG], blocks[C_BG], blocks[C_CG], blocks[C_DG] = sl("ag"), sl("bg"), sl("cg"), sl("dg")
            blocks[C_AV], blocks[C_BV], blocks[C_CV], blocks[C_DV] = av, sl("bv"), sl("cv"), sl("dv")
            win[l * 4 + hg] = np.concatenate(blocks, axis=1)
    gb = np.concatenate([np.asarray(norm_g, f), np.asarray(final_g, f)[None]], 0)
    gbc = np.ascontiguousarray(np.broadcast_to(gb[:, None, :], (L + 1, 128, D)))
    aqg = np.asarray(a_q_gain, f); akg = np.asarray(a_k_gain, f)
    again = np.stack([aqg, aqg[:, perm], akg, akg[:, perm]], axis=2)
    lam = np.concatenate([np.asarray(c_lambda_q1, f), np.asarray(c_lambda_k1, f),
                          np.asarray(c_lambda_q2, f), np.asarray(c_lambda_k2, f)], axis=1)
    clam = np.ascontiguousarray(np.broadcast_to(lam[:, None, :], (L, 64, 128)))
    csub = np.asarray(c_subln_g, f)[:, :, None].copy()
    t5 = np.asarray(t5_bias, f)
    cfar = np.zeros((4, 128, 2), f)
    for h in range(4):
        cfar[h, :, 0] = t5[15, 4 + h]
        cfar[h, :, 1] = t5[31, 4 + h]
    offB = _toep_off(B_OMAX, B_NT)
    bkB = _t5_bucket_np(offB)
    tabB = np.stack([t5[bkB, h] for h in range(4)], 0).astype(f)
    mulB = _b_mult(offB).astype(f)
    offC = _toep_off(C_OMAX, C_NT)
    bkC = _t5_bucket_np(offC)
    tabC = np.stack([t5[bkC, 4 + h] for h in range(4)], 0).astype(f)
    vD, drD, dcD = _d_tables()
    rpb = np.asarray(d_rpb, f)
    tabD = np.stack([rpb[l, h][drD, dcD] for l in range(L) for h in range(4)], 0).astype(f)
    mskD = vD.astype(f)
    cos, sin = _rope_tables(NMAX)
    return dict(win=win, wout=w_out, gbc=gbc, again=np.ascontiguousarray(again), clam=clam, csub=csub, cfar=cfar,
                tabB=np.ascontiguousarray(tabB), mulB=np.ascontiguousarray(mulB), tabC=np.ascontiguousarray(tabC),
                tabD=np.ascontiguousarray(tabD), mskD=np.ascontiguousarray(mskD), ropec=cos, ropes=sin,
                ident=np.eye(128, dtype=f))


_NC_CACHE = {}


def run(x_prompt, x_sample, **params):
    x_prompt = np.asarray(x_prompt, np.float32)
    x_sample = np.asarray(x_sample, np.float32)
    BP, NP, _ = x_prompt.shape
    BS, NS, _ = x_sample.shape
    L = np.asarray(params["w_in"]).shape[0]
    shared = prepare_shared(NMAX=max(NP, NS), **params)
    key = (NP, NS, L)
    if key not in _NC_CACHE:
        _NC_CACHE[key] = build_program(NP, NS, L)
    nc = _NC_CACHE[key]
    ncores = 8
    in_maps = []
    for c in range(ncores):
        m = dict(shared)
        m["xp"] = np.ascontiguousarray(x_prompt[(c * BP) // ncores])
        m["xs"] = np.ascontiguousarray(x_sample[c % BS])
        in_maps.append(m)
    res = run_bass_kernel_spmd(nc, in_maps, core_ids=list(range(ncores)))
    yp = np.stack([np.asarray(res.results[(b * ncores) // BP]["yp"], np.float32) for b in range(BP)], 0)
    ys = np.stack([np.asarray(res.results[c]["ys"], np.float32) for c in range(BS)], 0)
    return yp, ys


def kernel(x_prompt, x_sample, w_in, w_out, norm_g, final_g, a_q_gain, a_k_gain, t5_bias,
           c_lambda_q1, c_lambda_k1, c_lambda_q2, c_lambda_k2, c_subln_g, d_rpb):
    return run(x_prompt, x_sample, w_in=w_in, w_out=w_out, norm_g=norm_g, final_g=final_g, a_q_gain=a_q_gain,
               a_k_gain=a_k_gain, t5_bias=t5_bias, c_lambda_q1=c_lambda_q1, c_lambda_k1=c_lambda_k1,
               c_lambda_q2=c_lambda_q2, c_lambda_k2=c_lambda_k2, c_subln_g=c_subln_g, d_rpb=d_rpb)
```

```python
import math
import numpy as np
import concourse.bass as bass
import concourse.mybir as mybir
from concourse.bass_utils import run_bass_kernel_spmd

F32 = mybir.dt.float32
BF16 = mybir.dt.bfloat16
ALU = mybir.AluOpType
AF = mybir.ActivationFunctionType

D = 1024
HD = 64
GRID_W = 64
EPS = 1e-6
NCOL = 18 * 64
C_AQ, C_BQ, C_AK, C_BK, C_AQS, C_AG, C_AKS, C_BG, C_AV, C_BV, C_CQ, C_DQ, C_CK, C_DK, C_CG, C_DG, C_CV, C_DV = range(18)

B_OMAX, B_NT = 11, 23
C_OMAX, C_NT = 5, 11
D_OMAX, D_NT = 5, 11
D_SPEC = 12
TB_B = B_NT * 128
TB_C = C_NT * 128
TB_D = D_NT * 128 + D_SPEC * 512


def _t5_bucket_np(rel):
    rel = np.asarray(rel, np.int64)
    half, max_exact = 16, 8
    dist = np.abs(rel)
    lg = np.log(np.maximum(dist, 1).astype(np.float32) / np.float32(max_exact)).astype(np.float32)
    large = max_exact + (lg / np.float32(math.log(128 / max_exact)) * np.float32(half - max_exact)).astype(np.int32)
    large = np.minimum(large, half - 1)
    return np.where(rel > 0, half, 0) + np.where(dist < max_exact, dist, large)


def _b_mult(off):
    off = np.asarray(off, np.int64)
    m = np.zeros(off.shape, np.float32)
    for (w, d) in ((128, 1), (512, 4), (2048, 16)):
        m += ((off % d == 0) & (np.abs(off) <= w // 2)).astype(np.float32)
    return m


def _toep_off(omax, nt):
    p = np.arange(128)[:, None]
    c = np.arange(128)[None, :]
    return np.concatenate([(omax - m) * 128 + p - c for m in range(nt)], axis=1)


def _d_idx(tk, tq, rows):
    rk, ck = tk // GRID_W, tk % GRID_W
    rq, cq = tq // GRID_W, tq % GRID_W
    r0 = np.clip(rq - 4, 0, rows - 8)
    cs = np.clip(cq - 8, 0, GRID_W - 16)
    valid = (rk >= r0) & (rk < r0 + 8) & (ck >= cs) & (ck < cs + 16) & (tk >= 0) & (tk < rows * GRID_W)
    dr = np.clip(rk - rq + 7, 0, 14)
    dc = np.clip(ck - cq + 15, 0, 30)
    return valid, dr, dc


def _d_tables():
    rows = 64
    n = rows * GRID_W
    p = np.arange(128)[:, None]
    c128 = np.arange(128)[None, :]
    c512 = np.arange(512)[None, :]
    vs, drs, dcs = [], [], []
    jq0 = 16
    for m in range(D_NT):
        o = D_OMAX - m
        v, dr, dc = _d_idx((jq0 + o) * 128 + p, jq0 * 128 + c128, rows)
        vs.append(v); drs.append(dr); dcs.append(dc)
    for kb in range(6):
        v, dr, dc = _d_idx(kb * 128 + p, c512, rows)
        vs.append(v); drs.append(dr); dcs.append(dc)
    nb = n // 128
    for i in range(6):
        v, dr, dc = _d_idx((nb - 6 + i) * 128 + p, n - 512 + c512, rows)
        vs.append(v); drs.append(dr); dcs.append(dc)
    return (np.concatenate(vs, 1), np.concatenate(drs, 1), np.concatenate(dcs, 1))


def _rope_tables(n):
    t = np.arange(n)
    row = (t // GRID_W).astype(np.float32)
    col = (t % GRID_W).astype(np.float32)
    freqs = (np.float32(10000.0) ** (-np.arange(16, dtype=np.float32) * np.float32(2.0) / np.float32(32))).astype(np.float32)
    cos = np.zeros((64, n), np.float32)
    sin = np.zeros((64, n), np.float32)
    for d in range(64):
        pos = row if d < 32 else col
        j = d % 16
        ang = (pos * freqs[j]).astype(np.float32)
        cos[d] = np.cos(ang)
        s = np.sin(ang)
        sin[d] = -s if (d % 32) < 16 else s
    return cos, sin


def _swap_perm():
    perm = np.arange(64)
    for d in range(64):
        perm[d] = d + 16 if (d % 32) < 16 else d - 16
    return perm


class Lane:
    def __init__(self, nc, name, step, is_pe=False):
        self.sem = nc.semaphore(name).__enter__()
        self.cnt = 0
        self.step = step
        self.is_pe = is_pe
        self.name = name

    def mark(self, ins):
        self.cnt += self.step
        ins.then_inc(self.sem, self.step)
        return (self, self.cnt)


class Eng:
    def __init__(self, nc, eng, name, is_pe=False):
        self.eng = eng
        self.lane = Lane(nc, "s_" + name, 1, is_pe)
        self.waited = {}
        self.is_pe = is_pe

    def wait(self, tok):
        if tok is None:
            return
        lane, cnt = tok
        if lane is self.lane and self.is_pe:
            return
        if self.waited.get(lane.name, 0) >= cnt:
            return
        self.waited[lane.name] = cnt
        self.eng.wait_ge(lane.sem, cnt)


class T:
    __slots__ = ("w", "r", "ps")

    def __init__(self, ps=False):
        self.w = {}
        self.r = {}
        self.ps = ps


def do(E, fn, reads=(), writes=(), lane=None, embed=True):
    need = {}

    def req(tok):
        ln, cnt = tok
        if ln is E.lane and E.is_pe:
            return
        if E.waited.get(ln.name, 0) >= cnt:
            return
        if ln.name not in need or need[ln.name][1] < cnt:
            need[ln.name] = tok

    for t in reads:
        for wt in t.w.values():
            req(wt)
        if t.ps:
            for rt in t.r.values():
                if rt[0] is not E.lane:
                    req(rt)
    for t in writes:
        for rt in t.r.values():
            req(rt)
        for wt in t.w.values():
            req(wt)
    toks = list(need.values())
    emb = None
    if toks and embed and lane is None and EMBED_WAITS:
        emb = toks.pop()
    for ln, cnt in toks:
        E.waited[ln.name] = cnt
        E.eng.wait_ge(ln.sem, cnt)
    ins = fn()
    if emb is not None:
        E.waited[emb[0].name] = emb[1]
        ins.wait_op(emb[0].sem, emb[1], "sem-ge")
    tok = (lane or E.lane).mark(ins)
    for t in reads:
        t.r[tok[0].name] = tok
    for t in writes:
        t.w[tok[0].name] = tok
        t.r = {}
    return tok


EMBED_WAITS = True


def build_program(NP, NS, L):
    nc = bass.Bass("TRN2", target_bir_lowering=False)
    NMAX = max(NP, NS)
    seqs = [("p", NP), ("s", NS)]

    def dram(name, shape, dt, kind):
        return nc.dram_tensor(name, list(shape), dt, kind=kind)

    x_in = {"p": dram("xp", [NP, D], F32, "ExternalInput"), "s": dram("xs", [NS, D], F32, "ExternalInput")}
    y_out = {"p": dram("yp", [NP, D], F32, "ExternalOutput"), "s": dram("ys", [NS, D], F32, "ExternalOutput")}
    win = dram("win", [L * 4, D, NCOL], F32, "ExternalInput")
    wout = dram("wout", [L, D, D], F32, "ExternalInput")
    gbc = dram("gbc", [L + 1, 128, D], F32, "ExternalInput")
    again = dram("again", [L, 64, 4], F32, "ExternalInput")
    clam = dram("clam", [L, 64, 4 * 32], F32, "ExternalInput")
    csub = dram("csub", [L, 64, 1], F32, "ExternalInput")
    cfar = dram("cfar", [4, 128, 2], F32, "ExternalInput")
    tabB = dram("tabB", [4, 128, TB_B], F32, "ExternalInput")
    mulB = dram("mulB", [128, TB_B], F32, "ExternalInput")
    tabC = dram("tabC", [4, 128, TB_C], F32, "ExternalInput")
    tabD = dram("tabD", [L * 4, 128, TB_D], F32, "ExternalInput")
    mskD = dram("mskD", [128, TB_D], F32, "ExternalInput")
    ropec = dram("ropec", [64, NMAX], F32, "ExternalInput")
    ropes = dram("ropes", [64, NMAX], F32, "ExternalInput")
    ident_d = dram("ident", [128, 128], F32, "ExternalInput")

    xnT = {k: dram("xnT_" + k, [8, 128, n], BF16, "Internal") for k, n in seqs}
    x1 = {k: dram("x1_" + k, [n, D], F32, "Internal") for k, n in seqs}
    mixT = {k: dram("mixT_" + k, [D, n], BF16, "Internal") for k, n in seqs}
    gT = {k: dram("gT_" + k, [4, 64, n], F32, "Internal") for k, n in seqs}

    def sb(name, shape, dt):
        return nc.sbuf_tensor(name, list(shape), dt).__enter__()

    NBMAX = NMAX // 128
    QT = sb("QT", [128, NMAX], BF16)
    KT = sb("KT", [128, NMAX], BF16)
    KT2 = sb("KT2", [64, NMAX], BF16)
    VA = sb("VA", [128, NBMAX, 2, 65], BF16)
    WB = sb("WB", [128, TB_B], BF16)
    WC = sb("WC", [128, 2, TB_C], BF16)
    WD = sb("WD", [128, TB_D], BF16)
    wbuf = sb("wbuf", [128, 8, NCOL], BF16)
    wstage = [sb("wstage%d" % i, [128, 1024], F32) for i in range(2)]
    xt = [sb("xt%d" % i, [128, 8, 512], BF16) for i in range(2)]
    PT = [sb("PT%d" % i, [128, 1024], BF16) for i in range(4)]
    ftmp = [sb("ftmp%d" % i, [128, 512], F32) for i in range(10)]
    gtile = [sb("gtile%d" % i, [64, 512], F32) for i in range(2)]
    ropt = [[sb("rc%d" % i, [64, 512], F32), sb("rs%d" % i, [64, 512], F32)] for i in range(2)]
    mxt = [sb("mxt%d" % i, [64, 512], BF16) for i in range(2)]
    xrow = [sb("xrow%d" % i, [128, D], F32) for i in range(2)]
    xnrow = [sb("xnrow%d" % i, [128, D], F32) for i in range(2)]
    xnTs = [sb("xnTs%d" % i, [128, 8, 128], BF16) for i in range(2)]
    mts = [sb("mts%d" % i, [128, 8, 128], BF16) for i in range(2)]
    gsb = sb("gsb", [128, D], F32)
    ident = sb("identsb", [128, 128], F32)
    ones64 = sb("ones64", [128, 128], F32)
    sel65 = sb("sel65", [128, 128], F32)
    small = sb("small", [128, 32], F32)
    nsm = [sb("nsm%d" % i, [128, 4], F32) for i in range(2)]
    lamt = sb("lamt", [64, 4 * 32], F32)
    cfar_sb = sb("cfar_sb", [128, 8], F32)
    ps = nc.psum_tensor("ps", [128, 8, 512], F32).__enter__()

    PE = Eng(nc, nc.tensor, "pe", True)
    ACT = Eng(nc, nc.scalar, "act")
    DVE = Eng(nc, nc.vector, "dve")
    POOL = Eng(nc, nc.gpsimd, "pool")
    SP = Eng(nc, nc.sync, "sp")
    dl = {}

    def lane_for(key):
        if key not in dl:
            dl[key] = Lane(nc, "d_" + str(key).replace(" ", "").replace("'", "").replace("(", "").replace(")", "").replace(",", "_"), 16)
        return dl[key]

    NT5 = NMAX // 512
    tQT = [T() for _ in range(NT5)]
    tKT = [T() for _ in range(NT5)]
    tKT2 = [T() for _ in range(NT5)]
    tVA = [T() for _ in range(NT5)]
    tWB, tWC, tWD, tw = T(), T(), T(), T()
    twst = [T(), T()]
    txt = [T(), T()]
    tPT = [[T(), T()] for _ in PT]
    tf = [T() for _ in ftmp]
    tg = [T(), T()]
    trp = [T(), T()]
    tmx = [T(), T()]
    txrow = [T(), T()]
    txn = [T(), T()]
    txnT = [T(), T()]
    tmts = [T(), T()]
    tnsm = [T(), T()]
    tgsb, tident, tones, tsel, tsmall, tlam, tcfar = T(), T(), T(), T(), T(), T(), T()
    tps = [T(ps=True) for _ in range(8)]
    tD = {nm: {k: T() for k, _ in seqs} for nm in ("xnT", "x1", "mixT", "gT")}
    tDx1 = {(k, l_): T() for k, _ in seqs for l_ in range(L)}
    fidx = {id(t_): i_ for i_, t_ in enumerate(tf)}

    def dma(E, lane, out, in_, reads=(), writes=()):
        return do(E, lambda: E.eng.dma_start(out=out, in_=in_), reads, writes, lane=lane_for(lane))

    dma(SP, "ident", ident[:, :], ident_d[:, :], writes=[tident])
    do(DVE, lambda: nc.vector.memset(ones64[:, :], 0.0), writes=[tones])
    do(DVE, lambda: nc.vector.memset(ones64[0:64, :], 1.0 / 64.0), writes=[tones])
    do(DVE, lambda: nc.vector.memset(sel65[:, :], 0.0), writes=[tsel])
    do(DVE, lambda: nc.vector.memset(sel65[64:65, :], 1.0), writes=[tsel])
    for i_ in range(len(ftmp)):
        do(DVE, lambda i_=i_: nc.vector.memset(ftmp[i_][:, :], 0.0), writes=[tf[i_]])
    do(DVE, lambda: nc.vector.memset(VA[:, :, :, :], 1.0), writes=tVA)
    do(DVE, lambda: nc.vector.memset(KT2[:, :], 0.0), writes=tKT2)

    ftmp_rr = [0]

    def ft():
        i = ftmp_rr[0] % len(ftmp)
        ftmp_rr[0] += 1
        return ftmp[i], tf[i]

    rr = {}

    def nxt(key, n):
        i = rr.get(key, 0) % n
        rr[key] = rr.get(key, 0) + 1
        return i

    def norm_block(xr, txr, seqk, b, final):
        i = nxt("xn", 2)
        xn, tx = xnrow[i], txn[i]
        si = nxt("nsm", 2)
        sm, tsm = nsm[si], tnsm[si]
        do(ACT, lambda: nc.scalar.activation(out=xn[:, :], in_=xr[:, :], func=AF.Square, accum_out=sm[:, 0:1]),
           reads=[txr], writes=[tx, tsm], embed=False)
        do(ACT, lambda: nc.scalar.activation(out=sm[:, 1:2], in_=sm[:, 0:1], func=AF.Ln, bias=EPS, scale=1.0 / D),
           writes=[tsm])
        do(ACT, lambda: nc.scalar.activation(out=sm[:, 2:3], in_=sm[:, 1:2], func=AF.Exp, scale=-0.5),
           writes=[tsm])
        do(DVE, lambda: nc.vector.scalar_tensor_tensor(out=xn[:, :], in0=xr[:, :], scalar=sm[:, 2:3], in1=gsb[:, :],
                                                        op0=ALU.mult, op1=ALU.mult),
           reads=[txr, tsm, tgsb], writes=[tx])
        if final:
            dma(POOL, ("st_xn", i), y_out[seqk][b * 128:(b + 1) * 128, :], xn[:, :], reads=[tx])
            return
        j = nxt("xnT", 2)
        for half in range(2):
            bank = 4 + half
            for c4 in range(4):
                c = half * 4 + c4
                do(PE, lambda c=c, c4=c4, bank=bank: nc.tensor.transpose(out=ps[:, bank, c4 * 128:(c4 + 1) * 128],
                                                                         in_=xn[:, c * 128:(c + 1) * 128], identity=ident[:, :]),
                   reads=[tx, tident], writes=[tps[bank]])
            src = ps[:, bank, :].rearrange("p (c t) -> p c t", c=4)
            dst = xnTs[j][:, half * 4:(half + 1) * 4, :]
            if half == 0:
                do(ACT, lambda src=src, dst=dst: nc.scalar.copy(out=dst, in_=src), reads=[tps[bank]], writes=[txnT[j]])
            else:
                do(DVE, lambda src=src, dst=dst: nc.vector.tensor_copy(out=dst, in_=src), reads=[tps[bank]], writes=[txnT[j]])
        dma(POOL, ("st_xnT", j), xnT[seqk][:, :, b * 128:(b + 1) * 128].rearrange("c p t -> p c t"), xnTs[j][:, :, :],
            reads=[txnT[j]], writes=[tD["xnT"][seqk]])

    def load_g(l):
        dma(SP, "gsb", gsb[:, :], gbc[l, :, :], writes=[tgsb])

    def load_cast(dst_ap_fn, src_ap_fn, ncols_total, tdst):
        per = 128
        flip = 0
        for c0 in range(0, ncols_total, per):
            w = min(per, ncols_total - c0)
            i = nxt("wst", 2)
            st = wstage[i]
            stv = st[:, 0:8 * w].rearrange("p (c n) -> p c n", c=8)
            dma(SP, ("wst", i), stv, src_ap_fn(c0, w), writes=[twst[i]])
            if flip % 2 == 0:
                do(ACT, lambda stv=stv, c0=c0, w=w: nc.scalar.copy(out=dst_ap_fn(c0, w), in_=stv), reads=[twst[i]], writes=[tdst])
            else:
                do(DVE, lambda stv=stv, c0=c0, w=w: nc.vector.tensor_copy(out=dst_ap_fn(c0, w), in_=stv), reads=[twst[i]], writes=[tdst])
            flip += 1

    def build_tables(l, hg):
        def one(Wt, tW, src, ncols, mul_src):
            for c0 in range(0, ncols, 1024):
                w = min(1024, ncols - c0)
                i = nxt("wst", 2)
                st = wstage[i]
                dma(SP, ("wst", i), st[:, 0:w], src[:, c0:c0 + w], writes=[twst[i]])
                if mul_src is None:
                    do(ACT, lambda st=st, c0=c0, w=w: nc.scalar.activation(out=Wt[:, c0:c0 + w], in_=st[:, 0:w], func=AF.Exp),
                       reads=[twst[i]], writes=[tW])
                else:
                    do(ACT, lambda st=st, w=w: nc.scalar.activation(out=st[:, 0:w], in_=st[:, 0:w], func=AF.Exp),
                       reads=[twst[i]], writes=[twst[i]])
                    i2 = nxt("wst", 2)
                    st2 = wstage[i2]
                    dma(SP, ("wst", i2), st2[:, 0:w], mul_src[:, c0:c0 + w], writes=[twst[i2]])
                    do(DVE, lambda st=st, st2=st2, c0=c0, w=w: nc.vector.tensor_tensor(out=Wt[:, c0:c0 + w], in0=st[:, 0:w],
                                                                                        in1=st2[:, 0:w], op=ALU.mult),
                       reads=[twst[i], twst[i2]], writes=[tW])
        one(WB, tWB, tabB[hg], TB_B, mulB)
        one(WD, tWD, tabD[l * 4 + hg], TB_D, mskD)
        dma(SP, "cfar", cfar_sb[:, 0:2], cfar[hg, :, :], writes=[tcfar])
        do(DVE, lambda: nc.vector.tensor_scalar(out=cfar_sb[:, 2:4], in0=cfar_sb[:, 0:2], scalar1=-1.0, scalar2=None, op0=ALU.mult),
           reads=[tcfar], writes=[tcfar])
        do(DVE, lambda: nc.vector.tensor_tensor(out=cfar_sb[:, 4:5], in0=cfar_sb[:, 1:2], in1=cfar_sb[:, 0:1], op=ALU.subtract),
           reads=[tcfar], writes=[tcfar])
        do(DVE, lambda: nc.vector.tensor_tensor(out=cfar_sb[:, 5:6], in0=cfar_sb[:, 0:1], in1=cfar_sb[:, 1:2], op=ALU.subtract),
           reads=[tcfar], writes=[tcfar])
        for c0 in range(0, TB_C, 1024):
            w = min(1024, TB_C - c0)
            i = nxt("wst", 2)
            st = wstage[i]
            dma(SP, ("wst", i), st[:, 0:w], tabC[hg][:, c0:c0 + w], writes=[twst[i]])
            for var in range(2):
                do(ACT, lambda st=st, c0=c0, w=w, var=var: nc.scalar.activation(out=WC[:, var, c0:c0 + w], in_=st[:, 0:w], func=AF.Exp,
                                                                                bias=cfar_sb[:, 2 + var:3 + var]),
                   reads=[twst[i], tcfar], writes=[tWC])

    def layer_params(l):
        dma(SP, "small", small[0:64, 8:12], again[l, :, :], writes=[tsmall])
        dma(SP, "lamt", lamt[:, :], clam[l, :, :], writes=[tlam])
        dma(SP, "small", small[0:64, 13:14], csub[l, :, :], writes=[tsmall])
        li = 0.8 - 0.6 * math.exp(-0.3 * l)
        pr, tpr = ft()
        do(DVE, lambda: nc.vector.tensor_tensor(out=pr[0:64, 0:32], in0=lamt[:, 0:32], in1=lamt[:, 32:64], op=ALU.mult),
           reads=[tlam], writes=[tpr])
        do(DVE, lambda: nc.vector.tensor_tensor(out=pr[0:64, 32:64], in0=lamt[:, 64:96], in1=lamt[:, 96:128], op=ALU.mult),
           reads=[tlam], writes=[tpr])
        do(DVE, lambda: nc.vector.reduce_sum(out=small[0:64, 14:15], in_=pr[0:64, 0:32], axis=mybir.AxisListType.X),
           reads=[tpr], writes=[tsmall])
        do(DVE, lambda: nc.vector.reduce_sum(out=small[0:64, 15:16], in_=pr[0:64, 32:64], axis=mybir.AxisListType.X),
           reads=[tpr], writes=[tsmall])
        do(ACT, lambda: nc.scalar.activation(out=small[0:64, 16:18], in_=small[0:64, 14:16], func=AF.Exp),
           writes=[tsmall])
        do(DVE, lambda: nc.vector.scalar_tensor_tensor(out=small[0:64, 12:13], in0=small[0:64, 17:18], scalar=-li,
                                                        in1=small[0:64, 16:17], op0=ALU.add, op1=ALU.subtract),
           writes=[tsmall])
        do(DVE, lambda: nc.vector.tensor_scalar(out=small[0:64, 13:14], in0=small[0:64, 13:14], scalar1=1.0 - li, scalar2=None,
                                                op0=ALU.mult),
           writes=[tsmall])

    def load_win(l, hg):
        load_cast(lambda c0, w: wbuf[:, :, c0:c0 + w],
                  lambda c0, w: win[l * 4 + hg, :, c0:c0 + w].rearrange("(c p) n -> p c n", p=128),
                  NCOL, tw)

    def inproj(seqk, n, l, hg, pr_):
        BST = globals().get("BSTEP", 99)
        for tt in range(n // 512):
            t0 = tt * 512
            xi = nxt("xt", 2)
            x_t, tx = xt[xi], txt[xi]
            dma(SP, ("xt", xi), x_t[:, :, :], xnT[seqk][:, :, t0:t0 + 512].rearrange("c p t -> p c t"),
                reads=[tD["xnT"][seqk]], writes=[tx])
            if pr_ == 0:
                ri = nxt("rp", 2)
                dma(SP, ("rp", ri), ropt[ri][0][:, :], ropec[:, t0:t0 + 512], writes=[trp[ri]])
                dma(SP, ("rp", ri), ropt[ri][1][:, :], ropes[:, t0:t0 + 512], writes=[trp[ri]])

            def proj(bank, col0, m):
                for c in range(8):
                    do(PE, lambda c=c: nc.tensor.matmul(out=ps[0:m, bank, :], lhsT=wbuf[:, c, col0:col0 + m], rhs=x_t[:, c, :],
                                                        start=(c == 0), stop=(c == 7)),
                       reads=[tx, tw], writes=[tps[bank]])

            def a_part(bank_main, bank_sw, gcol, dst, tdst):
                BSUB = globals().get("BSUB", 99)
                sq, tsq = ft()
                do(ACT, lambda: nc.scalar.activation(out=sq[0:64, :], in_=ps[0:64, bank_main, :], func=AF.Square),
                   reads=[tps[bank_main]], writes=[tsq])
                if globals().get("VARIANT", 0) in (7, 8):
                    t9, tt9 = ft()
                    do(DVE, lambda: nc.vector.tensor_copy(out=t9[:, :], in_=ps[:, bank_main, :]),
                       reads=[tps[bank_main]] + ([tsq] if globals().get("VARIANT", 0) == 8 else []), writes=[tt9])
                if BSUB <= 1:
                    return
                do(PE, lambda: nc.tensor.matmul(out=ps[:, 7, :], lhsT=ones64[:, :], rhs=sq[:, :], start=True, stop=True),
                   reads=[tsq, tones], writes=[tps[7]])
                if BSUB <= 2:
                    return
                ln_, tln = ft()
                do(ACT, lambda: nc.scalar.activation(out=ln_[0:64, :], in_=ps[0:64, 7, :], func=AF.Ln, bias=EPS, scale=1.0),
                   reads=[tps[7]], writes=[tln])
                if BSUB <= 3:
                    return
                do(ACT, lambda: nc.scalar.activation(out=ln_[0:64, :], in_=ln_[0:64, :], func=AF.Exp, scale=-0.5),
                   writes=[tln])
                if BSUB <= 4:
                    return
                t1, tt1 = ft()
                VAR = globals().get("VARIANT", 0)
                if VAR == 0:
                    do(DVE, lambda: nc.vector.scalar_tensor_tensor(out=t1[0:64, :], in0=ps[0:64, bank_main, :],
                                                                    scalar=small[0:64, gcol:gcol + 1], in1=ropt[ri][0][:, :],
                                                                    op0=ALU.mult, op1=ALU.mult),
                       reads=[tps[bank_main], tsmall, trp[ri]], writes=[tt1])
                elif VAR == 1:
                    do(DVE, lambda: nc.vector.tensor_scalar(out=t1[0:64, :], in0=ps[0:64, bank_main, :],
                                                            scalar1=small[0:64, gcol:gcol + 1], scalar2=None, op0=ALU.mult),
                       reads=[tps[bank_main], tsmall], writes=[tt1])
                elif VAR == 2:
                    do(DVE, lambda: nc.vector.tensor_tensor(out=t1[0:64, :], in0=ps[0:64, bank_main, :], in1=ropt[ri][0][:, :], op=ALU.mult),
                       reads=[tps[bank_main], trp[ri]], writes=[tt1])
                elif VAR == 4:
                    do(DVE, lambda: nc.vector.tensor_copy(out=t1[0:64, :], in_=ps[0:64, bank_main, :]),
                       reads=[tps[bank_main]], writes=[tt1])
                elif VAR == 5:
                    do(DVE, lambda: nc.vector.tensor_copy(out=t1[:, :], in_=ps[:, bank_main, :]),
                       reads=[tps[bank_main]], writes=[tt1])
                elif VAR == 6:
                    do(DVE, lambda: nc.vector.tensor_copy(out=t1[0:64, :], in_=sq[0:64, :]),
                       reads=[tsq], writes=[tt1])
                elif VAR == 3:
                    do(DVE, lambda: nc.vector.scalar_tensor_tensor(out=t1[0:64, :], in0=ps[0:64, bank_main, :],
                                                                    scalar=2.0, in1=ropt[ri][0][:, :],
                                                                    op0=ALU.mult, op1=ALU.mult),
                       reads=[tps[bank_main], trp[ri]], writes=[tt1])
                if BSUB <= 5:
                    return
                t2, tt2 = ft()
                do(DVE, lambda: nc.vector.scalar_tensor_tensor(out=t2[0:64, :], in0=ps[0:64, bank_sw, :],
                                                                scalar=small[0:64, gcol + 1:gcol + 2], in1=ropt[ri][1][:, :],
                                                                op0=ALU.mult, op1=ALU.mult),
                   reads=[tps[bank_sw], tsmall, trp[ri]], writes=[tt2])
                do(DVE, lambda: nc.vector.tensor_tensor(out=t1[0:64, :], in0=t1[0:64, :], in1=t2[0:64, :], op=ALU.add),
                   reads=[tt2], writes=[tt1])
                if BSUB <= 6:
                    return
                do(DVE, lambda: nc.vector.tensor_tensor(out=dst[0:64, t0:t0 + 512], in0=t1[0:64, :], in1=ln_[0:64, :], op=ALU.mult),
                   reads=[tt1, tln], writes=[tdst])

            if BST <= 1:
                continue
            if pr_ == 0:
                proj(0, C_AQ * 64, 128)
                if BST <= 2:
                    continue
                proj(1, C_AQS * 64, 128)
                if BST <= 3:
                    continue
                a_part(0, 1, 8, QT, tQT[tt])
                if BST <= 4:
                    continue
                do(ACT, lambda: nc.scalar.copy(out=QT[64:128, t0:t0 + 512], in_=ps[64:128, 0, :]),
                   reads=[tps[0]], writes=[tQT[tt]])
                if BST <= 5:
                    continue
                proj(2, C_AK * 64, 128)
                proj(3, C_AKS * 64, 128)
                a_part(2, 3, 10, KT, tKT[tt])
                do(ACT, lambda: nc.scalar.copy(out=KT[64:128, t0:t0 + 512], in_=ps[64:128, 2, :]),
                   reads=[tps[2]], writes=[tKT[tt]])
                gcol0, vcol0 = C_AG, C_AV
            else:
                proj(0, C_CQ * 64, 128)
                do(DVE, lambda: nc.vector.tensor_copy(out=QT[:, t0:t0 + 512], in_=ps[:, 0, :]),
                   reads=[tps[0]], writes=[tQT[tt]])
                proj(2, C_CK * 64, 128)
                do(DVE, lambda: nc.vector.tensor_copy(out=KT[0:32, t0:t0 + 512], in_=ps[0:32, 2, :]),
                   reads=[tps[2]], writes=[tKT[tt]])
                do(DVE, lambda: nc.vector.memset(KT[32:64, t0:t0 + 512], 0.0), writes=[tKT[tt]])
                do(DVE, lambda: nc.vector.tensor_copy(out=KT2[32:64, t0:t0 + 512], in_=ps[32:64, 2, :]),
                   reads=[tps[2]], writes=[tKT2[tt]])
                do(DVE, lambda: nc.vector.tensor_copy(out=KT[64:128, t0:t0 + 512], in_=ps[64:128, 2, :]),
                   reads=[tps[2]], writes=[tKT[tt]])
                gcol0, vcol0 = C_CG, C_CV
            if BST <= 6:
                continue
            if pr_ == 0:
                for ml, gbank in ((0, 1), (1, 3)):
                    gt_, tgt = ft()
                    do(ACT, lambda gt_=gt_, gbank=gbank: nc.scalar.activation(out=gt_[64:128, :], in_=ps[64:128, gbank, :], func=AF.Silu),
                       reads=[tps[gbank]], writes=[tgt])
                    dma(POOL, ("st_f", fidx[id(tgt)]), gT[seqk][ml, :, t0:t0 + 512], gt_[64:128, :],
                        reads=[tgt], writes=[tD["gT"][seqk]])
            else:
                proj(4, gcol0 * 64, 128)
                gt_, tgt = ft()
                do(ACT, lambda gt_=gt_: nc.scalar.activation(out=gt_[:, :], in_=ps[:, 4, :], func=AF.Silu),
                   reads=[tps[4]], writes=[tgt])
                dma(POOL, ("st_f", fidx[id(tgt)]), gT[seqk][2:4, :, t0:t0 + 512].rearrange("m p t -> (m p) t"), gt_[:, :],
                    reads=[tgt], writes=[tD["gT"][seqk]])
            if BST <= 7:
                continue
            for s4 in range(4):
                bank = 5 + (s4 % 2)
                for c in range(8):
                    do(PE, lambda c=c, s4=s4, bank=bank: nc.tensor.matmul(out=ps[:, bank, 0:128], lhsT=x_t[:, c, s4 * 128:(s4 + 1) * 128],
                                                                          rhs=wbuf[:, c, vcol0 * 64:vcol0 * 64 + 128],
                                                                          start=(c == 0), stop=(c == 7)),
                       reads=[tx, tw], writes=[tps[bank]])
                blk = tt * 4 + s4
                srcv = ps[:, bank, 0:128].rearrange("p (m d) -> p m d", m=2)
                if s4 % 2 == 0:
                    do(DVE, lambda blk=blk, srcv=srcv: nc.vector.tensor_copy(out=VA[:, blk, :, 0:64], in_=srcv),
                       reads=[tps[bank]], writes=[tVA[tt]])
                else:
                    do(ACT, lambda blk=blk, srcv=srcv: nc.scalar.copy(out=VA[:, blk, :, 0:64], in_=srcv),
                       reads=[tps[bank]], writes=[tVA[tt]])

    def attention(seqk, n, l, hg, pr_):
        nb = n // 128
        nq = n // 512
        sc64 = HD ** -0.5
        sc32 = 32 ** -0.5
        runs = []
        for m in (2 * pr_, 2 * pr_ + 1):
            for j in range(nq):
                units = []
                if m == 0:
                    for kb in range(0, nb, 2):
                        units.append((kb, "plain", None, None))
                    runs.append((m, 0, j, units))
                elif m == 1:
                    for kb in range(max(0, 4 * j - 8), min(nb, 4 * j + 12), 2):
                        v = [(B_OMAX - (k - 4 * j)) * 128 for k in (kb, kb + 1)]
                        units.append((kb, "w", WB[:, v[0]:v[0] + 512], WB[:, v[1]:v[1] + 512]))
                    runs.append((m, 0, j, units))
                elif m == 2:
                    nL = len([kb for kb in range(0, nb, 2) if kb - 4 * j < -2])
                    nR = len([kb for kb in range(0, nb, 2) if kb - 4 * j > 4])
                    ref = 0 if nL >= nR else 1
                    for kb in range(0, nb, 2):
                        o = kb - 4 * j
                        if -2 <= o <= 4:
                            v = [(C_OMAX - (k - 4 * j)) * 128 for k in (kb, kb + 1)]
                            units.append((kb, "w", WC[:, ref, v[0]:v[0] + 512], WC[:, ref, v[1]:v[1] + 512]))
                        elif (o < 0) == (ref == 0):
                            units.append((kb, "plain", None, None))
                        else:
                            units.append((kb, "farL" if o < 0 else "farR", None, None))
                    runs.append((m, 0, j, units))
                    runs.append((m, 1, j, units))
                else:
                    if j == 0:
                        for i in range(0, 6, 2):
                            b0 = D_NT * 128 + i * 512
                            units.append((i, "w", WD[:, b0:b0 + 512], WD[:, b0 + 512:b0 + 1024]))
                    elif j == nq - 1:
                        for i in range(0, 6, 2):
                            b0 = D_NT * 128 + (6 + i) * 512
                            units.append((nb - 6 + i, "w", WD[:, b0:b0 + 512], WD[:, b0 + 512:b0 + 1024]))
                    else:
                        for kb in range(4 * j - 2, 4 * j + 6, 2):
                            v = [(D_OMAX - (k - 4 * j)) * 128 for k in (kb, kb + 1)]
                            units.append((kb, "w", WD[:, v[0]:v[0] + 512], WD[:, v[1]:v[1] + 512]))
                    runs.append((m, 0, j, units))

        flat = []
        for ri_, (m, mp, j, units) in enumerate(runs):
            for ui, u in enumerate(units):
                flat.append((ri_, ui, len(units), m, mp, j, u))

        run_obank = {}
        unit_pi = {}
        pending = []
        c_hold = {}

        def qk_operands(m, mp, j, kb):
            p0 = 64 * (m % 2)
            if m == 2 and mp == 1:
                return (KT2[0:64, kb * 128:(kb + 1) * 128], QT[0:64, j * 512:(j + 1) * 512], tKT2[kb // 4], tQT[j])
            return (KT[p0:p0 + 64, kb * 128:(kb + 1) * 128], QT[p0:p0 + 64, j * 512:(j + 1) * 512], tKT[kb // 4], tQT[j])

        def emit_S(idx):
            ri_, ui, nu, m, mp, j, (kb, mode, w0, w1) = flat[idx]
            sp_ = idx % 2
            for h in range(2):
                lhsT, rhs, tk, tq = qk_operands(m, mp, j, kb + h)
                bank = sp_ * 2 + h
                do(PE, lambda lhsT=lhsT, rhs=rhs, bank=bank: nc.tensor.matmul(out=ps[:, bank, :], lhsT=lhsT, rhs=rhs, start=True, stop=True),
                   reads=[tk, tq], writes=[tps[bank]])

        def finalize_A(ob):
            osb, tosb = ft()
            ohi, tohi = ft()
            i_lo, i_hi = fidx[id(tosb)], fidx[id(tohi)]
            if i_hi == i_lo + 1:
                pass
            do(DVE, lambda: nc.vector.tensor_copy(out=osb[0:65, :], in_=ps[0:65, ob, :]), reads=[tps[ob]], writes=[tosb])
            do(DVE, lambda: nc.vector.tensor_tensor(out=osb[0:65, :], in0=ps[0:65, ob + 1, :], in1=osb[0:65, :], op=ALU.add),
               reads=[tps[ob + 1]], writes=[tosb])
            return osb, tosb, ob

        def finalize_B(m, j, holders):
            outs = []
            mb = holders[-1][2]
            for (osb, tosb, ob_) in holders:
                do(PE, lambda osb=osb: nc.tensor.matmul(out=ps[:, mb, :], lhsT=sel65[:, :], rhs=osb[:, :], start=True, stop=True),
                   reads=[tosb, tsel], writes=[tps[mb]])
                rd, trd = ft()
                do(ACT, lambda rd=rd: nc.scalar.activation(out=rd[0:64, :], in_=ps[0:64, mb, :], func=AF.Ln), reads=[tps[mb]], writes=[trd])
                do(ACT, lambda rd=rd: nc.scalar.activation(out=rd[0:64, :], in_=rd[0:64, :], func=AF.Exp, scale=-1.0), writes=[trd])
                do(DVE, lambda rd=rd, osb=osb: nc.vector.tensor_tensor(out=rd[0:64, :], in0=osb[0:64, :], in1=rd[0:64, :], op=ALU.mult),
                   reads=[tosb], writes=[trd])
                outs.append((rd, trd))
            o, to = outs[0]
            gi = nxt("g", 2)
            dma(SP, ("g", gi), gtile[gi][:, :], gT[seqk][m, :, j * 512:(j + 1) * 512], reads=[tD["gT"][seqk]], writes=[tg[gi]])
            if m == 2:
                o2, to2 = outs[1]
                do(DVE, lambda: nc.vector.scalar_tensor_tensor(out=o[0:64, :], in0=o2[0:64, :], scalar=small[0:64, 12:13], in1=o[0:64, :],
                                                                op0=ALU.mult, op1=ALU.add),
                   reads=[to2, tsmall], writes=[to])
                sq, tsq = ft()
                do(ACT, lambda: nc.scalar.activation(out=sq[0:64, :], in_=o[0:64, :], func=AF.Square), reads=[to], writes=[tsq])
                obc = mb
                do(PE, lambda: nc.tensor.matmul(out=ps[:, obc, :], lhsT=ones64[:, :], rhs=sq[:, :], start=True, stop=True),
                   reads=[tsq, tones], writes=[tps[obc]])
                do(ACT, lambda: nc.scalar.activation(out=sq[0:64, :], in_=ps[0:64, obc, :], func=AF.Ln, bias=EPS, scale=1.0),
                   reads=[tps[obc]], writes=[tsq])
                do(ACT, lambda: nc.scalar.activation(out=sq[0:64, :], in_=sq[0:64, :], func=AF.Exp, scale=-0.5), writes=[tsq])
                do(DVE, lambda: nc.vector.scalar_tensor_tensor(out=o[0:64, :], in0=o[0:64, :], scalar=small[0:64, 13:14], in1=sq[0:64, :],
                                                                op0=ALU.mult, op1=ALU.mult),
                   reads=[tsq, tsmall], writes=[to])
            mi = nxt("mx", 2)
            do(DVE, lambda: nc.vector.tensor_tensor(out=mxt[mi][:, :], in0=o[0:64, :], in1=gtile[gi][:, :], op=ALU.mult),
               reads=[to, tg[gi]], writes=[tmx[mi]])
            r0 = m * 256 + hg * 64
            dma(POOL, ("st_mx", mi), mixT[seqk][r0:r0 + 64, j * 512:(j + 1) * 512], mxt[mi][:, :], reads=[tmx[mi]], writes=[tD["mixT"][seqk]])

        def emit_rest(idx):
            ri_, ui, nu, m, mp, j, (kb, mode, w0, w1) = flat[idx]
            sp_ = idx % 2
            pi = nxt("pt", len(PT))
            scale = sc32 if m == 2 else sc64
            src = ps[:, sp_ * 2:sp_ * 2 + 2, :]
            dst = PT[pi][:, :].rearrange("p (b q) -> p b q", b=2)
            if mode in ("farL", "farR"):
                bcol = cfar_sb[:, 5:6] if mode == "farL" else cfar_sb[:, 4:5]
                do(ACT, lambda: nc.scalar.activation(out=dst, in_=src, func=AF.Exp, bias=bcol, scale=scale),
                   reads=[tps[sp_ * 2], tps[sp_ * 2 + 1], tcfar], writes=tPT[pi])
            else:
                do(ACT, lambda: nc.scalar.activation(out=dst, in_=src, func=AF.Exp, scale=scale),
                   reads=[tps[sp_ * 2], tps[sp_ * 2 + 1]], writes=tPT[pi])
            if mode == "w":
                tW = {1: tWB, 2: tWC, 3: tWD}[m]
                for h, wv in ((0, w0), (1, w1)):
                    do(DVE, lambda h=h, wv=wv: nc.vector.tensor_tensor(out=PT[pi][:, h * 512:(h + 1) * 512], in0=PT[pi][:, h * 512:(h + 1) * 512],
                                                                        in1=wv, op=ALU.mult),
                       reads=[tW], writes=[tPT[pi][h]])
            unit_pi[idx] = pi

        def emit_pv(idx):
            ri_, ui, nu, m, mp, j, (kb, mode, w0, w1) = flat[idx]
            pi = unit_pi.pop(idx)
            if ui == 0:
                run_obank[ri_] = 4 + 2 * nxt("ob", 2)
            ob = run_obank[ri_]
            ml = m % 2
            for h in range(2):
                k = kb + h
                for half in range(2):
                    r0_ = 64 * half
                    do(PE, lambda h=h, k=k, half=half, r0_=r0_: nc.tensor.matmul(
                        out=ps[0:65, ob + half, :], lhsT=VA[r0_:r0_ + 64, k, ml, :], rhs=PT[pi][r0_:r0_ + 64, h * 512:(h + 1) * 512],
                        start=(ui == 0 and h == 0), stop=(ui == nu - 1 and h == 1)),
                       reads=[tPT[pi][h], tVA[k // 4]], writes=[tps[ob + half]])
            if ui == nu - 1:
                hold = finalize_A(ob)
                if m == 2:
                    c_hold.setdefault(j, []).append(hold)
                    if mp == 1:
                        hs = c_hold.pop(j)
                        pending.append((idx + 3, lambda hs=hs, m=m, j=j: finalize_B(m, j, hs)))
                else:
                    pending.append((idx + 3, lambda hold=hold, m=m, j=j: finalize_B(m, j, [hold])))

        NF = len(flat)
        emit_S(0)
        if NF > 1:
            emit_S(1)
        for idx in range(NF):
            emit_rest(idx)
            if idx + 2 < NF:
                emit_S(idx + 2)
            if idx >= 1:
                emit_pv(idx - 1)
            while pending and pending[0][0] <= idx:
                pending.pop(0)[1]()
        emit_pv(NF - 1)
        while pending:
            pending.pop(0)[1]()

    def outphase(seqk, n, l):
        last = (l == L - 1)
        load_cast(lambda c0, w: wbuf[:, :, c0:c0 + w],
                  lambda c0, w: wout[l, :, c0:c0 + w].rearrange("(c p) n -> p c n", p=128),
                  D, tw)
        load_g(l + 1)
        xsrc = x_in[seqk] if l == 0 else x1[seqk]
        for b in range(n // 128):
            mi = nxt("mts", 2)
            dma(SP, ("mts", mi), mts[mi][:, :, :], mixT[seqk][:, b * 128:(b + 1) * 128].rearrange("(c p) t -> p c t", p=128),
                reads=[tD["mixT"][seqk]], writes=[tmts[mi]])
            xi = nxt("xrow", 2)
            dma(SP, ("xrow", xi), xrow[xi][:, :], xsrc[b * 128:(b + 1) * 128, :], reads=([tDx1[(seqk, l - 1)]] if l > 0 else []), writes=[txrow[xi]])
            for half in range(2):
                bank = half + 2 * (b % 2)
                for c in range(8):
                    do(PE, lambda c=c, half=half, bank=bank: nc.tensor.matmul(out=ps[:, bank, :], lhsT=mts[mi][:, c, :],
                                                                              rhs=wbuf[:, c, half * 512:(half + 1) * 512],
                                                                              start=(c == 0), stop=(c == 7)),
                       reads=[tmts[mi], tw], writes=[tps[bank]])
                do(DVE, lambda half=half, bank=bank: nc.vector.tensor_tensor(out=xrow[xi][:, half * 512:(half + 1) * 512],
                                                                             in0=ps[:, bank, :], in1=xrow[xi][:, half * 512:(half + 1) * 512],
                                                                             op=ALU.add),
                   reads=[tps[bank]], writes=[txrow[xi]])
            if not last:
                dma(POOL, ("st_xrow", xi), x1[seqk][b * 128:(b + 1) * 128, :], xrow[xi][:, :], reads=[txrow[xi]], writes=[tDx1[(seqk, l)]])
            norm_block(xrow[xi], txrow[xi], seqk, b, final=last)

    load_g(0)
    for seqk, n in seqs:
        for b in range(n // 128):
            xi = nxt("xrow", 2)
            dma(SP, ("xrow", xi), xrow[xi][:, :], x_in[seqk][b * 128:(b + 1) * 128, :], writes=[txrow[xi]])
            norm_block(xrow[xi], txrow[xi], seqk, b, final=False)
    BIS = globals().get("BISECT", "full")
    for l in range(L):
        if BIS == "norm":
            break
        layer_params(l)
        for hg in range(4):
            build_tables(l, hg)
            if BIS == "tables":
                continue
            load_win(l, hg)
            if BIS == "loadwin":
                continue
            for seqk, n in seqs:
                for pr_ in range(2):
                    inproj(seqk, n, l, hg, pr_)
                    if BIS == "inproj":
                        continue
                    attention(seqk, n, l, hg, pr_)
        if BIS in ("tables", "inproj", "attn", "loadwin"):
            continue
        for seqk, n in seqs:
            outphase(seqk, n, l)
    engs = [PE, ACT, DVE, POOL, SP]
    for E in engs:
        for O in engs:
            if O is not E and O.lane.cnt > 0:
                E.eng.wait_ge(O.lane.sem, O.lane.cnt)
        for k, ln in dl.items():
            if ln.cnt > 0:
                E.eng.wait_ge(ln.sem, ln.cnt)
    return nc


def prepare_shared(w_in, w_out, norm_g, final_g, a_q_gain, a_k_gain, t5_bias, c_lambda_q1, c_lambda_k1,
                   c_lambda_q2, c_lambda_k2, c_subln_g, d_rpb, NMAX):
    L = w_in.shape[0]
    f = np.float32
    w_in = np.asarray(w_in, f); w_out = np.asarray(w_out, f)
    perm = _swap_perm()
    offs = np.cumsum([0, 256, 128, 128, 256] + [256] * 12)
    names = ["aq", "ak", "av", "ag", "bq", "bk", "bv", "bg", "cq", "ck", "cv", "cg", "dq", "dk", "dv", "dg"]
    o = {nm: int(offs[i]) for i, nm in enumerate(names)}
    win = np.zeros((L * 4, D, NCOL), f)
    for l in range(L):
        for hg in range(4):
            W = w_in[l]
            kvh = hg // 2
            aq = W[:, o["aq"] + hg * 64: o["aq"] + (hg + 1) * 64]
            ak = W[:, o["ak"] + kvh * 64: o["ak"] + (kvh + 1) * 64]
            av = W[:, o["av"] + kvh * 64: o["av"] + (kvh + 1) * 64]
            sl = lambda nm: W[:, o[nm] + hg * 64: o[nm] + (hg + 1) * 64]
            blocks = [None] * 18
            blocks[C_AQ], blocks[C_BQ], blocks[C_CQ], blocks[C_DQ] = aq, sl("bq"), sl("cq"), sl("dq")
            blocks[C_AK], blocks[C_BK], blocks[C_CK], blocks[C_DK] = ak, sl("bk"), sl("ck"), sl("dk")
            blocks[C_AQS], blocks[C_AKS] = aq[:, perm], ak[:, perm]
            blocks[C_AG], blocks[C_BG], blocks[C_CG], blocks[C_DG] = sl("ag"), sl("bg"), sl("cg"), sl("dg")
            blocks[C_AV], blocks[C_BV], blocks[C_CV], blocks[C_DV] = av, sl("bv"), sl("cv"), sl("dv")
            win[l * 4 + hg] = np.concatenate(blocks, axis=1)
    gb = np.concatenate([np.asarray(norm_g, f), np.asarray(final_g, f)[None]], 0)
    gbc = np.ascontiguousarray(np.broadcast_to(gb[:, None, :], (L + 1, 128, D)))
    aqg = np.asarray(a_q_gain, f); akg = np.asarray(a_k_gain, f)
    again = np.stack([aqg, aqg[:, perm], akg, akg[:, perm]], axis=2)
    lam = np.concatenate([np.asarray(c_lambda_q1, f), np.asarray(c_lambda_k1, f),
                          np.asarray(c_lambda_q2, f), np.asarray(c_lambda_k2, f)], axis=1)
    clam = np.ascontiguousarray(np.broadcast_to(lam[:, None, :], (L, 64, 128)))
    csub = np.asarray(c_subln_g, f)[:, :, None].copy()
    t5 = np.asarray(t5_bias, f)
    cfar = np.zeros((4, 128, 2), f)
    for h in range(4):
        cfar[h, :, 0] = t5[15, 4 + h]
        cfar[h, :, 1] = t5[31, 4 + h]
    offB = _toep_off(B_OMAX, B_NT)
    bkB = _t5_bucket_np(offB)
    tabB = np.stack([t5[bkB, h] for h in range(4)], 0).astype(f)
    mulB = _b_mult(offB).astype(f)
    offC = _toep_off(C_OMAX, C_NT)
    bkC = _t5_bucket_np(offC)
    tabC = np.stack([t5[bkC, 4 + h] for h in range(4)], 0).astype(f)
    vD, drD, dcD = _d_tables()
    rpb = np.asarray(d_rpb, f)
    tabD = np.stack([rpb[l, h][drD, dcD] for l in range(L) for h in range(4)], 0).astype(f)
    mskD = vD.astype(f)
    cos, sin = _rope_tables(NMAX)
    return dict(win=win, wout=w_out, gbc=gbc, again=np.ascontiguousarray(again), clam=clam, csub=csub, cfar=cfar,
                tabB=np.ascontiguousarray(tabB), mulB=np.ascontiguousarray(mulB), tabC=np.ascontiguousarray(tabC),
                tabD=np.ascontiguousarray(tabD), mskD=np.ascontiguousarray(mskD), ropec=cos, ropes=sin,
                ident=np.eye(128, dtype=f))


_NC_CACHE = {}


def run(x_prompt, x_sample, **params):
    x_prompt = np.asarray(x_prompt, np.float32)
    x_sample = np.asarray(x_sample, np.float32)
    BP, NP, _ = x_prompt.shape
    BS, NS, _ = x_sample.shape
    L = np.asarray(params["w_in"]).shape[0]
    shared = prepare_shared(NMAX=max(NP, NS), **params)
    key = (NP, NS, L)
    if key not in _NC_CACHE:
        _NC_CACHE[key] = build_program(NP, NS, L)
    nc = _NC_CACHE[key]
    ncores = 8
    in_maps = []
    for c in range(ncores):
        m = dict(shared)
        m["xp"] = np.ascontiguousarray(x_prompt[(c * BP) // ncores])
        m["xs"] = np.ascontiguousarray(x_sample[c % BS])
        in_maps.append(m)
    res = run_bass_kernel_spmd(nc, in_maps, core_ids=list(range(ncores)))
    yp = np.stack([np.asarray(res.results[(b * ncores) // BP]["yp"], np.float32) for b in range(BP)], 0)
    ys = np.stack([np.asarray(res.results[c]["ys"], np.float32) for c in range(BS)], 0)
    return yp, ys


def kernel(x_prompt, x_sample, w_in, w_out, norm_g, final_g, a_q_gain, a_k_gain, t5_bias,
           c_lambda_q1, c_lambda_k1, c_lambda_q2, c_lambda_k2, c_subln_g, d_rpb):
    return run(x_prompt, x_sample, w_in=w_in, w_out=w_out, norm_g=norm_g, final_g=final_g, a_q_gain=a_q_gain,
               a_k_gain=a_k_gain, t5_bias=t5_bias, c_lambda_q1=c_lambda_q1, c_lambda_k1=c_lambda_k1,
               c_lambda_q2=c_lambda_q2, c_lambda_k2=c_lambda_k2, c_subln_g=c_subln_g, d_rpb=d_rpb)
```

```python
import math
import numpy as np
import concourse.bass as bass
import concourse.mybir as mybir
from concourse.bass_utils import run_bass_kernel_spmd

F32 = mybir.dt.float32
BF16 = mybir.dt.bfloat16
ALU = mybir.AluOpType
AF = mybir.ActivationFunctionType

D = 1024
HD = 64
GRID_W = 64
EPS = 1e-6
NCOL = 18 * 64
C_AQ, C_BQ, C_AK, C_BK, C_AQS, C_AKS, C_AG, C_BG, C_AV, C_BV, C_CQ, C_DQ, C_CK, C_DK, C_CG, C_DG, C_CV, C_DV = range(18)

B_OMAX, B_NT = 11, 23
C_OMAX, C_NT = 5, 11
D_OMAX, D_NT = 5, 11
D_SPEC = 12
TB_B = B_NT * 128
TB_C = C_NT * 128
TB_D = D_NT * 128 + D_SPEC * 512


def _t5_bucket_np(rel):
    rel = np.asarray(rel, np.int64)
    half, max_exact = 16, 8
    dist = np.abs(rel)
    lg = np.log(np.maximum(dist, 1).astype(np.float32) / np.float32(max_exact)).astype(np.float32)
    large = max_exact + (lg / np.float32(math.log(128 / max_exact)) * np.float32(half - max_exact)).astype(np.int32)
    large = np.minimum(large, half - 1)
    return np.where(rel > 0, half, 0) + np.where(dist < max_exact, dist, large)


def _b_mult(off):
    off = np.asarray(off, np.int64)
    m = np.zeros(off.shape, np.float32)
    for (w, d) in ((128, 1), (512, 4), (2048, 16)):
        m += ((off % d == 0) & (np.abs(off) <= w // 2)).astype(np.float32)
    return m


def _toep_off(omax, nt):
    p = np.arange(128)[:, None]
    c = np.arange(128)[None, :]
    return np.concatenate([(omax - m) * 128 + p - c for m in range(nt)], axis=1)


def _d_idx(tk, tq, rows):
    rk, ck = tk // GRID_W, tk % GRID_W
    rq, cq = tq // GRID_W, tq % GRID_W
    r0 = np.clip(rq - 4, 0, rows - 8)
    cs = np.clip(cq - 8, 0, GRID_W - 16)
    valid = (rk >= r0) & (rk < r0 + 8) & (ck >= cs) & (ck < cs + 16) & (tk >= 0) & (tk < rows * GRID_W)
    dr = np.clip(rk - rq + 7, 0, 14)
    dc = np.clip(ck - cq + 15, 0, 30)
    return valid, dr, dc


def _d_tables():
    rows = 64
    n = rows * GRID_W
    p = np.arange(128)[:, None]
    c128 = np.arange(128)[None, :]
    c512 = np.arange(512)[None, :]
    vs, drs, dcs = [], [], []
    jq0 = 16
    for m in range(D_NT):
        o = D_OMAX - m
        v, dr, dc = _d_idx((jq0 + o) * 128 + p, jq0 * 128 + c128, rows)
        vs.append(v); drs.append(dr); dcs.append(dc)
    for kb in range(6):
        v, dr, dc = _d_idx(kb * 128 + p, c512, rows)
        vs.append(v); drs.append(dr); dcs.append(dc)
    nb = n // 128
    for i in range(6):
        v, dr, dc = _d_idx((nb - 6 + i) * 128 + p, n - 512 + c512, rows)
        vs.append(v); drs.append(dr); dcs.append(dc)
    return (np.concatenate(vs, 1), np.concatenate(drs, 1), np.concatenate(dcs, 1))


def _rope_tables(n):
    t = np.arange(n)
    row = (t // GRID_W).astype(np.float32)
    col = (t % GRID_W).astype(np.float32)
    freqs = (np.float32(10000.0) ** (-np.arange(16, dtype=np.float32) * np.float32(2.0) / np.float32(32))).astype(np.float32)
    cos = np.zeros((64, n), np.float32)
    sin = np.zeros((64, n), np.float32)
    for d in range(64):
        pos = row if d < 32 else col
        j = d % 16
        ang = (pos * freqs[j]).astype(np.float32)
        cos[d] = np.cos(ang)
        s = np.sin(ang)
        sin[d] = -s if (d % 32) < 16 else s
    return cos, sin


def _swap_perm():
    perm = np.arange(64)
    for d in range(64):
        perm[d] = d + 16 if (d % 32) < 16 else d - 16
    return perm


class Lane:
    def __init__(self, nc, name, step, is_pe=False):
        self.sem = nc.semaphore(name).__enter__()
        self.cnt = 0
        self.step = step
        self.is_pe = is_pe
        self.name = name

    def mark(self, ins):
        self.cnt += self.step
        ins.then_inc(self.sem, self.step)
        return (self, self.cnt)


class Eng:
    def __init__(self, nc, eng, name, is_pe=False):
        self.eng = eng
        self.lane = Lane(nc, "s_" + name, 1, is_pe)
        self.waited = {}
        self.is_pe = is_pe

    def wait(self, tok):
        if tok is None:
            return
        lane, cnt = tok
        if lane is self.lane and self.is_pe:
            return
        if self.waited.get(lane.name, 0) >= cnt:
            return
        self.waited[lane.name] = cnt
        self.eng.wait_ge(lane.sem, cnt)


class T:
    __slots__ = ("w", "r", "ps")

    def __init__(self, ps=False):
        self.w = {}
        self.r = {}
        self.ps = ps


def do(E, fn, reads=(), writes=(), lane=None, embed=True):
    need = {}

    def req(tok):
        ln, cnt = tok
        if ln is E.lane and E.is_pe:
            return
        if E.waited.get(ln.name, 0) >= cnt:
            return
        if ln is E.lane and cnt <= ln.cnt - OWN_LANE_SKIP:
            return
        if ln.name not in need or need[ln.name][1] < cnt:
            need[ln.name] = tok

    for t in reads:
        for wt in t.w.values():
            req(wt)
        if t.ps:
            for rt in t.r.values():
                if rt[0] is not E.lane:
                    req(rt)
    for t in writes:
        for rt in t.r.values():
            req(rt)
        for wt in t.w.values():
            req(wt)
    toks = list(need.values())
    emb = None
    if toks and embed and lane is None and EMBED_WAITS:
        emb = toks.pop()
    for ln, cnt in toks:
        E.waited[ln.name] = cnt
        E.eng.wait_ge(ln.sem, cnt)
    ins = fn()
    if emb is not None:
        E.waited[emb[0].name] = emb[1]
        ins.wait_op(emb[0].sem, emb[1], "sem-ge")
    tok = (lane or E.lane).mark(ins)
    for t in reads:
        t.r[tok[0].name] = tok
    for t in writes:
        t.w[tok[0].name] = tok
        t.r = {}
    return tok


EMBED_WAITS = True
OWN_LANE_SKIP = 4


def build_program(NP, NS, L):
    nc = bass.Bass("TRN2", target_bir_lowering=False)
    NMAX = max(NP, NS)
    seqs = [("p", NP), ("s", NS)]

    def dram(name, shape, dt, kind):
        return nc.dram_tensor(name, list(shape), dt, kind=kind)

    x_in = {"p": dram("xp", [NP, D], F32, "ExternalInput"), "s": dram("xs", [NS, D], F32, "ExternalInput")}
    y_out = {"p": dram("yp", [NP, D], F32, "ExternalOutput"), "s": dram("ys", [NS, D], F32, "ExternalOutput")}
    win = dram("win", [L * 4, D, NCOL], F32, "ExternalInput")
    wout = dram("wout", [L, D, D], F32, "ExternalInput")
    gbc = dram("gbc", [L + 1, 128, D], F32, "ExternalInput")
    again = dram("again", [L, 64, 4], F32, "ExternalInput")
    clam = dram("clam", [L, 64, 4 * 32], F32, "ExternalInput")
    csub = dram("csub", [L, 64, 1], F32, "ExternalInput")
    cfar = dram("cfar", [4, 128, 2], F32, "ExternalInput")
    tabB = dram("tabB", [4, 128, TB_B], F32, "ExternalInput")
    mulB = dram("mulB", [128, TB_B], F32, "ExternalInput")
    tabC = dram("tabC", [4, 128, TB_C], F32, "ExternalInput")
    tabD = dram("tabD", [L * 4, 128, TB_D], F32, "ExternalInput")
    mskD = dram("mskD", [128, TB_D], F32, "ExternalInput")
    ropec = dram("ropec", [64, NMAX], F32, "ExternalInput")
    ropes = dram("ropes", [64, NMAX], F32, "ExternalInput")
    ident_d = dram("ident", [128, 128], F32, "ExternalInput")

    xnT = {k: dram("xnT_" + k, [8, 128, n], BF16, "Internal") for k, n in seqs}
    x1 = {k: dram("x1_" + k, [n, D], F32, "Internal") for k, n in seqs}
    mixT = {k: dram("mixT_" + k, [D, n], BF16, "Internal") for k, n in seqs}
    gT = {k: dram("gT_" + k, [4, 64, n], F32, "Internal") for k, n in seqs}

    def sb(name, shape, dt):
        return nc.sbuf_tensor(name, list(shape), dt).__enter__()

    NBMAX = NMAX // 128
    QT = sb("QT", [128, NMAX], BF16)
    KT = sb("KT", [128, NMAX], BF16)
    KT2 = sb("KT2", [64, NMAX], BF16)
    VA = sb("VA", [128, NBMAX, 2, 65], BF16)
    WB = sb("WB", [128, TB_B], BF16)
    WC = sb("WC", [128, 2, TB_C], BF16)
    WD = sb("WD", [128, TB_D], BF16)
    wbuf = sb("wbuf", [128, 8, NCOL], BF16)
    wstage = [sb("wstage%d" % i, [128, 1024], F32) for i in range(2)]
    xt = [sb("xt%d" % i, [128, 8, 512], BF16) for i in range(2)]
    PT = [sb("PT%d" % i, [128, 1024], BF16) for i in range(4)]
    ftmp = [sb("ftmp%d" % i, [128, 512], F32) for i in range(10)]
    gtile = [sb("gtile%d" % i, [64, 512], F32) for i in range(2)]
    ropt = [[sb("rc%d" % i, [64, 512], F32), sb("rs%d" % i, [64, 512], F32)] for i in range(2)]
    mxt = [sb("mxt%d" % i, [64, 512], BF16) for i in range(2)]
    xrow = [sb("xrow%d" % i, [128, D], F32) for i in range(2)]
    xnrow = [sb("xnrow%d" % i, [128, D], F32) for i in range(2)]
    xnTs = [sb("xnTs%d" % i, [128, 8, 128], BF16) for i in range(2)]
    mts = [sb("mts%d" % i, [128, 8, 128], BF16) for i in range(2)]
    gsb = sb("gsb", [128, D], F32)
    ident = sb("identsb", [128, 128], F32)
    ones64 = sb("ones64", [128, 128], F32)
    sel65 = sb("sel65", [128, 128], F32)
    small = sb("small", [128, 32], F32)
    nsm = [sb("nsm%d" % i, [128, 4], F32) for i in range(2)]
    lamt = sb("lamt", [64, 4 * 32], F32)
    cfar_sb = sb("cfar_sb", [128, 8], F32)
    ps = nc.psum_tensor("ps", [128, 8, 512], F32).__enter__()

    PE = Eng(nc, nc.tensor, "pe", True)
    ACT = Eng(nc, nc.scalar, "act")
    DVE = Eng(nc, nc.vector, "dve")
    POOL = Eng(nc, nc.gpsimd, "pool")
    SP = Eng(nc, nc.sync, "sp")
    dl = {}

    def lane_for(key):
        if key not in dl:
            dl[key] = Lane(nc, "d_" + str(key).replace(" ", "").replace("'", "").replace("(", "").replace(")", "").replace(",", "_"), 16)
        return dl[key]

    NT5 = NMAX // 512
    tQT = [T() for _ in range(NT5)]
    tKT = [T() for _ in range(NT5)]
    tKT2 = [T() for _ in range(NT5)]
    tVA = [T() for _ in range(NT5)]
    tWB, tWC, tWD, tw = T(), T(), T(), T()
    twst = [T(), T()]
    txt = [T(), T()]
    tPT = [[T(), T()] for _ in PT]
    tf = [T() for _ in ftmp]
    tg = [T(), T()]
    trp = [T(), T()]
    tmx = [T(), T()]
    txrow = [T(), T()]
    txn = [T(), T()]
    txnT = [T(), T()]
    tmts = [T(), T()]
    tnsm = [T(), T()]
    tgsb, tident, tones, tsel, tsmall, tlam, tcfar = T(), T(), T(), T(), T(), T(), T()
    tps = [T(ps=True) for _ in range(8)]
    tD = {nm: {k: T() for k, _ in seqs} for nm in ("xnT", "x1", "mixT", "gT")}
    tDx1 = {(k, l_): T() for k, _ in seqs for l_ in range(L)}
    fidx = {id(t_): i_ for i_, t_ in enumerate(tf)}

    def dma(E, lane, out, in_, reads=(), writes=()):
        return do(E, lambda: E.eng.dma_start(out=out, in_=in_), reads, writes, lane=lane_for(lane))

    dma(SP, "ident", ident[:, :], ident_d[:, :], writes=[tident])
    do(DVE, lambda: nc.vector.memset(ones64[:, :], 0.0), writes=[tones])
    do(DVE, lambda: nc.vector.memset(ones64[0:64, :], 1.0 / 64.0), writes=[tones])
    do(DVE, lambda: nc.vector.memset(sel65[:, :], 0.0), writes=[tsel])
    do(DVE, lambda: nc.vector.memset(sel65[64:65, :], 1.0), writes=[tsel])
    for i_ in range(len(ftmp)):
        do(DVE, lambda i_=i_: nc.vector.memset(ftmp[i_][:, :], 0.0), writes=[tf[i_]])
    do(DVE, lambda: nc.vector.memset(VA[:, :, :, :], 1.0), writes=tVA)
    do(DVE, lambda: nc.vector.memset(KT2[:, :], 0.0), writes=tKT2)

    ftmp_rr = [0]

    def ft():
        i = ftmp_rr[0] % len(ftmp)
        ftmp_rr[0] += 1
        return ftmp[i], tf[i]

    rr = {}

    def nxt(key, n):
        i = rr.get(key, 0) % n
        rr[key] = rr.get(key, 0) + 1
        return i

    def norm_block(xr, txr, seqk, b, final):
        i = nxt("xn", 2)
        xn, tx = xnrow[i], txn[i]
        si = nxt("nsm", 2)
        sm, tsm = nsm[si], tnsm[si]
        do(ACT, lambda: nc.scalar.activation(out=xn[:, :], in_=xr[:, :], func=AF.Square, accum_out=sm[:, 0:1]),
           reads=[txr], writes=[tx, tsm], embed=False)
        do(ACT, lambda: nc.scalar.activation(out=sm[:, 1:2], in_=sm[:, 0:1], func=AF.Ln, bias=EPS, scale=1.0 / D),
           writes=[tsm])
        do(ACT, lambda: nc.scalar.activation(out=sm[:, 2:3], in_=sm[:, 1:2], func=AF.Exp, scale=-0.5),
           writes=[tsm])
        do(DVE, lambda: nc.vector.scalar_tensor_tensor(out=xn[:, :], in0=xr[:, :], scalar=sm[:, 2:3], in1=gsb[:, :],
                                                        op0=ALU.mult, op1=ALU.mult),
           reads=[txr, tsm, tgsb], writes=[tx])
        if final:
            dma(POOL, ("st_xn", i), y_out[seqk][b * 128:(b + 1) * 128, :], xn[:, :], reads=[tx])
            return
        j = nxt("xnT", 2)
        for half in range(2):
            bank = 4 + half
            for c4 in range(4):
                c = half * 4 + c4
                do(PE, lambda c=c, c4=c4, bank=bank: nc.tensor.transpose(out=ps[:, bank, c4 * 128:(c4 + 1) * 128],
                                                                         in_=xn[:, c * 128:(c + 1) * 128], identity=ident[:, :]),
                   reads=[tx, tident], writes=[tps[bank]])
            src = ps[:, bank, :].rearrange("p (c t) -> p c t", c=4)
            dst = xnTs[j][:, half * 4:(half + 1) * 4, :]
            if half == 0:
                do(ACT, lambda src=src, dst=dst: nc.scalar.copy(out=dst, in_=src), reads=[tps[bank]], writes=[txnT[j]])
            else:
                do(DVE, lambda src=src, dst=dst: nc.vector.tensor_copy(out=dst, in_=src), reads=[tps[bank]], writes=[txnT[j]])
        dma(POOL, ("st_xnT", j), xnT[seqk][:, :, b * 128:(b + 1) * 128].rearrange("c p t -> p c t"), xnTs[j][:, :, :],
            reads=[txnT[j]], writes=[tD["xnT"][seqk]])

    def load_g(l):
        dma(SP, "gsb", gsb[:, :], gbc[l, :, :], writes=[tgsb])

    def load_cast(dst_ap_fn, src_ap_fn, ncols_total, tdst):
        per = 128
        flip = 0
        for c0 in range(0, ncols_total, per):
            w = min(per, ncols_total - c0)
            i = nxt("wst", 2)
            st = wstage[i]
            stv = st[:, 0:8 * w].rearrange("p (c n) -> p c n", c=8)
            dma(SP, ("wst", i), stv, src_ap_fn(c0, w), writes=[twst[i]])
            if flip % 2 == 0:
                do(ACT, lambda stv=stv, c0=c0, w=w: nc.scalar.copy(out=dst_ap_fn(c0, w), in_=stv), reads=[twst[i]], writes=[tdst])
            else:
                do(DVE, lambda stv=stv, c0=c0, w=w: nc.vector.tensor_copy(out=dst_ap_fn(c0, w), in_=stv), reads=[twst[i]], writes=[tdst])
            flip += 1

    def build_tables(l, hg):
        def one(Wt, tW, src, ncols, mul_src):
            for c0 in range(0, ncols, 1024):
                w = min(1024, ncols - c0)
                i = nxt("wst", 2)
                st = wstage[i]
                dma(SP, ("wst", i), st[:, 0:w], src[:, c0:c0 + w], writes=[twst[i]])
                if mul_src is None:
                    do(ACT, lambda st=st, c0=c0, w=w: nc.scalar.activation(out=Wt[:, c0:c0 + w], in_=st[:, 0:w], func=AF.Exp),
                       reads=[twst[i]], writes=[tW])
                else:
                    do(ACT, lambda st=st, w=w: nc.scalar.activation(out=st[:, 0:w], in_=st[:, 0:w], func=AF.Exp),
                       reads=[twst[i]], writes=[twst[i]])
                    i2 = nxt("wst", 2)
                    st2 = wstage[i2]
                    dma(SP, ("wst", i2), st2[:, 0:w], mul_src[:, c0:c0 + w], writes=[twst[i2]])
                    do(DVE, lambda st=st, st2=st2, c0=c0, w=w: nc.vector.tensor_tensor(out=Wt[:, c0:c0 + w], in0=st[:, 0:w],
                                                                                        in1=st2[:, 0:w], op=ALU.mult),
                       reads=[twst[i], twst[i2]], writes=[tW])
        one(WB, tWB, tabB[hg], TB_B, mulB)
        one(WD, tWD, tabD[l * 4 + hg], TB_D, mskD)
        dma(SP, "cfar", cfar_sb[:, 0:2], cfar[hg, :, :], writes=[tcfar])
        do(DVE, lambda: nc.vector.tensor_scalar(out=cfar_sb[:, 2:4], in0=cfar_sb[:, 0:2], scalar1=-1.0, scalar2=None, op0=ALU.mult),
           reads=[tcfar], writes=[tcfar])
        do(DVE, lambda: nc.vector.tensor_tensor(out=cfar_sb[:, 4:5], in0=cfar_sb[:, 1:2], in1=cfar_sb[:, 0:1], op=ALU.subtract),
           reads=[tcfar], writes=[tcfar])
        do(DVE, lambda: nc.vector.tensor_tensor(out=cfar_sb[:, 5:6], in0=cfar_sb[:, 0:1], in1=cfar_sb[:, 1:2], op=ALU.subtract),
           reads=[tcfar], writes=[tcfar])
        for c0 in range(0, TB_C, 1024):
            w = min(1024, TB_C - c0)
            i = nxt("wst", 2)
            st = wstage[i]
            dma(SP, ("wst", i), st[:, 0:w], tabC[hg][:, c0:c0 + w], writes=[twst[i]])
            for var in range(2):
                do(ACT, lambda st=st, c0=c0, w=w, var=var: nc.scalar.activation(out=WC[:, var, c0:c0 + w], in_=st[:, 0:w], func=AF.Exp,
                                                                                bias=cfar_sb[:, 2 + var:3 + var]),
                   reads=[twst[i], tcfar], writes=[tWC])

    def layer_params(l):
        dma(SP, "small", small[0:64, 8:12], again[l, :, :], writes=[tsmall])
        dma(SP, "lamt", lamt[:, :], clam[l, :, :], writes=[tlam])
        dma(SP, "small", small[0:64, 13:14], csub[l, :, :], writes=[tsmall])
        li = 0.8 - 0.6 * math.exp(-0.3 * l)
        pr, tpr = ft()
        do(DVE, lambda: nc.vector.tensor_tensor(out=pr[0:64, 0:32], in0=lamt[:, 0:32], in1=lamt[:, 32:64], op=ALU.mult),
           reads=[tlam], writes=[tpr])
        do(DVE, lambda: nc.vector.tensor_tensor(out=pr[0:64, 32:64], in0=lamt[:, 64:96], in1=lamt[:, 96:128], op=ALU.mult),
           reads=[tlam], writes=[tpr])
        do(DVE, lambda: nc.vector.reduce_sum(out=small[0:64, 14:15], in_=pr[0:64, 0:32], axis=mybir.AxisListType.X),
           reads=[tpr], writes=[tsmall])
        do(DVE, lambda: nc.vector.reduce_sum(out=small[0:64, 15:16], in_=pr[0:64, 32:64], axis=mybir.AxisListType.X),
           reads=[tpr], writes=[tsmall])
        do(ACT, lambda: nc.scalar.activation(out=small[0:64, 16:18], in_=small[0:64, 14:16], func=AF.Exp),
           writes=[tsmall])
        do(DVE, lambda: nc.vector.scalar_tensor_tensor(out=small[0:64, 12:13], in0=small[0:64, 17:18], scalar=-li,
                                                        in1=small[0:64, 16:17], op0=ALU.add, op1=ALU.subtract),
           writes=[tsmall])
        do(DVE, lambda: nc.vector.tensor_scalar(out=small[0:64, 13:14], in0=small[0:64, 13:14], scalar1=1.0 - li, scalar2=None,
                                                op0=ALU.mult),
           writes=[tsmall])

    def load_win(l, hg):
        load_cast(lambda c0, w: wbuf[:, :, c0:c0 + w],
                  lambda c0, w: win[l * 4 + hg, :, c0:c0 + w].rearrange("(c p) n -> p c n", p=128),
                  NCOL, tw)

    def inproj(seqk, n, l, hg, pr_):
        BST = globals().get("BSTEP", 99)
        for tt in range(n // 512):
            t0 = tt * 512
            xi = nxt("xt", 2)
            x_t, tx = xt[xi], txt[xi]
            dma(SP, ("xt", xi), x_t[:, :, :], xnT[seqk][:, :, t0:t0 + 512].rearrange("c p t -> p c t"),
                reads=[tD["xnT"][seqk]], writes=[tx])
            if pr_ == 0:
                ri = nxt("rp", 2)
                dma(SP, ("rp", ri), ropt[ri][0][:, :], ropec[:, t0:t0 + 512], writes=[trp[ri]])
                dma(SP, ("rp", ri), ropt[ri][1][:, :], ropes[:, t0:t0 + 512], writes=[trp[ri]])

            def proj(bank, col0, m):
                for c in range(8):
                    do(PE, lambda c=c: nc.tensor.matmul(out=ps[0:m, bank, :], lhsT=wbuf[:, c, col0:col0 + m], rhs=x_t[:, c, :],
                                                        start=(c == 0), stop=(c == 7)),
                       reads=[tx, tw], writes=[tps[bank]])

            def a_part(bank_main, bank_sw, gcol, dst, tdst):
                BSUB = globals().get("BSUB", 99)
                sq, tsq = ft()
                do(ACT, lambda: nc.scalar.activation(out=sq[0:64, :], in_=ps[0:64, bank_main, :], func=AF.Square),
                   reads=[tps[bank_main]], writes=[tsq])
                if globals().get("VARIANT", 0) in (7, 8):
                    t9, tt9 = ft()
                    do(DVE, lambda: nc.vector.tensor_copy(out=t9[:, :], in_=ps[:, bank_main, :]),
                       reads=[tps[bank_main]] + ([tsq] if globals().get("VARIANT", 0) == 8 else []), writes=[tt9])
                if BSUB <= 1:
                    return
                do(PE, lambda: nc.tensor.matmul(out=ps[:, 7, :], lhsT=ones64[:, :], rhs=sq[:, :], start=True, stop=True),
                   reads=[tsq, tones], writes=[tps[7]])
                if BSUB <= 2:
                    return
                ln_, tln = ft()
                do(ACT, lambda: nc.scalar.activation(out=ln_[0:64, :], in_=ps[0:64, 7, :], func=AF.Ln, bias=EPS, scale=1.0),
                   reads=[tps[7]], writes=[tln])
                if BSUB <= 3:
                    return
                do(ACT, lambda: nc.scalar.activation(out=ln_[0:64, :], in_=ln_[0:64, :], func=AF.Exp, scale=-0.5),
                   writes=[tln])
                if BSUB <= 4:
                    return
                t1, tt1 = ft()
                VAR = globals().get("VARIANT", 0)
                if VAR == 0:
                    do(DVE, lambda: nc.vector.scalar_tensor_tensor(out=t1[0:64, :], in0=ps[0:64, bank_main, :],
                                                                    scalar=small[0:64, gcol:gcol + 1], in1=ropt[ri][0][:, :],
                                                                    op0=ALU.mult, op1=ALU.mult),
                       reads=[tps[bank_main], tsmall, trp[ri]], writes=[tt1])
                elif VAR == 1:
                    do(DVE, lambda: nc.vector.tensor_scalar(out=t1[0:64, :], in0=ps[0:64, bank_main, :],
                                                            scalar1=small[0:64, gcol:gcol + 1], scalar2=None, op0=ALU.mult),
                       reads=[tps[bank_main], tsmall], writes=[tt1])
                elif VAR == 2:
                    do(DVE, lambda: nc.vector.tensor_tensor(out=t1[0:64, :], in0=ps[0:64, bank_main, :], in1=ropt[ri][0][:, :], op=ALU.mult),
                       reads=[tps[bank_main], trp[ri]], writes=[tt1])
                elif VAR == 4:
                    do(DVE, lambda: nc.vector.tensor_copy(out=t1[0:64, :], in_=ps[0:64, bank_main, :]),
                       reads=[tps[bank_main]], writes=[tt1])
                elif VAR == 5:
                    do(DVE, lambda: nc.vector.tensor_copy(out=t1[:, :], in_=ps[:, bank_main, :]),
                       reads=[tps[bank_main]], writes=[tt1])
                elif VAR == 6:
                    do(DVE, lambda: nc.vector.tensor_copy(out=t1[0:64, :], in_=sq[0:64, :]),
                       reads=[tsq], writes=[tt1])
                elif VAR == 3:
                    do(DVE, lambda: nc.vector.scalar_tensor_tensor(out=t1[0:64, :], in0=ps[0:64, bank_main, :],
                                                                    scalar=2.0, in1=ropt[ri][0][:, :],
                                                                    op0=ALU.mult, op1=ALU.mult),
                       reads=[tps[bank_main], trp[ri]], writes=[tt1])
                if BSUB <= 5:
                    return
                t2, tt2 = ft()
                do(DVE, lambda: nc.vector.scalar_tensor_tensor(out=t2[0:64, :], in0=ps[0:64, bank_sw, :],
                                                                scalar=small[0:64, gcol + 1:gcol + 2], in1=ropt[ri][1][:, :],
                                                                op0=ALU.mult, op1=ALU.mult),
                   reads=[tps[bank_sw], tsmall, trp[ri]], writes=[tt2])
                do(DVE, lambda: nc.vector.tensor_tensor(out=t1[0:64, :], in0=t1[0:64, :], in1=t2[0:64, :], op=ALU.add),
                   reads=[tt2], writes=[tt1])
                if BSUB <= 6:
                    return
                do(DVE, lambda: nc.vector.tensor_tensor(out=dst[0:64, t0:t0 + 512], in0=t1[0:64, :], in1=ln_[0:64, :], op=ALU.mult),
                   reads=[tt1, tln], writes=[tdst])

            if BST <= 1:
                continue
            if pr_ == 0:
                proj(0, C_AQ * 64, 128)
                if BST <= 2:
                    continue
                proj(1, C_AQS * 64, 64)
                if BST <= 3:
                    continue
                a_part(0, 1, 8, QT, tQT[tt])
                if BST <= 4:
                    continue
                do(ACT, lambda: nc.scalar.copy(out=QT[64:128, t0:t0 + 512], in_=ps[64:128, 0, :]),
                   reads=[tps[0]], writes=[tQT[tt]])
                if BST <= 5:
                    continue
                proj(2, C_AK * 64, 128)
                proj(3, C_AKS * 64, 64)
                a_part(2, 3, 10, KT, tKT[tt])
                do(ACT, lambda: nc.scalar.copy(out=KT[64:128, t0:t0 + 512], in_=ps[64:128, 2, :]),
                   reads=[tps[2]], writes=[tKT[tt]])
                gcol0, vcol0 = C_AG, C_AV
            else:
                proj(0, C_CQ * 64, 128)
                do(DVE, lambda: nc.vector.tensor_copy(out=QT[:, t0:t0 + 512], in_=ps[:, 0, :]),
                   reads=[tps[0]], writes=[tQT[tt]])
                proj(2, C_CK * 64, 128)
                do(DVE, lambda: nc.vector.tensor_copy(out=KT[0:32, t0:t0 + 512], in_=ps[0:32, 2, :]),
                   reads=[tps[2]], writes=[tKT[tt]])
                do(DVE, lambda: nc.vector.memset(KT[32:64, t0:t0 + 512], 0.0), writes=[tKT[tt]])
                do(DVE, lambda: nc.vector.tensor_copy(out=KT2[32:64, t0:t0 + 512], in_=ps[32:64, 2, :]),
                   reads=[tps[2]], writes=[tKT2[tt]])
                do(DVE, lambda: nc.vector.tensor_copy(out=KT[64:128, t0:t0 + 512], in_=ps[64:128, 2, :]),
                   reads=[tps[2]], writes=[tKT[tt]])
                gcol0, vcol0 = C_CG, C_CV
            if BST <= 6:
                continue
            proj(4, gcol0 * 64, 128)
            gt_, tgt = ft()
            do(ACT, lambda gt_=gt_: nc.scalar.activation(out=gt_[:, :], in_=ps[:, 4, :], func=AF.Silu),
               reads=[tps[4]], writes=[tgt])
            dma(POOL, ("st_f", fidx[id(tgt)]), gT[seqk][pr_ * 2:pr_ * 2 + 2, :, t0:t0 + 512].rearrange("m p t -> (m p) t"), gt_[:, :],
                reads=[tgt], writes=[tD["gT"][seqk]])
            if BST <= 7:
                continue
            for s4 in range(4):
                bank = 5 + (s4 % 2)
                for c in range(8):
                    do(PE, lambda c=c, s4=s4, bank=bank: nc.tensor.matmul(out=ps[:, bank, 0:128], lhsT=x_t[:, c, s4 * 128:(s4 + 1) * 128],
                                                                          rhs=wbuf[:, c, vcol0 * 64:vcol0 * 64 + 128],
                                                                          start=(c == 0), stop=(c == 7)),
                       reads=[tx, tw], writes=[tps[bank]])
                blk = tt * 4 + s4
                srcv = ps[:, bank, 0:128].rearrange("p (m d) -> p m d", m=2)
                if s4 % 2 == 0:
                    do(DVE, lambda blk=blk, srcv=srcv: nc.vector.tensor_copy(out=VA[:, blk, :, 0:64], in_=srcv),
                       reads=[tps[bank]], writes=[tVA[tt]])
                else:
                    do(ACT, lambda blk=blk, srcv=srcv: nc.scalar.copy(out=VA[:, blk, :, 0:64], in_=srcv),
                       reads=[tps[bank]], writes=[tVA[tt]])

    def attention(seqk, n, l, hg, pr_):
        nb = n // 128
        nq = n // 512
        sc64 = HD ** -0.5
        sc32 = 32 ** -0.5
        runs = []
        for m in (2 * pr_, 2 * pr_ + 1):
            for j in range(nq):
                units = []
                if m == 0:
                    for kb in range(0, nb, 2):
                        units.append((kb, "plain", None, None))
                    runs.append((m, 0, j, units))
                elif m == 1:
                    for kb in range(max(0, 4 * j - 8), min(nb, 4 * j + 12), 2):
                        v = [(B_OMAX - (k - 4 * j)) * 128 for k in (kb, kb + 1)]
                        units.append((kb, "w", WB[:, v[0]:v[0] + 512], WB[:, v[1]:v[1] + 512]))
                    runs.append((m, 0, j, units))
                elif m == 2:
                    nL = len([kb for kb in range(0, nb, 2) if kb - 4 * j < -2])
                    nR = len([kb for kb in range(0, nb, 2) if kb - 4 * j > 4])
                    ref = 0 if nL >= nR else 1
                    for kb in range(0, nb, 2):
                        o = kb - 4 * j
                        if -2 <= o <= 4:
                            v = [(C_OMAX - (k - 4 * j)) * 128 for k in (kb, kb + 1)]
                            units.append((kb, "w", WC[:, ref, v[0]:v[0] + 512], WC[:, ref, v[1]:v[1] + 512]))
                        elif (o < 0) == (ref == 0):
                            units.append((kb, "plain", None, None))
                        else:
                            units.append((kb, "farL" if o < 0 else "farR", None, None))
                    runs.append((m, 0, j, units))
                    runs.append((m, 1, j, units))
                else:
                    if j == 0:
                        for i in range(0, 6, 2):
                            b0 = D_NT * 128 + i * 512
                            units.append((i, "w", WD[:, b0:b0 + 512], WD[:, b0 + 512:b0 + 1024]))
                    elif j == nq - 1:
                        for i in range(0, 6, 2):
                            b0 = D_NT * 128 + (6 + i) * 512
                            units.append((nb - 6 + i, "w", WD[:, b0:b0 + 512], WD[:, b0 + 512:b0 + 1024]))
                    else:
                        for kb in range(4 * j - 2, 4 * j + 6, 2):
                            v = [(D_OMAX - (k - 4 * j)) * 128 for k in (kb, kb + 1)]
                            units.append((kb, "w", WD[:, v[0]:v[0] + 512], WD[:, v[1]:v[1] + 512]))
                    runs.append((m, 0, j, units))

        flat = []
        for ri_, (m, mp, j, units) in enumerate(runs):
            for ui, u in enumerate(units):
                flat.append((ri_, ui, len(units), m, mp, j, u))

        run_obank = {}
        unit_pi = {}
        pending = []
        c_hold = {}

        def qk_operands(m, mp, j, kb):
            p0 = 64 * (m % 2)
            if m == 2 and mp == 1:
                return (KT2[0:64, kb * 128:(kb + 1) * 128], QT[0:64, j * 512:(j + 1) * 512], tKT2[kb // 4], tQT[j])
            return (KT[p0:p0 + 64, kb * 128:(kb + 1) * 128], QT[p0:p0 + 64, j * 512:(j + 1) * 512], tKT[kb // 4], tQT[j])

        def emit_S(idx):
            ri_, ui, nu, m, mp, j, (kb, mode, w0, w1) = flat[idx]
            sp_ = idx % 2
            for h in range(2):
                lhsT, rhs, tk, tq = qk_operands(m, mp, j, kb + h)
                bank = sp_ * 2 + h
                do(PE, lambda lhsT=lhsT, rhs=rhs, bank=bank: nc.tensor.matmul(out=ps[:, bank, :], lhsT=lhsT, rhs=rhs, start=True, stop=True),
                   reads=[tk, tq], writes=[tps[bank]])

        def finalize_A(ob):
            osb, tosb = ft()
            ohi, tohi = ft()
            i_lo, i_hi = fidx[id(tosb)], fidx[id(tohi)]
            if i_hi == i_lo + 1:
                pass
            do(DVE, lambda: nc.vector.tensor_copy(out=osb[0:65, :], in_=ps[0:65, ob, :]), reads=[tps[ob]], writes=[tosb])
            do(DVE, lambda: nc.vector.tensor_tensor(out=osb[0:65, :], in0=ps[0:65, ob + 1, :], in1=osb[0:65, :], op=ALU.add),
               reads=[tps[ob + 1]], writes=[tosb])
            return osb, tosb, ob

        def finalize_B(m, j, holders):
            outs = []
            mb = holders[-1][2]
            for (osb, tosb, ob_) in holders:
                do(PE, lambda osb=osb: nc.tensor.matmul(out=ps[:, mb, :], lhsT=sel65[:, :], rhs=osb[:, :], start=True, stop=True),
                   reads=[tosb, tsel], writes=[tps[mb]])
                rd, trd = ft()
                do(ACT, lambda rd=rd: nc.scalar.activation(out=rd[0:64, :], in_=ps[0:64, mb, :], func=AF.Ln), reads=[tps[mb]], writes=[trd])
                do(ACT, lambda rd=rd: nc.scalar.activation(out=rd[0:64, :], in_=rd[0:64, :], func=AF.Exp, scale=-1.0), writes=[trd])
                do(DVE, lambda rd=rd, osb=osb: nc.vector.tensor_tensor(out=rd[0:64, :], in0=osb[0:64, :], in1=rd[0:64, :], op=ALU.mult),
                   reads=[tosb], writes=[trd])
                outs.append((rd, trd))
            o, to = outs[0]
            gi = nxt("g", 2)
            dma(SP, ("g", gi), gtile[gi][:, :], gT[seqk][m, :, j * 512:(j + 1) * 512], reads=[tD["gT"][seqk]], writes=[tg[gi]])
            if m == 2:
                o2, to2 = outs[1]
                do(DVE, lambda: nc.vector.scalar_tensor_tensor(out=o[0:64, :], in0=o2[0:64, :], scalar=small[0:64, 12:13], in1=o[0:64, :],
                                                                op0=ALU.mult, op1=ALU.add),
                   reads=[to2, tsmall], writes=[to])
                sq, tsq = ft()
                do(ACT, lambda: nc.scalar.activation(out=sq[0:64, :], in_=o[0:64, :], func=AF.Square), reads=[to], writes=[tsq])
                obc = mb
                do(PE, lambda: nc.tensor.matmul(out=ps[:, obc, :], lhsT=ones64[:, :], rhs=sq[:, :], start=True, stop=True),
                   reads=[tsq, tones], writes=[tps[obc]])
                do(ACT, lambda: nc.scalar.activation(out=sq[0:64, :], in_=ps[0:64, obc, :], func=AF.Ln, bias=EPS, scale=1.0),
                   reads=[tps[obc]], writes=[tsq])
                do(ACT, lambda: nc.scalar.activation(out=sq[0:64, :], in_=sq[0:64, :], func=AF.Exp, scale=-0.5), writes=[tsq])
                do(DVE, lambda: nc.vector.scalar_tensor_tensor(out=o[0:64, :], in0=o[0:64, :], scalar=small[0:64, 13:14], in1=sq[0:64, :],
                                                                op0=ALU.mult, op1=ALU.mult),
                   reads=[tsq, tsmall], writes=[to])
            mi = nxt("mx", 2)
            do(DVE, lambda: nc.vector.tensor_tensor(out=mxt[mi][:, :], in0=o[0:64, :], in1=gtile[gi][:, :], op=ALU.mult),
               reads=[to, tg[gi]], writes=[tmx[mi]])
            r0 = m * 256 + hg * 64
            dma(POOL, ("st_mx", mi), mixT[seqk][r0:r0 + 64, j * 512:(j + 1) * 512], mxt[mi][:, :], reads=[tmx[mi]], writes=[tD["mixT"][seqk]])

        def emit_rest(idx):
            ri_, ui, nu, m, mp, j, (kb, mode, w0, w1) = flat[idx]
            sp_ = idx % 2
            pi = nxt("pt", len(PT))
            scale = sc32 if m == 2 else sc64
            src = ps[:, sp_ * 2:sp_ * 2 + 2, :]
            dst = PT[pi][:, :].rearrange("p (b q) -> p b q", b=2)
            if mode in ("farL", "farR"):
                bcol = cfar_sb[:, 5:6] if mode == "farL" else cfar_sb[:, 4:5]
                do(ACT, lambda: nc.scalar.activation(out=dst, in_=src, func=AF.Exp, bias=bcol, scale=scale),
                   reads=[tps[sp_ * 2], tps[sp_ * 2 + 1], tcfar], writes=tPT[pi])
            else:
                do(ACT, lambda: nc.scalar.activation(out=dst, in_=src, func=AF.Exp, scale=scale),
                   reads=[tps[sp_ * 2], tps[sp_ * 2 + 1]], writes=tPT[pi])
            if mode == "w":
                tW = {1: tWB, 2: tWC, 3: tWD}[m]
                for h, wv in ((0, w0), (1, w1)):
                    do(DVE, lambda h=h, wv=wv: nc.vector.tensor_tensor(out=PT[pi][:, h * 512:(h + 1) * 512], in0=PT[pi][:, h * 512:(h + 1) * 512],
                                                                        in1=wv, op=ALU.mult),
                       reads=[tW], writes=[tPT[pi][h]])
            unit_pi[idx] = pi

        def emit_pv(idx):
            ri_, ui, nu, m, mp, j, (kb, mode, w0, w1) = flat[idx]
            pi = unit_pi.pop(idx)
            if ui == 0:
                run_obank[ri_] = 4 + 2 * nxt("ob", 2)
            ob = run_obank[ri_]
            ml = m % 2
            for h in range(2):
                k = kb + h
                for half in range(2):
                    r0_ = 64 * half
                    do(PE, lambda h=h, k=k, half=half, r0_=r0_: nc.tensor.matmul(
                        out=ps[0:65, ob + half, :], lhsT=VA[r0_:r0_ + 64, k, ml, :], rhs=PT[pi][r0_:r0_ + 64, h * 512:(h + 1) * 512],
                        start=(ui == 0 and h == 0), stop=(ui == nu - 1 and h == 1)),
                       reads=[tPT[pi][h], tVA[k // 4]], writes=[tps[ob + half]])
            if ui == nu - 1:
                hold = finalize_A(ob)
                if m == 2:
                    c_hold.setdefault(j, []).append(hold)
                    if mp == 1:
                        hs = c_hold.pop(j)
                        pending.append((idx + 3, lambda hs=hs, m=m, j=j: finalize_B(m, j, hs)))
                else:
                    pending.append((idx + 3, lambda hold=hold, m=m, j=j: finalize_B(m, j, [hold])))

        NF = len(flat)
        emit_S(0)
        if NF > 1:
            emit_S(1)
        for idx in range(NF):
            emit_rest(idx)
            if idx + 2 < NF:
                emit_S(idx + 2)
            if idx >= 1:
                emit_pv(idx - 1)
            while pending and pending[0][0] <= idx:
                pending.pop(0)[1]()
        emit_pv(NF - 1)
        while pending:
            pending.pop(0)[1]()

    def outphase(seqk, n, l):
        last = (l == L - 1)
        load_cast(lambda c0, w: wbuf[:, :, c0:c0 + w],
                  lambda c0, w: wout[l, :, c0:c0 + w].rearrange("(c p) n -> p c n", p=128),
                  D, tw)
        load_g(l + 1)
        xsrc = x_in[seqk] if l == 0 else x1[seqk]
        for b in range(n // 128):
            mi = nxt("mts", 2)
            dma(SP, ("mts", mi), mts[mi][:, :, :], mixT[seqk][:, b * 128:(b + 1) * 128].rearrange("(c p) t -> p c t", p=128),
                reads=[tD["mixT"][seqk]], writes=[tmts[mi]])
            xi = nxt("xrow", 2)
            dma(SP, ("xrow", xi), xrow[xi][:, :], xsrc[b * 128:(b + 1) * 128, :], reads=([tDx1[(seqk, l - 1)]] if l > 0 else []), writes=[txrow[xi]])
            for half in range(2):
                bank = half + 2 * (b % 2)
                for c in range(8):
                    do(PE, lambda c=c, half=half, bank=bank: nc.tensor.matmul(out=ps[:, bank, :], lhsT=mts[mi][:, c, :],
                                                                              rhs=wbuf[:, c, half * 512:(half + 1) * 512],
                                                                              start=(c == 0), stop=(c == 7)),
                       reads=[tmts[mi], tw], writes=[tps[bank]])
                do(DVE, lambda half=half, bank=bank: nc.vector.tensor_tensor(out=xrow[xi][:, half * 512:(half + 1) * 512],
                                                                             in0=ps[:, bank, :], in1=xrow[xi][:, half * 512:(half + 1) * 512],
                                                                             op=ALU.add),
                   reads=[tps[bank]], writes=[txrow[xi]])
            if not last:
                dma(POOL, ("st_xrow", xi), x1[seqk][b * 128:(b + 1) * 128, :], xrow[xi][:, :], reads=[txrow[xi]], writes=[tDx1[(seqk, l)]])
            norm_block(xrow[xi], txrow[xi], seqk, b, final=last)

    load_g(0)
    for seqk, n in seqs:
        for b in range(n // 128):
            xi = nxt("xrow", 2)
            dma(SP, ("xrow", xi), xrow[xi][:, :], x_in[seqk][b * 128:(b + 1) * 128, :], writes=[txrow[xi]])
            norm_block(xrow[xi], txrow[xi], seqk, b, final=False)
    BIS = globals().get("BISECT", "full")
    for l in range(L):
        if BIS == "norm":
            break
        layer_params(l)
        for hg in range(4):
            build_tables(l, hg)
            if BIS == "tables":
                continue
            load_win(l, hg)
            if BIS == "loadwin":
                continue
            for seqk, n in seqs:
                for pr_ in range(2):
                    inproj(seqk, n, l, hg, pr_)
                    if BIS == "inproj":
                        continue
                    attention(seqk, n, l, hg, pr_)
        if BIS in ("tables", "inproj", "attn", "loadwin"):
            continue
        for seqk, n in seqs:
            outphase(seqk, n, l)
    engs = [PE, ACT, DVE, POOL, SP]
    for E in engs:
        for O in engs:
            if O is not E and O.lane.cnt > 0:
                E.eng.wait_ge(O.lane.sem, O.lane.cnt)
        for k, ln in dl.items():
            if ln.cnt > 0:
                E.eng.wait_ge(ln.sem, ln.cnt)
    return nc


def prepare_shared(w_in, w_out, norm_g, final_g, a_q_gain, a_k_gain, t5_bias, c_lambda_q1, c_lambda_k1,
                   c_lambda_q2, c_lambda_k2, c_subln_g, d_rpb, NMAX):
    L = w_in.shape[0]
    f = np.float32
    w_in = np.asarray(w_in, f); w_out = np.asarray(w_out, f)
    perm = _swap_perm()
    offs = np.cumsum([0, 256, 128, 128, 256] + [256] * 12)
    names = ["aq", "ak", "av", "ag", "bq", "bk", "bv", "bg", "cq", "ck", "cv", "cg", "dq", "dk", "dv", "dg"]
    o = {nm: int(offs[i]) for i, nm in enumerate(names)}
    win = np.zeros((L * 4, D, NCOL), f)
    for l in range(L):
        for hg in range(4):
            W = w_in[l]
            kvh = hg // 2
            aq = W[:, o["aq"] + hg * 64: o["aq"] + (hg + 1) * 64]
            ak = W[:, o["ak"] + kvh * 64: o["ak"] + (kvh + 1) * 64]
            av = W[:, o["av"] + kvh * 64: o["av"] + (kvh + 1) * 64]
            sl = lambda nm: W[:, o[nm] + hg * 64: o[nm] + (hg + 1) * 64]
            blocks = [None] * 18
            blocks[C_AQ], blocks[C_BQ], blocks[C_CQ], blocks[C_DQ] = aq, sl("bq"), sl("cq"), sl("dq")
            blocks[C_AK], blocks[C_BK], blocks[C_CK], blocks[C_DK] = ak, sl("bk"), sl("ck"), sl("dk")
            blocks[C_AQS], blocks[C_AKS] = aq[:, perm], ak[:, perm]
            blocks[C_AG], blocks[C_BG], blocks[C_CG], blocks[C_DG] = sl("ag"), sl("bg"), sl("cg"), sl("dg")
            blocks[C_AV], blocks[C_BV], blocks[C_CV], blocks[C_DV] = av, sl("bv"), sl("cv"), sl("dv")
            win[l * 4 + hg] = np.concatenate(blocks, axis=1)
    gb = np.concatenate([np.asarray(norm_g, f), np.asarray(final_g, f)[None]], 0)
    gbc = np.ascontiguousarray(np.broadcast_to(gb[:, None, :], (L + 1, 128, D)))
    aqg = np.asarray(a_q_gain, f); akg = np.asarray(a_k_gain, f)
    again = np.stack([aqg, aqg[:, perm], akg, akg[:, perm]], axis=2)
    lam = np.concatenate([np.asarray(c_lambda_q1, f), np.asarray(c_lambda_k1, f),
                          np.asarray(c_lambda_q2, f), np.asarray(c_lambda_k2, f)], axis=1)
    clam = np.ascontiguousarray(np.broadcast_to(lam[:, None, :], (L, 64, 128)))
    csub = np.asarray(c_subln_g, f)[:, :, None].copy()
    t5 = np.asarray(t5_bias, f)
    cfar = np.zeros((4, 128, 2), f)
    for h in range(4):
        cfar[h, :, 0] = t5[15, 4 + h]
        cfar[h, :, 1] = t5[31, 4 + h]
    offB = _toep_off(B_OMAX, B_NT)
    bkB = _t5_bucket_np(offB)
    tabB = np.stack([t5[bkB, h] for h in range(4)], 0).astype(f)
    mulB = _b_mult(offB).astype(f)
    offC = _toep_off(C_OMAX, C_NT)
    bkC = _t5_bucket_np(offC)
    tabC = np.stack([t5[bkC, 4 + h] for h in range(4)], 0).astype(f)
    vD, drD, dcD = _d_tables()
    rpb = np.asarray(d_rpb, f)
    tabD = np.stack([rpb[l, h][drD, dcD] for l in range(L) for h in range(4)], 0).astype(f)
    mskD = vD.astype(f)
    cos, sin = _rope_tables(NMAX)
    return dict(win=win, wout=w_out, gbc=gbc, again=np.ascontiguousarray(again), clam=clam, csub=csub, cfar=cfar,
                tabB=np.ascontiguousarray(tabB), mulB=np.ascontiguousarray(mulB), tabC=np.ascontiguousarray(tabC),
                tabD=np.ascontiguousarray(tabD), mskD=np.ascontiguousarray(mskD), ropec=cos, ropes=sin,
                ident=np.eye(128, dtype=f))


_NC_CACHE = {}


def run(x_prompt, x_sample, **params):
    x_prompt = np.asarray(x_prompt, np.float32)
    x_sample = np.asarray(x_sample, np.float32)
    BP, NP, _ = x_prompt.shape
    BS, NS, _ = x_sample.shape
    L = np.asarray(params["w_in"]).shape[0]
    shared = prepare_shared(NMAX=max(NP, NS), **params)
    key = (NP, NS, L)
    if key not in _NC_CACHE:
        _NC_CACHE[key] = build_program(NP, NS, L)
    nc = _NC_CACHE[key]
    ncores = 8
    in_maps = []
    for c in range(ncores):
        m = dict(shared)
        m["xp"] = np.ascontiguousarray(x_prompt[(c * BP) // ncores])
        m["xs"] = np.ascontiguousarray(x_sample[c % BS])
        in_maps.append(m)
    res = run_bass_kernel_spmd(nc, in_maps, core_ids=list(range(ncores)))
    yp = np.stack([np.asarray(res.results[(b * ncores) // BP]["yp"], np.float32) for b in range(BP)], 0)
    ys = np.stack([np.asarray(res.results[c]["ys"], np.float32) for c in range(BS)], 0)
    return yp, ys


def kernel(x_prompt, x_sample, w_in, w_out, norm_g, final_g, a_q_gain, a_k_gain, t5_bias,
           c_lambda_q1, c_lambda_k1, c_lambda_q2, c_lambda_k2, c_subln_g, d_rpb):
    return run(x_prompt, x_sample, w_in=w_in, w_out=w_out, norm_g=norm_g, final_g=final_g, a_q_gain=a_q_gain,
               a_k_gain=a_k_gain, t5_bias=t5_bias, c_lambda_q1=c_lambda_q1, c_lambda_k1=c_lambda_k1,
               c_lambda_q2=c_lambda_q2, c_lambda_k2=c_lambda_k2, c_subln_g=c_subln_g, d_rpb=d_rpb)
```
